# Optimizing a Trainium2 kernel written in Bass

```python
import jax, jax.numpy as jnp
from jax import lax
import numpy as np

D_MODEL = 2048
BATCH = 1
SEQ = 16384
DEPTH = 2

N_EVEN = (DEPTH + 1) // 2
N_ODD = DEPTH // 2
NORM_EPS = 1e-6
NEG_INF = -1e30
ROPE_THETA = 10000.0
D_FF = 4 * D_MODEL

NSA_HEADS = 8
NSA_KV_HEADS = 2
NSA_GROUP = NSA_HEADS // NSA_KV_HEADS
NSA_DH = 128
CMP_LEN = 32
CMP_STRIDE = 16
SLC_LEN = 64
SLC_TOPN = 16
WIN = 512
Q_BLOCK = 128
SLC_FORCE = 1e4
NSA_W = NSA_HEADS * NSA_DH
NSA_KV_W = NSA_KV_HEADS * NSA_DH
NSA_COLS = NSA_W + 6 * NSA_KV_W + 3 * NSA_HEADS

RW_HEADS = 16
RW_DH = 64
RW_W = RW_HEADS * RW_DH
RW_DECAY_LORA = 96
RW_A_LORA = 96
RW_G_LORA = 256
RW_LNX_EPS = 64e-5
RW_COLS = 3 * RW_W + RW_DECAY_LORA + RW_A_LORA + RW_G_LORA

LRU_W = 1024
LRU_BLOCKS = 8
LRU_BW = LRU_W // LRU_BLOCKS
CONV_W = 4
LRU_C = 8.0

HG_HEADS = 8
HG_DK = 128
HG_DV = 128
HG_KW = HG_HEADS * HG_DK
HG_VW = HG_HEADS * HG_DV
HG_CHUNK = 64

EVEN_COLS = NSA_COLS + RW_COLS
ODD_COLS = 2 * LRU_W + 2 * HG_KW + 2 * HG_VW
MIX_EVEN = NSA_W + RW_W
MIX_ODD = LRU_W + HG_VW

kernel_name = 'hybrid_nsa_rwkv7_rglru_hgrn2_trunk'


def _split(h, sizes):
    cuts = [int(c) for c in np.cumsum(sizes)[:-1]]
    return jnp.split(h, cuts, axis=-1)


def _rms_norm(x, g, eps=NORM_EPS):
    xf = x.astype(jnp.float32)
    y = xf * lax.rsqrt(jnp.mean(xf * xf, axis=-1, keepdims=True) + eps)
    return (y * g.astype(jnp.float32)).astype(x.dtype)


def _rope_tables(pos):
    inv = ROPE_THETA ** (-(jnp.arange(0, NSA_DH, 2, dtype=jnp.float32) / NSA_DH))
    ang = pos[:, None] * inv[None, :]
    return jnp.cos(ang), jnp.sin(ang)


def _apply_rope(x, cos, sin):
    xf = x.astype(jnp.float32)
    half = xf.shape[-1] // 2
    x1, x2 = xf[..., :half], xf[..., half:]
    c, s = cos[None, :, None, :], sin[None, :, None, :]
    return jnp.concatenate([x1 * c - x2 * s, x1 * s + x2 * c], axis=-1).astype(x.dtype)


def _masked_softmax(s, mask):
    p = jax.nn.softmax(jnp.where(mask, s, NEG_INF), axis=-1)
    return p * mask


def _compress(t, w, pe):
    B, S, G, dh = t.shape
    chunks = t.reshape(B, S // CMP_STRIDE, CMP_STRIDE, G, dh)
    blocks = jnp.concatenate([chunks[:, :-1], chunks[:, 1:]], axis=2)
    blocks = blocks + pe[None, None, :, None, :]
    return jnp.einsum('bnlgd,lde->bnge', blocks, w)


def _nsa(u, qk_gain, cmp_w, cmp_pe, cos, sin):
    B, S, _ = u.shape
    f32 = jnp.float32
    q, kc, vc, ks, vs, kw, vw, gate = _split(u, [NSA_W] + [NSA_KV_W] * 6 + [3 * NSA_HEADS])
    q = q.reshape(B, S, NSA_HEADS, NSA_DH)
    kc, vc, ks, vs, kw, vw = (t.reshape(B, S, NSA_KV_HEADS, NSA_DH) for t in (kc, vc, ks, vs, kw, vw))
    n_cmp = (S - CMP_LEN) // CMP_STRIDE + 1
    n_slc = S // SLC_LEN
    n_qb = S // Q_BLOCK
    top_n = min(SLC_TOPN, n_slc)
    scale = NSA_DH ** -0.5

    q = _apply_rope(_rms_norm(q, qk_gain[0]), cos, sin)
    ks = _apply_rope(_rms_norm(ks, qk_gain[2]), cos, sin)
    kw = _apply_rope(_rms_norm(kw, qk_gain[3]), cos, sin)
    kcmp = _compress(kc, cmp_w[0], cmp_pe[0])
    vcmp = _compress(vc, cmp_w[1], cmp_pe[1])
    cmp_end = jnp.arange(n_cmp) * CMP_STRIDE + CMP_LEN - 1
    ccos, csin = _rope_tables(cmp_end.astype(f32))
    kcmp = _apply_rope(_rms_norm(kcmp, qk_gain[1]), ccos, csin)

    c0 = jnp.arange(n_cmp)[:, None] * CMP_STRIDE
    s0 = jnp.arange(n_slc)[None, :] * SLC_LEN
    overlap = jnp.clip(jnp.minimum(c0 + CMP_LEN, s0 + SLC_LEN) - jnp.maximum(c0, s0), 0, None).astype(f32) / CMP_LEN

    ks_blk = ks.reshape(B, n_slc, SLC_LEN, NSA_KV_HEADS, NSA_DH).transpose(0, 3, 1, 2, 4)
    vs_blk = vs.reshape(B, n_slc, SLC_LEN, NSA_KV_HEADS, NSA_DH).transpose(0, 3, 1, 2, 4)
    kw_pad = jnp.pad(kw, ((0, 0), (WIN, 0), (0, 0), (0, 0)))
    vw_pad = jnp.pad(vw, ((0, 0), (WIN, 0), (0, 0), (0, 0)))
    gather = jax.vmap(jax.vmap(lambda blk, idx: blk[idx]))

    q_blocks = q.reshape(B, n_qb, Q_BLOCK, NSA_KV_HEADS, NSA_GROUP, NSA_DH).transpose(1, 0, 2, 3, 4, 5)
    g_all = jax.nn.sigmoid(gate.astype(f32)).reshape(B, n_qb, Q_BLOCK, 3, NSA_KV_HEADS, NSA_GROUP)
    g_blocks = g_all.transpose(1, 0, 2, 3, 4, 5)
    slc_ids = jnp.arange(n_slc)

    def block_fn(args):
        qb, gb, blk = args
        t = blk * Q_BLOCK + jnp.arange(Q_BLOCK)
        s = jnp.einsum('bqgjd,bngd->bgjqn', qb, kcmp).astype(f32) * scale
        p = _masked_softmax(s, cmp_end[None, :] <= t[:, None])
        o_c = jnp.einsum('bgjqn,bngd->bqgjd', p.astype(vcmp.dtype), vcmp)
        imp = jnp.einsum('bgjqn,nm->bgqm', p, overlap)
        cur = t // SLC_LEN
        forced = (slc_ids[None, :] == 0) | (slc_ids[None, :] == cur[:, None]) | (slc_ids[None, :] == cur[:, None] - 1)
        valid = slc_ids[None, :] <= cur[:, None]
        score = jnp.where(forced, SLC_FORCE, jnp.where(valid, imp, -1.0))
        _, idx = lax.top_k(score, top_n)
        kg = gather(ks_blk, idx).reshape(B, NSA_KV_HEADS, Q_BLOCK, top_n * SLC_LEN, NSA_DH)
        vg = gather(vs_blk, idx).reshape(B, NSA_KV_HEADS, Q_BLOCK, top_n * SLC_LEN, NSA_DH)
        kpos = (idx[..., None] * SLC_LEN + jnp.arange(SLC_LEN)).reshape(B, NSA_KV_HEADS, Q_BLOCK, top_n * SLC_LEN)
        s = jnp.einsum('bqgjd,bgqkd->bgjqk', qb, kg).astype(f32) * scale
        p = _masked_softmax(s, (kpos <= t[None, None, :, None])[:, :, None])
        o_s = jnp.einsum('bgjqk,bgqkd->bqgjd', p.astype(vg.dtype), vg)
        kwb = lax.dynamic_slice_in_dim(kw_pad, blk * Q_BLOCK, WIN + Q_BLOCK, axis=1)
        vwb = lax.dynamic_slice_in_dim(vw_pad, blk * Q_BLOCK, WIN + Q_BLOCK, axis=1)
        wpos = blk * Q_BLOCK - WIN + jnp.arange(WIN + Q_BLOCK)
        d = t[:, None] - wpos[None, :]
        mw = (d >= 0) & (d < WIN) & (wpos[None, :] >= 0)
        s = jnp.einsum('bqgjd,bkgd->bgjqk', qb, kwb).astype(f32) * scale
        p = _masked_softmax(s, mw)
        o_w = jnp.einsum('bgjqk,bkgd->bqgjd', p.astype(vwb.dtype), vwb)
        o = (gb[:, :, 0, :, :, None] * o_c.astype(f32) + gb[:, :, 1, :, :, None] * o_s.astype(f32)
             + gb[:, :, 2, :, :, None] * o_w.astype(f32))
        return o.astype(qb.dtype)

    o = lax.map(block_fn, (q_blocks, g_blocks, jnp.arange(n_qb)))
    return o.transpose(1, 0, 2, 3, 4, 5).reshape(B, S, NSA_W)


def _rwkv7_scan(r, w, k, v, a, b):
    B, S, H, N = r.shape

    def step(state, inp):
        r_t, w_t, k_t, v_t, a_t, b_t = inp
        sa = jnp.einsum('bhij,bhj->bhi', state, a_t)
        state = state * w_t[:, :, None, :] + sa[..., None] * b_t[:, :, None, :] + v_t[..., None] * k_t[:, :, None, :]
        return state, jnp.einsum('bhij,bhj->bhi', state, r_t)

    xs = tuple(jnp.moveaxis(t, 1, 0) for t in (r, w, k, v, a, b))
    _, y = lax.scan(step, jnp.zeros((B, H, N, N), jnp.float32), xs)
    return jnp.moveaxis(y, 0, 1)


def _rwkv7(u, mu, w0, w2, a0, a2, g2, k_k, k_a, r_k, lnx_w, lnx_b):
    B, S, _ = u.shape
    f32 = jnp.float32
    u_prev = jnp.pad(u, ((0, 0), (1, 0), (0, 0)))[:, :-1]
    u = u + (u_prev - u) * mu
    r, k, v, wl, al, gl = _split(u, [RW_W, RW_W, RW_W, RW_DECAY_LORA, RW_A_LORA, RW_G_LORA])
    w = -jax.nn.softplus(-(w0 + jnp.tanh(wl) @ w2).astype(f32)) - 0.5
    decay = jnp.exp(-jnp.exp(w))
    a = jax.nn.sigmoid((a0 + al @ a2).astype(f32))
    g = (jax.nn.sigmoid(gl) @ g2).astype(f32)
    kf = k.astype(f32)
    kk = (kf * k_k.astype(f32)).reshape(B, S, RW_HEADS, RW_DH)
    kk = kk / jnp.maximum(jnp.sqrt(jnp.sum(kk * kk, axis=-1, keepdims=True)), 1e-12)
    kf = kf * (1.0 + (a - 1.0) * k_a.astype(f32))
    heads = lambda t: t.reshape(B, S, RW_HEADS, RW_DH)
    rh, kh, vh, ah, dh_ = heads(r.astype(f32)), heads(kf), heads(v.astype(f32)), heads(a), heads(decay)
    y = _rwkv7_scan(rh, dh_, kh, vh, -kk, kk * ah)
    mean = jnp.mean(y, axis=-1, keepdims=True)
    var = jnp.mean(jnp.square(y - mean), axis=-1, keepdims=True)
    y = ((y - mean) * lax.rsqrt(var + RW_LNX_EPS)).reshape(B, S, RW_W) * lnx_w.astype(f32) + lnx_b.astype(f32)
    bonus = jnp.sum(rh * kh * r_k.astype(f32), axis=-1, keepdims=True) * vh
    y = (y + bonus.reshape(B, S, RW_W)) * g
    return y.astype(u.dtype)


def _rglru(u, conv_w, conv_b, wa, ba, wx, bx, lam):
    B, S, _ = u.shape
    f32 = jnp.float32
    gate_in, xb = _split(u, [LRU_W, LRU_W])
    xc = lax.conv_general_dilated(xb, conv_w[:, None, :], window_strides=(1,), padding=[(CONV_W - 1, 0)],
                                  dimension_numbers=('NWC', 'WIO', 'NWC'), feature_group_count=LRU_W) + conv_b
    blocks = xc.reshape(B, S, LRU_BLOCKS, LRU_BW)
    r = jax.nn.sigmoid(jnp.einsum('bsnc,ncd->bsnd', blocks, wa).reshape(B, S, LRU_W).astype(f32) + ba.astype(f32))
    i = jax.nn.sigmoid(jnp.einsum('bsnc,ncd->bsnd', blocks, wx).reshape(B, S, LRU_W).astype(f32) + bx.astype(f32))
    log_a = -LRU_C * r * jax.nn.softplus(-lam.astype(f32))
    a = jnp.exp(log_a)
    b = jnp.sqrt(-jnp.expm1(2.0 * log_a)) * i * xc.astype(f32)

    def combine(lhs, rhs):
        a1, b1 = lhs
        a2, b2 = rhs
        return a1 * a2, a2 * b1 + b2

    _, h = lax.associative_scan(combine, (a, b), axis=1)
    return (h * jax.nn.gelu(gate_in.astype(f32))).astype(u.dtype)


def _hgrn2_chunkwise(q, k, v, log_f):
    B, S, H, dk = q.shape
    dv = v.shape[-1]
    n_chunk = S // HG_CHUNK

    def chunks(t):
        return t.reshape(B, n_chunk, HG_CHUNK, H, t.shape[-1]).transpose(1, 0, 3, 2, 4)

    tri = jnp.tril(jnp.ones((HG_CHUNK, HG_CHUNK), dtype=bool))

    def step(state, inp):
        qc, kc, vc, lc = inp
        bcum = jnp.cumsum(lc, axis=2)
        o = jnp.einsum('bhtk,bhkv->bhtv', qc * jnp.exp(bcum), state)
        diff = bcum[:, :, :, None, :] - bcum[:, :, None, :, :]
        decay = jnp.exp(jnp.where(tri[:, :, None], diff, NEG_INF))
        att = jnp.einsum('bhtk,bhsk,bhtsk->bhts', qc, kc, decay)
        o = o + jnp.einsum('bhts,bhsv->bhtv', att, vc)
        b_last = bcum[:, :, -1:, :]
        state = (jnp.exp(b_last[:, :, 0, :])[..., None] * state
                 + jnp.einsum('bhsk,bhsv->bhkv', kc * jnp.exp(b_last - bcum), vc))
        return state, o

    s0 = jnp.zeros((B, H, dk, dv), jnp.float32)
    _, o = lax.scan(step, s0, (chunks(q), chunks(k), chunks(v), chunks(log_f)))
    return o.transpose(1, 0, 3, 2, 4).reshape(B, S, H, dv)


def _hgrn2(u, lower, norm_g):
    B, S, _ = u.shape
    f32 = jnp.float32
    q, f, i, g = _split(u, [HG_KW, HG_KW, HG_VW, HG_VW])
    forget = lower + (1.0 - lower) * jax.nn.sigmoid(f.astype(f32))
    qh = jax.nn.silu(q.astype(f32)).reshape(B, S, HG_HEADS, HG_DK)
    kh = (1.0 - forget).reshape(B, S, HG_HEADS, HG_DK)
    lh = jnp.log(forget).reshape(B, S, HG_HEADS, HG_DK)
    vh = i.astype(f32).reshape(B, S, HG_HEADS, HG_DV)
    o = _hgrn2_chunkwise(qh, kh, vh, lh)
    o = _rms_norm(o, norm_g.reshape(HG_HEADS, HG_DV)).reshape(B, S, HG_VW)
    return (o * jax.nn.silu(g.astype(f32))).astype(u.dtype)


def _sq_relu_mlp(h, w1, w2):
    z = jax.nn.relu(h @ w1)
    return (z * z) @ w2


def setup_inputs(seed: int = 0) -> dict:
    key = jax.random.key(seed)
    keys = iter(jax.random.split(key, 48))
    f32 = jnp.float32

    def normal(shape, scale):
        return jax.random.normal(next(keys), shape, f32) * scale

    def gain(shape):
        return 1.0 + 0.02 * jax.random.normal(next(keys), shape, f32)

    def uniform(shape, lo, hi):
        return jax.random.uniform(next(keys), shape, f32, lo, hi)

    a_base = uniform((N_ODD, LRU_W), 0.9, 0.999) ** (1.0 / LRU_C)
    lam = jnp.log(a_base) - jnp.log1p(-a_base)
    return {
        'x': normal((BATCH, SEQ, D_MODEL), 1.0),
        'norm_mix': gain((DEPTH, D_MODEL)),
        'norm_mlp': gain((DEPTH, D_MODEL)),
        'w_ff1': normal((DEPTH, D_MODEL, D_FF), D_MODEL ** -0.5),
        'w_ff2': normal((DEPTH, D_FF, D_MODEL), D_FF ** -0.5),
        'w_in_a': normal((N_EVEN, D_MODEL, EVEN_COLS), D_MODEL ** -0.5),
        'w_out_a': normal((N_EVEN, MIX_EVEN, D_MODEL), MIX_EVEN ** -0.5),
        'nsa_qk_gain': gain((N_EVEN, 4, NSA_DH)),
        'nsa_cmp_w': normal((N_EVEN, 2, CMP_LEN, NSA_DH, NSA_DH), (CMP_LEN * NSA_DH) ** -0.5),
        'nsa_cmp_pe': normal((N_EVEN, 2, CMP_LEN, NSA_DH), 0.1),
        'rw_mu': uniform((N_EVEN, RW_COLS), 0.0, 1.0),
        'rw_w0': uniform((N_EVEN, RW_W), -6.0, -1.0),
        'rw_w2': normal((N_EVEN, RW_DECAY_LORA, RW_W), 0.5 * RW_DECAY_LORA ** -0.5),
        'rw_a0': normal((N_EVEN, RW_W), 0.1),
        'rw_a2': normal((N_EVEN, RW_A_LORA, RW_W), 0.5 * RW_A_LORA ** -0.5),
        'rw_g2': normal((N_EVEN, RW_G_LORA, RW_W), RW_G_LORA ** -0.5),
        'rw_k_k': 0.85 + normal((N_EVEN, RW_W), 0.02),
        'rw_k_a': gain((N_EVEN, RW_W)),
        'rw_r_k': normal((N_EVEN, RW_HEADS, RW_DH), 0.1),
        'rw_lnx_w': gain((N_EVEN, RW_W)),
        'rw_lnx_b': normal((N_EVEN, RW_W), 0.01),
        'w_in_b': normal((N_ODD, D_MODEL, ODD_COLS), D_MODEL ** -0.5),
        'w_out_b': normal((N_ODD, MIX_ODD, D_MODEL), MIX_ODD ** -0.5),
        'lru_conv_w': normal((N_ODD, CONV_W, LRU_W), CONV_W ** -0.5),
        'lru_conv_b': normal((N_ODD, LRU_W), 0.01),
        'lru_wa': normal((N_ODD, LRU_BLOCKS, LRU_BW, LRU_BW), LRU_BW ** -0.5),
        'lru_ba': normal((N_ODD, LRU_W), 0.01),
        'lru_wx': normal((N_ODD, LRU_BLOCKS, LRU_BW, LRU_BW), LRU_BW ** -0.5),
        'lru_bx': normal((N_ODD, LRU_W), 0.01),
        'lru_lambda': lam,
        'hg_lb': normal((DEPTH, HG_KW), 1.0),
        'hg_norm': gain((N_ODD, HG_VW)),
    }


def reference(x, norm_mix, norm_mlp, w_ff1, w_ff2, w_in_a, w_out_a, nsa_qk_gain, nsa_cmp_w, nsa_cmp_pe,
              rw_mu, rw_w0, rw_w2, rw_a0, rw_a2, rw_g2, rw_k_k, rw_k_a, rw_r_k, rw_lnx_w, rw_lnx_b,
              w_in_b, w_out_b, lru_conv_w, lru_conv_b, lru_wa, lru_ba, lru_wx, lru_bx, lru_lambda,
              hg_lb, hg_norm):
    B, S, _ = x.shape
    cos, sin = _rope_tables(jnp.arange(S, dtype=jnp.float32))
    lb_p = jax.nn.softmax(hg_lb.astype(jnp.float32), axis=0)
    lb_cum = jnp.cumsum(lb_p, axis=0)
    hg_lower = lb_cum - lb_cum[0:1]
    for layer in range(DEPTH):
        h = _rms_norm(x, norm_mix[layer])
        if layer % 2 == 0:
            e = layer // 2
            u = h @ w_in_a[e]
            u_nsa, u_rw = _split(u, [NSA_COLS, RW_COLS])
            y_a = _nsa(u_nsa, nsa_qk_gain[e], nsa_cmp_w[e], nsa_cmp_pe[e], cos, sin)
            y_b = _rwkv7(u_rw, rw_mu[e], rw_w0[e], rw_w2[e], rw_a0[e], rw_a2[e], rw_g2[e],
                         rw_k_k[e], rw_k_a[e], rw_r_k[e], rw_lnx_w[e], rw_lnx_b[e])
            y = jnp.concatenate([y_a, y_b], axis=-1) @ w_out_a[e]
        else:
            o = layer // 2
            u = h @ w_in_b[o]
            u_lru, u_hg = _split(u, [2 * LRU_W, 2 * HG_KW + 2 * HG_VW])
            y_c = _rglru(u_lru, lru_conv_w[o], lru_conv_b[o], lru_wa[o], lru_ba[o], lru_wx[o], lru_bx[o], lru_lambda[o])
            y_d = _hgrn2(u_hg, hg_lower[layer], hg_norm[o])
            y = jnp.concatenate([y_c, y_d], axis=-1) @ w_out_b[o]
        x = x + y.astype(x.dtype)
        x = x + _sq_relu_mlp(_rms_norm(x, norm_mlp[layer]), w_ff1[layer], w_ff2[layer]).astype(x.dtype)
    return x
```

```python
from contextlib import ExitStack
import numpy as np
import concourse.bass as bass
import concourse.mybir as mybir
from concourse.bass_utils import run_bass_kernel_spmd

F32 = mybir.dt.float32
BF16 = mybir.dt.bfloat16
AF = mybir.ActivationFunctionType
ALU = mybir.AluOpType
AX = mybir.AxisListType

NCORES = 8
ROT = 4000


class Dep:
    __slots__ = ("w", "r", "name")

    def __init__(self, name=None):
        self.w = None
        self.r = []
        self.name = name


class Prog:
    def __init__(self, nc, es, n_dma_sems=6):
        self.nc = nc
        self.es = es
        self.eng = {"pe": nc.tensor, "act": nc.scalar, "dve": nc.vector,
                    "pool": nc.gpsimd, "sp": nc.sync}
        self.sem = {}
        self.cnt = {}
        self.waited = {e: {} for e in self.eng}
        for e in self.eng:
            self.sem[e] = es.enter_context(nc.semaphore("s_" + e))
            self.cnt[e] = 0
        self.pe_sems = {id(self.sem["pe"])}
        self.dma_sems = {}
        for q in ("sp", "pool", "act"):
            self.dma_sems[q] = [[es.enter_context(nc.semaphore("d_%s%d" % (q, i))), 0]
                                for i in range(n_dma_sems)]
        self.dma_rr = {q: 0 for q in self.dma_sems}
        self.sem_ids = {}
        self.out_events = []
        self.ninstr = 0

    def sbuf(self, name, shape, dtype):
        return self.es.enter_context(self.nc.sbuf_tensor(name, list(shape), dtype))

    def psum(self, name, shape, dtype=F32):
        return self.es.enter_context(self.nc.psum_tensor(name, list(shape), dtype))

    def dep(self, name=None):
        return Dep(name)

    def _wait(self, e, ev):
        sem, val = ev
        k = id(sem)
        if self.waited[e].get(k, 0) >= val:
            return
        self.waited[e][k] = val
        self.eng[e].wait_ge(sem, val)

    def _needs(self, e, reads, writes):
        evs = []
        for d in reads:
            if d.w is not None:
                evs.append(d.w)
        for d in writes:
            if d.w is not None:
                evs.append(d.w)
            evs.extend(d.r)
        for ev in evs:
            if e == "pe" and id(ev[0]) in self.pe_sems:
                continue
            self._wait(e, ev)

    def _record(self, ev, reads, writes):
        for d in reads:
            d.r.append(ev)
            if len(d.r) > 64:
                d.r = d.r[-64:] if False else d.r
        for d in writes:
            d.w = ev
            d.r = []

    def I(self, e, fn, *args, reads=(), writes=(), **kw):
        self._needs(e, reads, writes)
        if self.cnt[e] >= ROT:
            self.sem[e] = self.es.enter_context(self.nc.semaphore("s_%s_%d" % (e, self.ninstr)))
            self.cnt[e] = 0
            self.old_final = getattr(self, "old_final", [])
            if e == "pe":
                self.pe_sems.add(id(self.sem[e]))
        ins = fn(*args, **kw)
        self.cnt[e] += 1
        ins.then_inc(self.sem[e], 1)
        ev = (self.sem[e], self.cnt[e])
        self._record(ev, reads, writes)
        self.ninstr += 1
        return ev

    def dma(self, q, out, in_, reads=(), writes=(), is_output=False, **kw):
        slots = self.dma_sems[q]
        i = self.dma_rr[q]
        self.dma_rr[q] = (i + 1) % len(slots)
        slot = slots[i]
        if slot[1] > 0:
            self._wait(q, (slot[0], slot[1]))
        self._needs(q, reads, writes)
        ins = self.eng[q].dma_start(out=out, in_=in_, **kw)
        slot[1] += 16
        ins.then_inc(slot[0], 16)
        ev = (slot[0], slot[1])
        self._record(ev, reads, writes)
        if is_output:
            self.out_events.append(ev)
        self.ninstr += 1
        return ev

    def finish(self):
        for q, slots in self.dma_sems.items():
            for s, v in slots:
                if v > 0:
                    self._wait("sp", (s, v))
        for e in ("pe", "act", "dve", "pool"):
            if self.cnt[e] > 0:
                self._wait("sp", (self.sem[e], self.cnt[e]))


def new_nc():
    return bass.Bass("TRN2", target_bir_lowering=False)


NT = 512
TOK = 2048
D = 2048
KC = 16


class DenseRes:
    def __init__(self, P, nw=3, nps=4, sq=None):
        self.P = P
        self.nw = nw
        self.wf = [P.sbuf("wf%d" % i, [128, 16, 128], F32) for i in range(nw)]
        self.wb = [P.sbuf("wb%d" % i, [128, 16, 128], BF16) for i in range(nw)]
        self.wf_d = [P.dep() for _ in range(nw)]
        self.wb_d = [P.dep() for _ in range(nw)]
        self.nps = nps
        self.ps = [P.psum("dps%d" % i, [128, NT], F32) for i in range(nps)]
        self.ps_d = [P.dep() for _ in range(nps)]
        self.wi = 0
        self.pi = 0
        self.ones = P.sbuf("ones_bf", [128, 128], BF16)
        self.ones_d = P.dep()
        P.I("dve", P.nc.vector.memset, self.ones[:], 1.0, writes=[self.ones_d])
        self.ss_ps = P.psum("ss_ps", [128, NT], F32)
        self.ss_d = P.dep()
        if sq is None:
            self.sq = P.sbuf("sq", [128, KC, NT], BF16)
            self.sq_d = P.dep()
        else:
            self.sq, self.sq_d = sq
        self.rs = P.sbuf("rs", [128, NT], F32)
        self.rs_d = P.dep()
        self.castsel = 0


def rmsnorm_T(P, R, xs, xs_d, gain_sb, gain_d, gcol0, hT, hT_d, eps=1e-6):
    nc = P.nc
    for c in range(KC):
        P.I("act", nc.scalar.activation, out=R.sq[:, c, :], in_=xs[:, c, :], func=AF.Square,
            reads=[xs_d], writes=[R.sq_d])
    for c in range(KC):
        P.I("pe", nc.tensor.matmul, R.ss_ps[:], lhsT=R.ones[:], rhs=R.sq[:, c, :],
            start=(c == 0), stop=(c == KC - 1), reads=[R.ones_d, R.sq_d], writes=[R.ss_d])
    P.I("dve", nc.vector.tensor_scalar, out=R.rs[:], in0=R.ss_ps[:], scalar1=1.0 / D, scalar2=eps,
        op0=ALU.mult, op1=ALU.add, reads=[R.ss_d], writes=[R.rs_d])
    P.I("act", nc.scalar.activation, out=R.rs[:], in_=R.rs[:], func=AF.Sqrt,
        reads=[R.rs_d], writes=[R.rs_d])
    P.I("dve", nc.vector.reciprocal, out=R.rs[:], in_=R.rs[:], reads=[R.rs_d], writes=[R.rs_d])
    for c in range(KC):
        P.I("dve", nc.vector.scalar_tensor_tensor, out=hT[:, c, :], in0=xs[:, c, :],
            scalar=gain_sb[:, gcol0 + c:gcol0 + c + 1], in1=R.rs[:], op0=ALU.mult, op1=ALU.mult,
            reads=[xs_d, gain_d, R.rs_d], writes=[hT_d])


def dense(P, R, wt, n_kg, n_oc, acts, epilogue, q="sp"):
    nc = P.nc
    tiles = [(oc, kg) for oc in range(n_oc) for kg in range(n_kg)]
    PF = R.nw - 1
    slots = {}

    def issue_load(i):
        oc, kg = tiles[i]
        s = R.wi
        R.wi = (R.wi + 1) % R.nw
        slots[i] = s
        P.dma(q, R.wf[s][:].rearrange("p c j -> p (c j)"), wt[kg, oc], writes=[R.wf_d[s]])

    for i in range(min(PF, len(tiles))):
        issue_load(i)
    ps = None
    for i, (oc, kg) in enumerate(tiles):
        if i + PF < len(tiles):
            issue_load(i + PF)
        s = slots.pop(i)
        R.castsel ^= 1
        if R.castsel:
            P.I("pool", nc.gpsimd.tensor_copy, out=R.wb[s][:], in_=R.wf[s][:],
                reads=[R.wf_d[s]], writes=[R.wb_d[s]])
        else:
            P.I("act", nc.scalar.copy, out=R.wb[s][:], in_=R.wf[s][:],
                reads=[R.wf_d[s]], writes=[R.wb_d[s]])
        if kg == 0:
            pi = R.pi
            R.pi = (R.pi + 1) % R.nps
            ps, ps_d = R.ps[pi], R.ps_d[pi]
        a, a_d = acts[kg]
        for c in range(KC):
            P.I("pe", nc.tensor.matmul, ps[:], lhsT=R.wb[s][:, c, :], rhs=a[:, c, :],
                start=(kg == 0 and c == 0), stop=(kg == n_kg - 1 and c == KC - 1),
                reads=[R.wb_d[s], a_d], writes=[ps_d])
        if kg == n_kg - 1:
            epilogue(oc, ps, ps_d)


def build_stage_a(n_oc):
    nc = new_nc()
    xT = nc.dram_tensor("xT", [D, TOK], F32, kind="ExternalInput").ap()
    gain = nc.dram_tensor("gain", [128, KC], F32, kind="ExternalInput").ap()
    wt = nc.dram_tensor("wt", [1, n_oc, 128, 2048], F32, kind="ExternalInput").ap()
    uT = nc.dram_tensor("uT", [n_oc * 128, TOK], F32, kind="ExternalOutput").ap()
    with ExitStack() as es:
        P = Prog(nc, es)
        R = DenseRes(P)
        gain_sb = P.sbuf("gain_sb", [128, KC], F32)
        gain_d = P.dep()
        P.dma("sp", gain_sb[:], gain, writes=[gain_d])
        xs = P.sbuf("xs", [128, KC, NT], F32)
        xs_d = P.dep()
        hT = P.sbuf("hT", [128, KC, NT], BF16)
        hT_d = P.dep()
        NO = 3
        uo = [P.sbuf("uo%d" % i, [128, NT], F32) for i in range(NO)]
        uo_d = [P.dep() for _ in range(NO)]
        oi = [0]
        xTv = xT.rearrange("(c p) t -> p c t", p=128)
        for tt in range(TOK // NT):
            tsl = slice(tt * NT, (tt + 1) * NT)
            P.dma("sp", xs[:], xTv[:, :, tsl], writes=[xs_d])
            rmsnorm_T(P, R, xs, xs_d, gain_sb, gain_d, 0, hT, hT_d)

            def epi(oc, ps, ps_d):
                o = oi[0]
                oi[0] = (o + 1) % NO
                P.I("act", nc.scalar.copy, out=uo[o][:], in_=ps[:], reads=[ps_d], writes=[uo_d[o]])
                P.dma("sp", uT[oc * 128:(oc + 1) * 128, tsl], uo[o][:], reads=[uo_d[o]], is_output=True)

            dense(P, R, wt, 1, n_oc, [(hT, hT_d)], epi)
        P.finish()
    return nc


def build_stage_c(n_oc_next):
    nc = new_nc()
    xT = nc.dram_tensor("xT", [D, TOK], F32, kind="ExternalInput").ap()
    yT = nc.dram_tensor("yT", [D, TOK], F32, kind="ExternalInput").ap()
    gain = nc.dram_tensor("gain", [128, 2 * KC], F32, kind="ExternalInput").ap()
    wo = nc.dram_tensor("wo", [1, 16, 128, 2048], F32, kind="ExternalInput").ap()
    w1 = nc.dram_tensor("w1", [1, 64, 128, 2048], F32, kind="ExternalInput").ap()
    w2 = nc.dram_tensor("w2", [4, 16, 128, 2048], F32, kind="ExternalInput").ap()
    if n_oc_next:
        wn = nc.dram_tensor("wn", [1, n_oc_next, 128, 2048], F32, kind="ExternalInput").ap()
        uT = nc.dram_tensor("uT", [n_oc_next * 128, TOK], F32, kind="ExternalOutput").ap()
    x2T = nc.dram_tensor("x2T", [D, TOK], F32, kind="ExternalOutput").ap()
    with ExitStack() as es:
        P = Prog(nc, es)
        gain_sb = P.sbuf("gain_sb", [128, 2 * KC], F32)
        gain_d = P.dep()
        P.dma("sp", gain_sb[:], gain, writes=[gain_d])
        xs = P.sbuf("xs", [128, KC, NT], F32)
        xs_d = P.dep()
        aT = P.sbuf("aT", [128, KC, NT], BF16)
        aT_d = P.dep()
        zT = P.sbuf("zT", [128, 64, NT], BF16)
        zT_d = P.dep()
        yst = [P.sbuf("yst%d" % i, [128, 4, NT], F32) for i in range(2)]
        yst_d = [P.dep() for _ in range(2)]
        NO = 3
        uo = [P.sbuf("uo%d" % i, [128, NT], F32) for i in range(NO)]
        uo_d = [P.dep() for _ in range(NO)]
        oi = [0]
        R = DenseRes(P, sq=(zT, zT_d))
        xTv = xT.rearrange("(c p) t -> p c t", p=128)
        yTv = yT.rearrange("(c p) t -> p c t", p=128)
        x2Tv = x2T.rearrange("(c p) t -> p c t", p=128)

        def nxt():
            o = oi[0]
            oi[0] = (o + 1) % NO
            return o

        for tt in range(TOK // NT):
            tsl = slice(tt * NT, (tt + 1) * NT)
            P.dma("sp", xs[:], xTv[:, :, tsl], writes=[xs_d])
            for j in range(4):
                b = j % 2
                P.dma("sp", yst[b][:], yTv[:, 4 * j:4 * j + 4, tsl], writes=[yst_d[b]])
                P.I("dve", nc.vector.tensor_copy, out=aT[:, 4 * j:4 * j + 4, :], in_=yst[b][:],
                    reads=[yst_d[b]], writes=[aT_d])

            def epi_res(oc, ps, ps_d):
                P.I("dve", nc.vector.tensor_tensor, out=xs[:, oc, :], in0=ps[:], in1=xs[:, oc, :],
                    op=ALU.add, reads=[ps_d], writes=[xs_d])

            dense(P, R, wo, 1, 16, [(aT, aT_d)], epi_res)
            rmsnorm_T(P, R, xs, xs_d, gain_sb, gain_d, 0, aT, aT_d)

            def epi_relu2(oc, ps, ps_d):
                o = nxt()
                P.I("act", nc.scalar.activation, out=uo[o][:], in_=ps[:], func=AF.Square,
                    reads=[ps_d], writes=[uo_d[o]])
                P.I("dve", nc.vector.scalar_tensor_tensor, out=zT[:, oc, :], in0=ps[:], scalar=0.0,
                    in1=uo[o][:], op0=ALU.is_gt, op1=ALU.mult, reads=[ps_d, uo_d[o]], writes=[zT_d])

            dense(P, R, w1, 1, 64, [(aT, aT_d)], epi_relu2)
            dense(P, R, w2, 4, 16, [(zT[:, 16 * k:16 * k + 16, :], zT_d) for k in range(4)], epi_res)
            P.dma("sp", x2Tv[:, :, tsl], xs[:], reads=[xs_d], is_output=True)
            if n_oc_next:
                rmsnorm_T(P, R, xs, xs_d, gain_sb, gain_d, KC, aT, aT_d)

                def epi_out(oc, ps, ps_d):
                    o = nxt()
                    P.I("act", nc.scalar.copy, out=uo[o][:], in_=ps[:], reads=[ps_d], writes=[uo_d[o]])
                    P.dma("sp", uT[oc * 128:(oc + 1) * 128, tsl], uo[o][:], reads=[uo_d[o]], is_output=True)

                dense(P, R, wn, 1, n_oc_next, [(aT, aT_d)], epi_out)
        P.finish()
    return nc


def gain_layout(g):
    return np.ascontiguousarray(np.asarray(g, np.float32).reshape(-1, 128).T)


def wtile(W, n_oc=None):
    W = np.asarray(W, np.float32)
    K_, N = W.shape
    if n_oc is None:
        n_oc = (N + 127) // 128
    if n_oc * 128 != N:
        W = np.concatenate([W, np.zeros((K_, n_oc * 128 - N), np.float32)], axis=1)
    n_kg = K_ // 2048
    return np.ascontiguousarray(W.reshape(n_kg, 16, 128, n_oc, 128).transpose(0, 3, 2, 1, 4).reshape(n_kg, n_oc, 128, 2048))


G = 512


def build_mix_b(S):
    nc = new_nc()
    dt = lambda n, shp: nc.dram_tensor(n, shp, F32, kind="ExternalInput").ap()
    l_gate = dt("l_gate", [128, S])
    l_xb = dt("l_xb", [128, S])
    l_par = dt("l_par", [128, 16])
    l_wa = dt("l_wa", [128, 128])
    l_wx = dt("l_wx", [128, 128])
    h_q = dt("h_q", [128, S])
    h_f = dt("h_f", [128, S])
    h_g = dt("h_g", [128, S])
    h_ftok = dt("h_ftok", [S, 128])
    h_vtok = dt("h_vtok", [S, 128])
    h_par = dt("h_par", [128, 8])
    h_lbrow = dt("h_lbrow", [128, 256])
    cst = dt("cst", [128, 1024])
    yT = nc.dram_tensor("yT", [256, S], F32, kind="ExternalOutput").ap()
    NG = S // G
    with ExitStack() as es:
        P = Prog(nc, es)
        I = P.I
        act, vec, pe = nc.scalar, nc.vector, nc.tensor

        def T(name, shape, dtype=F32):
            return P.sbuf(name, shape, dtype), P.dep(name)

        def PS(name, shape):
            return P.psum(name, shape, F32), P.dep(name)

        lpar, lpar_d = T("lpar", [128, 16])
        lwa, lwa_d = T("lwa", [128, 128])
        lwx, lwx_d = T("lwx", [128, 128])
        hpar, hpar_d = T("hpar", [128, 8])
        lbrow, lbrow_d = T("lbrow", [128, 256])
        cs, cs_d = T("cs", [128, 1024])
        for (t, d, src) in ((lpar, lpar_d, l_par), (lwa, lwa_d, l_wa), (lwx, lwx_d, l_wx),
                            (hpar, hpar_d, h_par), (lbrow, lbrow_d, h_lbrow), (cs, cs_d, cst)):
            P.dma("sp", t[:], src, writes=[d])
        Mgt32, Mle, cmask = cs[0:32, 0:32], cs[:, 128:256], cs[:, 256:768]
        ones_bf, ones_d = T("ones_bf", [128, 128], BF16)
        I("dve", vec.memset, ones_bf[:], 1.0, writes=[ones_d])
        lder, lder_d = T("lder", [128, 4])
        I("act", act.activation, out=lder[:, 0:1], in_=lpar[:, 7:8], func=AF.Exp, scale=-1.0,
          reads=[lpar_d], writes=[lder_d])
        I("act", act.activation, out=lder[:, 1:2], in_=lder[:, 0:1], func=AF.Ln, bias=1.0,
          reads=[lder_d], writes=[lder_d])
        I("dve", vec.tensor_scalar, out=lder[:, 2:3], in0=lder[:, 1:2], scalar1=-8.0, scalar2=None,
          op0=ALU.mult, reads=[lder_d], writes=[lder_d])
        I("dve", vec.tensor_scalar, out=lder[:, 3:4], in0=lder[:, 1:2], scalar1=-16.0, scalar2=None,
          op0=ALU.mult, reads=[lder_d], writes=[lder_d])
        hder, hder_d = T("hder", [128, 4])
        I("dve", vec.tensor_tensor, out=hder[:, 0:1], in0=hpar[:, 1:2], in1=hpar[:, 0:1], op=ALU.subtract,
          reads=[hpar_d], writes=[hder_d])
        I("act", act.activation, out=hder[:, 0:1], in_=hder[:, 0:1], func=AF.Sigmoid,
          reads=[hder_d], writes=[hder_d])
        I("dve", vec.tensor_scalar, out=hder[:, 1:2], in0=hder[:, 0:1], scalar1=-1.0, scalar2=1.0,
          op0=ALU.mult, op1=ALU.add, reads=[hder_d], writes=[hder_d])
        I("dve", vec.tensor_scalar, out=hder[:, 2:3], in0=hder[:, 0:1], scalar1=-1.0, scalar2=None,
          op0=ALU.add, reads=[hder_d], writes=[hder_d])
        omlrow, omlrow_d = T("omlrow", [128, 128])
        I("dve", vec.tensor_tensor, out=omlrow[:], in0=lbrow[:, 128:256], in1=lbrow[:, 0:128], op=ALU.subtract,
          reads=[lbrow_d], writes=[omlrow_d])
        I("act", act.activation, out=omlrow[:], in_=omlrow[:], func=AF.Sigmoid, reads=[omlrow_d], writes=[omlrow_d])
        I("dve", vec.tensor_scalar, out=omlrow[:], in0=omlrow[:], scalar1=-1.0, scalar2=1.0,
          op0=ALU.mult, op1=ALU.add, reads=[omlrow_d], writes=[omlrow_d])

        names = ["xb", "gt", "xc", "r", "ii", "a", "a2", "h", "t1", "t2"]
        L = {n: T("L_" + n, [128, G + 3] if n == "xb" else [128, G]) for n in names}
        lr_ps, lr_ps_d = PS("lr_ps", [128, G])
        li_ps, li_ps_d = PS("li_ps", [128, G])
        I("dve", vec.memset, L["xb"][0][:, 0:3], 0.0, writes=[L["xb"][1]])
        hprev, hprev_d = T("hprev", [128, 1])
        I("dve", vec.memset, hprev[:], 0.0, writes=[hprev_d])

        hn = ["f", "q", "g", "sg", "k", "lf", "bc", "E", "Ei", "qt", "kt", "o", "y", "rs"]
        H = {n: T("H_" + n, [128, G]) for n in hn}
        osq, osq_d = T("H_osq", [128, G], BF16)
        ftok, ftok_d = T("ftok", [32, 16, 128])
        vtok, vtok_d = T("vtok", [128, 4, 128])
        vtok32, vtok32_d = T("vtok32", [32, 16, 128])
        ktok, ktok_d = T("ktok", [32, 16, 128])
        lftok, lftok_d = T("lftok", [32, 16, 128])
        khat, khat_d = T("khat", [32, 16, 128])
        attm, attm_d = T("attm", [128, 128])
        Sb = [T("Sst%d" % i, [128, 128]) for i in range(2)]
        I("dve", vec.memset, Sb[0][0][:], 0.0, writes=[Sb[0][1]])
        ex_ps, ex_ps_d = PS("ex_ps", [32, 4, 128])
        att_ps, att_ps_d = PS("att_ps", [128, 128])
        o_ps, o_ps_d = PS("o_ps", [128, 128])
        sk_ps, sk_ps_d = PS("sk_ps", [128, 128])
        ss_ps, ss_ps_d = PS("ss_ps", [128, G])
        si = 0
        ftv = h_ftok.rearrange("(b p) i -> p b i", p=32)
        vtv32 = h_vtok.rearrange("(b p) i -> p b i", p=32)
        vtv = h_vtok.rearrange("(b p) i -> p b i", p=128)

        for g in range(NG):
            gs = slice(g * G, (g + 1) * G)
            xb, xb_d = L["xb"]
            gt, gt_d = L["gt"]
            xc, xc_d = L["xc"]
            P.dma("sp", xb[:, 3:G + 3], l_xb[:, gs], writes=[xb_d])
            P.dma("sp", gt[:], l_gate[:, gs], writes=[gt_d])
            I("dve", vec.tensor_scalar, out=xc[:], in0=xb[:, 0:G], scalar1=lpar[:, 0:1], scalar2=lpar[:, 4:5],
              op0=ALU.mult, op1=ALU.add, reads=[xb_d, lpar_d], writes=[xc_d])
            for j in range(1, 4):
                I("dve", vec.scalar_tensor_tensor, out=xc[:], in0=xb[:, j:G + j], scalar=lpar[:, j:j + 1],
                  in1=xc[:], op0=ALU.mult, op1=ALU.add, reads=[xb_d, lpar_d, xc_d], writes=[xc_d])
            I("pe", pe.matmul, lr_ps[:], lhsT=lwa[:], rhs=xc[:], start=True, stop=True,
              reads=[lwa_d, xc_d], writes=[lr_ps_d])
            I("pe", pe.matmul, li_ps[:], lhsT=lwx[:], rhs=xc[:], start=True, stop=True,
              reads=[lwx_d, xc_d], writes=[li_ps_d])
            r, r_d = L["r"]
            ii, ii_d = L["ii"]
            a, a_d = L["a"]
            a2, a2_d = L["a2"]
            h, h_d = L["h"]
            t1, t1_d = L["t1"]
            t2, t2_d = L["t2"]
            I("act", act.activation, out=r[:], in_=lr_ps[:], func=AF.Sigmoid, bias=lpar[:, 5:6],
              reads=[lr_ps_d, lpar_d], writes=[r_d])
            I("act", act.activation, out=ii[:], in_=li_ps[:], func=AF.Sigmoid, bias=lpar[:, 6:7],
              reads=[li_ps_d, lpar_d], writes=[ii_d])
            I("act", act.activation, out=a[:], in_=r[:], func=AF.Exp, scale=lder[:, 2:3],
              reads=[r_d, lder_d], writes=[a_d])
            I("act", act.activation, out=a2[:], in_=r[:], func=AF.Exp, scale=lder[:, 3:4],
              reads=[r_d, lder_d], writes=[a2_d])
            I("dve", vec.tensor_scalar, out=a2[:], in0=a2[:], scalar1=-1.0, scalar2=1.0, op0=ALU.mult, op1=ALU.add,
              reads=[a2_d], writes=[a2_d])
            I("act", act.activation, out=a2[:], in_=a2[:], func=AF.Sqrt, reads=[a2_d], writes=[a2_d])
            I("dve", vec.tensor_tensor, out=ii[:], in0=ii[:], in1=xc[:], op=ALU.mult, reads=[ii_d, xc_d], writes=[ii_d])
            I("dve", vec.tensor_tensor, out=ii[:], in0=ii[:], in1=a2[:], op=ALU.mult, reads=[ii_d, a2_d], writes=[ii_d])
            I("dve", vec.tensor_copy, out=xb[:, 0:3], in_=xb[:, G:G + 3], reads=[xb_d], writes=[xb_d])
            I("dve", vec.tensor_tensor_scan, out=h[:], data0=a[:], data1=ii[:], initial=hprev[:],
              op0=ALU.mult, op1=ALU.add, reads=[a_d, ii_d, hprev_d], writes=[h_d])
            I("dve", vec.tensor_copy, out=hprev[:], in_=h[:, G - 1:G], reads=[h_d], writes=[hprev_d])
            I("act", act.activation, out=t1[:], in_=gt[:], func=AF.Square, reads=[gt_d], writes=[t1_d])
            I("dve", vec.tensor_scalar, out=t1[:], in0=t1[:], scalar1=0.044715, scalar2=1.0, op0=ALU.mult, op1=ALU.add,
              reads=[t1_d], writes=[t1_d])
            I("dve", vec.tensor_tensor, out=t1[:], in0=t1[:], in1=gt[:], op=ALU.mult, reads=[t1_d, gt_d], writes=[t1_d])
            I("act", act.activation, out=t1[:], in_=t1[:], func=AF.Sigmoid, scale=1.5957691216057308,
              reads=[t1_d], writes=[t1_d])
            I("dve", vec.tensor_tensor, out=t1[:], in0=t1[:], in1=gt[:], op=ALU.mult, reads=[t1_d, gt_d], writes=[t1_d])
            I("dve", vec.tensor_tensor, out=t2[:], in0=t1[:], in1=h[:], op=ALU.mult, reads=[t1_d, h_d], writes=[t2_d])
            P.dma("sp", yT[0:128, gs], t2[:], reads=[t2_d], is_output=True)

            f_, f_d = H["f"]
            q_, q_d = H["q"]
            g_, g_d = H["g"]
            P.dma("sp", f_[:], h_f[:, gs], writes=[f_d])
            P.dma("sp", q_[:], h_q[:, gs], writes=[q_d])
            P.dma("sp", g_[:], h_g[:, gs], writes=[g_d])
            P.dma("sp", ftok[:], ftv[:, 16 * g:16 * g + 16, :], writes=[ftok_d])
            P.dma("sp", vtok32[:], vtv32[:, 16 * g:16 * g + 16, :], writes=[vtok32_d])
            P.dma("sp", vtok[:], vtv[:, 4 * g:4 * g + 4, :], writes=[vtok_d])
            sg, sg_d = H["sg"]
            k_, k_d = H["k"]
            lf, lf_d = H["lf"]
            bc, bc_d = H["bc"]
            E, E_d = H["E"]
            Ei, Ei_d = H["Ei"]
            qt, qt_d = H["qt"]
            kt, kt_d = H["kt"]
            o_, o_d = H["o"]
            y_, y_d = H["y"]
            rs, rs_d = H["rs"]
            I("act", act.activation, out=sg[:], in_=f_[:], func=AF.Sigmoid, scale=-1.0, reads=[f_d], writes=[sg_d])
            I("dve", vec.tensor_scalar, out=k_[:], in0=sg[:], scalar1=hder[:, 1:2], scalar2=None, op0=ALU.mult,
              reads=[sg_d, hder_d], writes=[k_d])
            I("dve", vec.tensor_scalar, out=lf[:], in0=k_[:], scalar1=-1.0, scalar2=1.0, op0=ALU.mult, op1=ALU.add,
              reads=[k_d], writes=[lf_d])
            I("act", act.activation, out=lf[:], in_=lf[:], func=AF.Ln, reads=[lf_d], writes=[lf_d])
            I("dve", vec.tensor_tensor_scan, out=bc[:], data0=cmask, data1=lf[:], initial=0.0,
              op0=ALU.mult, op1=ALU.add, reads=[cs_d, lf_d], writes=[bc_d])
            I("act", act.activation, out=E[:], in_=bc[:], func=AF.Exp, reads=[bc_d], writes=[E_d])
            I("act", act.activation, out=Ei[:], in_=bc[:], func=AF.Exp, scale=-1.0, reads=[bc_d], writes=[Ei_d])
            I("act", act.activation, out=qt[:], in_=q_[:], func=AF.Silu, reads=[q_d], writes=[qt_d])
            I("dve", vec.tensor_tensor, out=qt[:], in0=qt[:], in1=E[:], op=ALU.mult, reads=[qt_d, E_d], writes=[qt_d])
            I("dve", vec.tensor_tensor, out=kt[:], in0=k_[:], in1=Ei[:], op=ALU.mult, reads=[k_d, Ei_d], writes=[kt_d])
            I("act", act.activation, out=ktok[:], in_=ftok[:], func=AF.Sigmoid, scale=-1.0,
              reads=[ftok_d], writes=[ktok_d])
            I("dve", vec.tensor_tensor, out=ktok[:], in0=ktok[:],
              in1=omlrow[0:32, :].unsqueeze(1).to_broadcast([32, 16, 128]), op=ALU.mult,
              reads=[ktok_d, omlrow_d], writes=[ktok_d])
            I("dve", vec.tensor_scalar, out=lftok[:], in0=ktok[:], scalar1=-1.0, scalar2=1.0, op0=ALU.mult, op1=ALU.add,
              reads=[ktok_d], writes=[lftok_d])
            I("act", act.activation, out=lftok[:], in_=lftok[:], func=AF.Ln, reads=[lftok_d], writes=[lftok_d])
            for b in range(4):
                I("pe", pe.matmul, ex_ps[:], lhsT=Mgt32, rhs=lftok[:, 4 * b:4 * b + 4, :], start=True, stop=True,
                  reads=[cs_d, lftok_d], writes=[ex_ps_d])
                I("act", act.activation, out=khat[:, 4 * b:4 * b + 4, :], in_=ex_ps[:], func=AF.Exp,
                  reads=[ex_ps_d], writes=[khat_d])
            I("dve", vec.tensor_tensor, out=khat[:], in0=khat[:], in1=ktok[:], op=ALU.mult,
              reads=[khat_d, ktok_d], writes=[khat_d])
            for b in range(4):
                bs = slice(b * 128, (b + 1) * 128)
                I("pe", pe.matmul, att_ps[:], lhsT=kt[:, bs], rhs=qt[:, bs], start=True, stop=True,
                  reads=[kt_d, qt_d], writes=[att_ps_d])
                I("dve", vec.tensor_tensor, out=attm[:], in0=att_ps[:], in1=Mle, op=ALU.mult,
                  reads=[att_ps_d, cs_d], writes=[attm_d])
                I("pe", pe.matmul, o_ps[:], lhsT=vtok[:, b, :], rhs=attm[:], start=True, stop=False,
                  reads=[vtok_d, attm_d], writes=[o_ps_d])
                for k in range(4):
                    c0 = b * 128 + k * 32
                    Sc, Sc_d = Sb[si]
                    Sn, Sn_d = Sb[1 - si]
                    I("pe", pe.matmul, o_ps[:, k * 32:(k + 1) * 32], lhsT=Sc[:], rhs=qt[:, c0:c0 + 32],
                      start=False, stop=(k == 3), reads=[Sc_d, qt_d], writes=[o_ps_d])
                    I("pe", pe.matmul, sk_ps[:], lhsT=khat[:, 4 * b + k, :], rhs=vtok32[:, 4 * b + k, :],
                      start=True, stop=True, reads=[khat_d, vtok32_d], writes=[sk_ps_d])
                    I("dve", vec.scalar_tensor_tensor, out=Sn[:], in0=Sc[:], scalar=E[:, c0 + 31:c0 + 32],
                      in1=sk_ps[:], op0=ALU.mult, op1=ALU.add, reads=[Sc_d, E_d, sk_ps_d], writes=[Sn_d])
                    si = 1 - si
                I("act", act.copy, out=o_[:, bs], in_=o_ps[:], reads=[o_ps_d], writes=[o_d])
            I("act", act.activation, out=osq[:], in_=o_[:], func=AF.Square, reads=[o_d], writes=[osq_d])
            I("pe", pe.matmul, ss_ps[:], lhsT=ones_bf[:], rhs=osq[:], start=True, stop=True,
              reads=[ones_d, osq_d], writes=[ss_ps_d])
            I("dve", vec.tensor_scalar, out=rs[:], in0=ss_ps[:], scalar1=1.0 / 128, scalar2=1e-6, op0=ALU.mult, op1=ALU.add,
              reads=[ss_ps_d], writes=[rs_d])
            I("act", act.activation, out=rs[:], in_=rs[:], func=AF.Sqrt, reads=[rs_d], writes=[rs_d])
            I("dve", vec.reciprocal, out=rs[:], in_=rs[:], reads=[rs_d], writes=[rs_d])
            I("dve", vec.tensor_tensor, out=o_[:], in0=o_[:], in1=rs[:], op=ALU.mult, reads=[o_d, rs_d], writes=[o_d])
            I("act", act.activation, out=g_[:], in_=g_[:], func=AF.Silu, reads=[g_d], writes=[g_d])
            I("dve", vec.scalar_tensor_tensor, out=y_[:], in0=o_[:], scalar=hpar[:, 2:3], in1=g_[:],
              op0=ALU.mult, op1=ALU.mult, reads=[o_d, hpar_d, g_d], writes=[y_d])
            P.dma("sp", yT[128:256, gs], y_[:], reads=[y_d], is_output=True)
        P.finish()
    return nc


def mixb_consts():
    t = np.arange(128)
    same = (t[:, None] // 32) == (t[None, :] // 32)
    Mgt = (same & (t[:, None] > t[None, :])).astype(np.float32)
    Mle = (same & (t[:, None] <= t[None, :])).astype(np.float32)
    cm = np.ones(512, np.float32)
    cm[::32] = 0.0
    out = np.zeros((128, 1024), np.float32)
    out[:, 0:128] = Mgt
    out[:, 128:256] = Mle
    out[:, 256:768] = cm[None, :]
    return out


def mixb_inputs(uT, p, S):
    maps = []
    cst = mixb_consts()
    for c in range(NCORES):
        r = slice(c * 128, (c + 1) * 128)
        lpar = np.zeros((128, 16), np.float32)
        lpar[:, 0:4] = p["lru_conv_w"][0][:, r].T
        lpar[:, 4] = p["lru_conv_b"][0][r]
        lpar[:, 5] = p["lru_ba"][0][r]
        lpar[:, 6] = p["lru_bx"][0][r]
        lpar[:, 7] = p["lru_lambda"][0][r]
        hpar = np.zeros((128, 8), np.float32)
        hpar[:, 0] = p["hg_lb"][0][r]
        hpar[:, 1] = p["hg_lb"][1][r]
        hpar[:, 2] = p["hg_norm"][0][r]
        lbrow = np.zeros((128, 256), np.float32)
        lbrow[:, 0:128] = p["hg_lb"][0][r][None, :]
        lbrow[:, 128:256] = p["hg_lb"][1][r][None, :]
        o = 2048
        maps.append({
            "l_gate": np.ascontiguousarray(uT[c * 128:(c + 1) * 128]),
            "l_xb": np.ascontiguousarray(uT[1024 + c * 128:1024 + (c + 1) * 128]),
            "l_par": lpar, "l_wa": np.ascontiguousarray(p["lru_wa"][0][c]),
            "l_wx": np.ascontiguousarray(p["lru_wx"][0][c]),
            "h_q": np.ascontiguousarray(uT[o + c * 128:o + (c + 1) * 128]),
            "h_f": np.ascontiguousarray(uT[o + 1024 + c * 128:o + 1024 + (c + 1) * 128]),
            "h_g": np.ascontiguousarray(uT[o + 3072 + c * 128:o + 3072 + (c + 1) * 128]),
            "h_ftok": np.ascontiguousarray(uT[o + 1024 + c * 128:o + 1024 + (c + 1) * 128].T),
            "h_vtok": np.ascontiguousarray(uT[o + 2048 + c * 128:o + 2048 + (c + 1) * 128].T),
            "h_par": hpar, "h_lbrow": lbrow, "cst": cst,
        })
    return maps


def build_rwkv(S):
    nc = new_nc()
    dt = lambda n, shp: nc.dram_tensor(n, shp, F32, kind="ExternalInput").ap()
    rkv = dt("rkv", [128, 3, S + 1])
    wl_i = dt("wl", [96, S + 1])
    al_i = dt("al", [96, S + 1])
    gl_i = dt("gl", [128, 2, S + 1])
    par = dt("par", [128, 24])
    w2_i = dt("w2", [96, 128])
    a2_i = dt("a2", [96, 128])
    g2_i = dt("g2", [128, 2, 128])
    cst = dt("cst", [128, 192])
    yT = nc.dram_tensor("yT", [128, S], F32, kind="ExternalOutput").ap()
    NG = S // G
    TC = 8
    with ExitStack() as es:
        P = Prog(nc, es)
        I = P.I
        act, vec, pe, pool = nc.scalar, nc.vector, nc.tensor, nc.gpsimd

        def T(name, shape, dtype=F32):
            return P.sbuf(name, shape, dtype), P.dep(name)

        def PS(name, shape):
            return P.psum(name, shape, F32), P.dep(name)

        pr, pr_d = T("par_sb", [128, 24])
        w2, w2_d = T("w2_sb", [96, 128])
        a2, a2_d = T("a2_sb", [96, 128])
        g2, g2_d = T("g2_sb", [128, 2, 128])
        cs, cs_d = T("cs_sb", [128, 192])
        for (t, d, src) in ((pr, pr_d, par), (w2, w2_d, w2_i), (a2, a2_d, a2_i), (g2, g2_d, g2_i), (cs, cs_d, cst)):
            P.dma("sp", t[:], src, writes=[d])
        bones = cs[:, 0:128]
        I64 = cs[:, 128:192]
        for (src, dst) in ((0, 14), (1, 15), (2, 16), (10, 17), (11, 18), (12, 19), (13, 20), (6, 21)):
            I("dve", vec.tensor_scalar, out=pr[:, dst:dst + 1], in0=pr[:, src:src + 1], scalar1=-1.0, scalar2=1.0,
              op0=ALU.mult, op1=ALU.add, reads=[pr_d], writes=[pr_d])

        x3, x3_d = T("x3", [128, 3, G + 1])
        wlr, wlr_d = T("wlr", [96, G + 1])
        alr, alr_d = T("alr", [96, G + 1])
        glr, glr_d = T("glr", [128, 2, G + 1])
        names = ["r", "k", "v", "t", "w", "a", "kk", "sq", "nr", "kf", "al", "be", "y", "yc", "gg", "tmp"]
        W = {n: T("W_" + n, [128, G]) for n in names}
        wls, wls_d = T("wls", [96, G])
        als, als_d = T("als", [96, G])
        gls, gls_d = T("gls", [128, 2, G])
        ps_a, ps_a_d = PS("ps_a", [128, G])
        ps_b, ps_b_d = PS("ps_b", [128, G])
        bcp = [PS("bcp%d" % i, [128, TC, 64]) for i in range(5)]
        Rb = [[T("Rb%d_%d" % (i, p), [128, TC, 64]) for p in range(2)] for i in range(5)]
        bc = [[T("bc%d_%d" % (i, p), [128, TC, 64]) for p in range(2)] for i in range(5)]
        St = [T("St%d" % i, [128, 64]) for i in range(2)]
        junk, _ = T("junk", [128, 64])
        sa, sa_d = T("sa", [128, 2])
        I("dve", vec.memset, St[0][0][:], 0.0, writes=[St[0][1]])
        si = 0
        par_i = 0
        nstep = 0

        def shift(dst, dst_d, src_cur, src_prev, src_d, mu_col, omm_col, np_=128):
            tmp, tmp_d = W["tmp"]
            I("dve", vec.tensor_scalar, out=tmp[0:np_, :], in0=src_prev, scalar1=pr[0:np_, mu_col:mu_col + 1], scalar2=None,
              op0=ALU.mult, reads=[src_d, pr_d], writes=[tmp_d])
            I("dve", vec.scalar_tensor_tensor, out=dst, in0=src_cur, scalar=pr[0:np_, omm_col:omm_col + 1], in1=tmp[0:np_, :],
              op0=ALU.mult, op1=ALU.add, reads=[src_d, pr_d, tmp_d], writes=[dst_d])

        for g in range(NG):
            gs = slice(g * G, (g + 1) * G)
            gs1 = slice(g * G, (g + 1) * G + 1)
            P.dma("sp", x3[:], rkv[:, :, gs1], writes=[x3_d])
            P.dma("sp", wlr[:], wl_i[:, gs1], writes=[wlr_d])
            P.dma("sp", alr[:], al_i[:, gs1], writes=[alr_d])
            P.dma("sp", glr[:], gl_i[:, :, gs1], writes=[glr_d])
            r_, r_d = W["r"]
            k_, k_d = W["k"]
            v_, v_d = W["v"]
            shift(r_[:], r_d, x3[:, 0, 1:G + 1], x3[:, 0, 0:G], x3_d, 0, 14)
            shift(k_[:], k_d, x3[:, 1, 1:G + 1], x3[:, 1, 0:G], x3_d, 1, 15)
            shift(v_[:], v_d, x3[:, 2, 1:G + 1], x3[:, 2, 0:G], x3_d, 2, 16)
            shift(wls[:], wls_d, wlr[:, 1:G + 1], wlr[:, 0:G], wlr_d, 10, 17, 96)
            shift(als[:], als_d, alr[:, 1:G + 1], alr[:, 0:G], alr_d, 11, 18, 96)
            shift(gls[:, 0, :], gls_d, glr[:, 0, 1:G + 1], glr[:, 0, 0:G], glr_d, 12, 19)
            shift(gls[:, 1, :], gls_d, glr[:, 1, 1:G + 1], glr[:, 1, 0:G], glr_d, 13, 20)
            w_, w_d = W["w"]
            a_, a_d = W["a"]
            I("act", act.activation, out=wls[:], in_=wls[:], func=AF.Tanh, reads=[wls_d], writes=[wls_d])
            I("pe", pe.matmul, ps_a[:], lhsT=w2[:], rhs=wls[:], start=True, stop=True, reads=[w2_d, wls_d], writes=[ps_a_d])
            I("act", act.activation, out=w_[:], in_=ps_a[:], func=AF.Sigmoid, bias=pr[:, 3:4], reads=[ps_a_d, pr_d], writes=[w_d])
            I("act", act.activation, out=w_[:], in_=w_[:], func=AF.Exp, scale=-0.6065306597126334, reads=[w_d], writes=[w_d])
            I("pe", pe.matmul, ps_b[:], lhsT=a2[:], rhs=als[:], start=True, stop=True, reads=[a2_d, als_d], writes=[ps_b_d])
            I("act", act.activation, out=a_[:], in_=ps_b[:], func=AF.Sigmoid, bias=pr[:, 4:5], reads=[ps_b_d, pr_d], writes=[a_d])
            gg, gg_d = W["gg"]
            I("act", act.activation, out=gls[:], in_=gls[:], func=AF.Sigmoid, reads=[gls_d], writes=[gls_d])
            for c2 in range(2):
                I("pe", pe.matmul, ps_a[:], lhsT=g2[:, c2, :], rhs=gls[:, c2, :], start=(c2 == 0), stop=(c2 == 1),
                  reads=[g2_d, gls_d], writes=[ps_a_d])
            I("act", act.copy, out=gg[:], in_=ps_a[:], reads=[ps_a_d], writes=[gg_d])
            kk, kk_d = W["kk"]
            sq, sq_d = W["sq"]
            nr, nr_d = W["nr"]
            kf, kf_d = W["kf"]
            al, al_d = W["al"]
            be, be_d = W["be"]
            I("dve", vec.tensor_scalar, out=kk[:], in0=k_[:], scalar1=pr[:, 5:6], scalar2=None, op0=ALU.mult,
              reads=[k_d, pr_d], writes=[kk_d])
            I("act", act.activation, out=sq[:], in_=kk[:], func=AF.Square, reads=[kk_d], writes=[sq_d])
            I("pe", pe.matmul, ps_b[:], lhsT=bones, rhs=sq[:], start=True, stop=True, reads=[cs_d, sq_d], writes=[ps_b_d])
            I("act", act.activation, out=nr[:], in_=ps_b[:], func=AF.Sqrt, reads=[ps_b_d], writes=[nr_d])
            I("dve", vec.tensor_scalar, out=nr[:], in0=nr[:], scalar1=1e-12, scalar2=None, op0=ALU.max,
              reads=[nr_d], writes=[nr_d])
            I("dve", vec.reciprocal, out=nr[:], in_=nr[:], reads=[nr_d], writes=[nr_d])
            I("dve", vec.tensor_tensor, out=kk[:], in0=kk[:], in1=nr[:], op=ALU.mult, reads=[kk_d, nr_d], writes=[kk_d])
            I("dve", vec.tensor_scalar, out=al[:], in0=kk[:], scalar1=-1.0, scalar2=None, op0=ALU.mult,
              reads=[kk_d], writes=[al_d])
            I("dve", vec.tensor_tensor, out=be[:], in0=kk[:], in1=a_[:], op=ALU.mult, reads=[kk_d, a_d], writes=[be_d])
            I("dve", vec.tensor_scalar, out=kf[:], in0=a_[:], scalar1=pr[:, 6:7], scalar2=pr[:, 21:22], op0=ALU.mult, op1=ALU.add,
              reads=[a_d, pr_d], writes=[kf_d])
            I("dve", vec.tensor_tensor, out=kf[:], in0=kf[:], in1=k_[:], op=ALU.mult, reads=[kf_d, k_d], writes=[kf_d])
            y_, y_d = W["y"]
            vecs = [(w_, w_d), (al, al_d), (be, be_d), (kf, kf_d), (r_, r_d)]
            last_ev = None
            for ch in range(G // TC):
                t0 = ch * TC
                for xi, (xv, xv_d) in enumerate(vecs):
                    Rt, Rt_d = Rb[xi][par_i]
                    bt, bt_d = bc[xi][par_i]
                    bp, bp_d = bcp[xi]
                    I("pool", pool.tensor_tensor, out=Rt[:], in0=xv[:, t0:t0 + TC].unsqueeze(2).to_broadcast([128, TC, 64]),
                      in1=I64.unsqueeze(1).to_broadcast([128, TC, 64]), op=ALU.mult,
                      reads=[xv_d, cs_d], writes=[Rt_d])
                    I("pe", pe.matmul, bp[:], lhsT=bones, rhs=Rt[:], start=True, stop=True, reads=[cs_d, Rt_d], writes=[bp_d])
                    I("act", act.copy, out=bt[:], in_=bp[:], reads=[bp_d], writes=[bt_d])
                wb, alb, beb, kb, rb = [bc[xi][par_i] for xi in range(5)]
                for tl in range(TC):
                    t = t0 + tl
                    Sc, Sc_d = St[si]
                    Sn, Sn_d = St[1 - si]
                    sc = nstep % 2
                    I("dve", vec.scalar_tensor_tensor, out=junk[:], in0=Sc[:], scalar=1.0, in1=alb[0][:, tl, :],
                      op0=ALU.mult, op1=ALU.mult, accum_out=sa[:, sc:sc + 1], reads=[Sc_d, alb[1]], writes=[sa_d])
                    I("dve", vec.tensor_tensor, out=Sn[:], in0=Sc[:], in1=wb[0][:, tl, :], op=ALU.mult,
                      reads=[Sc_d, wb[1]], writes=[Sn_d])
                    I("dve", vec.scalar_tensor_tensor, out=Sn[:], in0=beb[0][:, tl, :], scalar=sa[:, sc:sc + 1], in1=Sn[:],
                      op0=ALU.mult, op1=ALU.add, reads=[beb[1], sa_d, Sn_d], writes=[Sn_d])
                    I("dve", vec.scalar_tensor_tensor, out=Sn[:], in0=kb[0][:, tl, :], scalar=v_[:, t:t + 1], in1=Sn[:],
                      op0=ALU.mult, op1=ALU.add, reads=[kb[1], v_d, Sn_d], writes=[Sn_d])
                    last_ev = I("dve", vec.scalar_tensor_tensor, out=junk[:], in0=Sn[:], scalar=1.0, in1=rb[0][:, tl, :],
                                op0=ALU.mult, op1=ALU.mult, accum_out=y_[:, t:t + 1], reads=[Sn_d, rb[1], y_d], writes=[])
                    si = 1 - si
                    nstep += 1
                par_i = 1 - par_i
            y_d.w = last_ev
            y_d.r = []
            yc, yc_d = W["yc"]
            I("pe", pe.matmul, ps_a[:], lhsT=bones, rhs=y_[:], start=True, stop=True, reads=[cs_d, y_d], writes=[ps_a_d])
            I("dve", vec.scalar_tensor_tensor, out=yc[:], in0=ps_a[:], scalar=-1.0 / 64, in1=y_[:], op0=ALU.mult, op1=ALU.add,
              reads=[ps_a_d, y_d], writes=[yc_d])
            I("act", act.activation, out=sq[:], in_=yc[:], func=AF.Square, reads=[yc_d], writes=[sq_d])
            I("pe", pe.matmul, ps_b[:], lhsT=bones, rhs=sq[:], start=True, stop=True, reads=[cs_d, sq_d], writes=[ps_b_d])
            I("dve", vec.tensor_scalar, out=nr[:], in0=ps_b[:], scalar1=1.0 / 64, scalar2=64e-5, op0=ALU.mult, op1=ALU.add,
              reads=[ps_b_d], writes=[nr_d])
            I("act", act.activation, out=nr[:], in_=nr[:], func=AF.Sqrt, reads=[nr_d], writes=[nr_d])
            I("dve", vec.reciprocal, out=nr[:], in_=nr[:], reads=[nr_d], writes=[nr_d])
            I("dve", vec.tensor_tensor, out=yc[:], in0=yc[:], in1=nr[:], op=ALU.mult, reads=[yc_d, nr_d], writes=[yc_d])
            I("dve", vec.tensor_scalar, out=yc[:], in0=yc[:], scalar1=pr[:, 8:9], scalar2=pr[:, 9:10], op0=ALU.mult, op1=ALU.add,
              reads=[yc_d, pr_d], writes=[yc_d])
            I("dve", vec.scalar_tensor_tensor, out=sq[:], in0=r_[:], scalar=pr[:, 7:8], in1=kf[:], op0=ALU.mult, op1=ALU.mult,
              reads=[r_d, pr_d, kf_d], writes=[sq_d])
            I("pe", pe.matmul, ps_a[:], lhsT=bones, rhs=sq[:], start=True, stop=True, reads=[cs_d, sq_d], writes=[ps_a_d])
            I("dve", vec.tensor_tensor, out=sq[:], in0=ps_a[:], in1=v_[:], op=ALU.mult, reads=[ps_a_d, v_d], writes=[sq_d])
            I("dve", vec.tensor_tensor, out=yc[:], in0=yc[:], in1=sq[:], op=ALU.add, reads=[yc_d, sq_d], writes=[yc_d])
            I("dve", vec.tensor_tensor, out=yc[:], in0=yc[:], in1=gg[:], op=ALU.mult, reads=[yc_d, gg_d], writes=[yc_d])
            P.dma("sp", yT[:, gs], yc[:], reads=[yc_d], is_output=True)
        P.finish()
    return nc


def rwkv_inputs(uT, p, S):
    maps = []
    cst = np.zeros((128, 192), np.float32)
    hh = np.arange(128) // 64
    cst[:, 0:128] = (hh[:, None] == hh[None, :]).astype(np.float32)
    cst[np.arange(128), 128 + (np.arange(128) % 64)] = 1.0
    pad = lambda a: np.concatenate([np.zeros(a.shape[:-1] + (1,), np.float32), a], axis=-1)
    mu = p["rw_mu"][0]
    wl = np.ascontiguousarray(pad(uT[3072:3168]))
    al = np.ascontiguousarray(pad(uT[3168:3264]))
    gl = np.ascontiguousarray(pad(uT[3264:3520]).reshape(2, 128, S + 1).transpose(1, 0, 2))
    for c in range(NCORES):
        r = slice(c * 128, (c + 1) * 128)
        rkv = np.ascontiguousarray(pad(np.stack([uT[0:1024][r], uT[1024:2048][r], uT[2048:3072][r]], axis=1)))
        par = np.zeros((128, 24), np.float32)
        par[:, 0] = mu[0:1024][r]
        par[:, 1] = mu[1024:2048][r]
        par[:, 2] = mu[2048:3072][r]
        par[:, 3] = p["rw_w0"][0][r]
        par[:, 4] = p["rw_a0"][0][r]
        par[:, 5] = p["rw_k_k"][0][r]
        par[:, 6] = p["rw_k_a"][0][r]
        par[:, 7] = p["rw_r_k"][0].reshape(-1)[r]
        par[:, 8] = p["rw_lnx_w"][0][r]
        par[:, 9] = p["rw_lnx_b"][0][r]
        par[0:96, 10] = mu[3072:3168]
        par[0:96, 11] = mu[3168:3264]
        par[:, 12] = mu[3264:3392]
        par[:, 13] = mu[3392:3520]
        maps.append({
            "rkv": rkv, "wl": wl, "al": al, "gl": gl, "par": par,
            "w2": np.ascontiguousarray(p["rw_w2"][0][:, r]), "a2": np.ascontiguousarray(p["rw_a2"][0][:, r]),
            "g2": np.ascontiguousarray(p["rw_g2"][0][:, r].reshape(2, 128, 128).transpose(1, 0, 2)),
            "cst": cst,
        })
    return maps


NEG = -30000.0


def build_nsa(S):
    NKT = S // 128
    NOWN = NKT // 8
    NCMP = S // 16
    NNT = NCMP // 128
    NMT = max(1, (S // 64 + 127) // 128)
    NM = NMT * 128
    nc = new_nc()
    dt = lambda n, shp: nc.dram_tensor(n, shp, F32, kind="ExternalInput").ap()
    q_own = dt("q_own", [128, NOWN, 8, 128])
    gates = dt("gates", [128, NOWN, 24])
    cosq = dt("cosq", [128, NOWN, 128])
    sinq = dt("sinq", [128, NOWN, 128])
    kc_i = [dt("kc%d" % g, [128, S]) for g in range(2)]
    vc_i = [dt("vc%d" % g, [128, S]) for g in range(2)]
    ks_i = [dt("ks%d" % g, [128, S]) for g in range(2)]
    vs_i = [dt("vs_tok%d" % g, [S, 128]) for g in range(2)]
    kw_i = [dt("kw_own%d" % g, [128, NOWN, 640]) for g in range(2)]
    vw_i = [dt("vw_own%d" % g, [128, NOWN, 5, 128]) for g in range(2)]
    cosk = dt("cosk", [128, S])
    sink = dt("sink", [128, S])
    cosw = dt("cosw", [128, NOWN, 640])
    sinw = dt("sinw", [128, NOWN, 640])
    ccos = dt("ccos", [128, NCMP])
    csin = dt("csin", [128, NCMP])
    gain_i = dt("gain", [128, 4])
    cmpw = dt("cmpw", [32, 128, 2, 128])
    peT_i = dt("peT", [128, 2, 32])
    cmask = dt("cmask", [128, NOWN, NNT, 128])
    dmask_i = dt("dmask", [128, 8, 128])
    wmask_i = dt("wmask", [128, 2, 5, 128])
    tk_i = dt("tk", [128, NOWN, 3, NM])
    rsel_i = dt("rsel", [128, 8192])
    ovl_i = dt("ovl", [128, NNT, NM])
    pi_i = dt("permid", [128, 256])
    y_o = nc.dram_tensor("y", [128, NOWN, 8, 128], F32, kind="ExternalOutput").ap()

    with ExitStack() as es:
        P = Prog(nc, es)
        I = P.I
        act, vec, pe, pool = nc.scalar, nc.vector, nc.tensor, nc.gpsimd

        def T(name, shape, dtype=F32):
            return P.sbuf(name, shape, dtype), P.dep(name)

        B = [(P.psum("bank%d" % i, [128, 512], F32), P.dep("bank%d" % i)) for i in range(8)]

        gain, gain_d = T("gain_sb", [128, 4])
        pid, pid_d = T("pid_sb", [128, 256])
        dmask, dmask_d = T("dmask_sb", [128, 8, 128])
        wmask, wmask_d = T("wmask_sb", [128, 2, 5, 128])
        peT, peT_d = T("peT_sb", [128, 2, 32])
        peb, peb_d = T("peb", [128, 2, 32], BF16)
        ovf, ovf_d = T("ovf", [128, NNT, NM])
        ovl, ovl_d = T("ovl_sb", [128, NNT, NM], BF16)
        gts, gts_d = T("gts", [128, NOWN, 24])
        rsel, rsel_d = T("rsel_sb", [128, 8192], BF16)
        for (t, d, src) in ((gain, gain_d, gain_i), (pid, pid_d, pi_i), (dmask, dmask_d, dmask_i),
                            (wmask, wmask_d, wmask_i), (peT, peT_d, peT_i), (ovf, ovf_d, ovl_i), (gts, gts_d, gates)):
            P.dma("sp", t[:], src, writes=[d])
        permT = pid[:, 0:128]
        ident = pid[:, 128:256]
        I("dve", vec.tensor_copy, out=peb[:], in_=peT[:], reads=[peT_d], writes=[peb_d])
        I("dve", vec.tensor_copy, out=ovl[:], in_=ovf[:], reads=[ovf_d], writes=[ovl_d])
        I("act", act.activation, out=gts[:], in_=gts[:], func=AF.Sigmoid, reads=[gts_d], writes=[gts_d])
        gq, gq_d = T("gq", [128, 1])
        I("dve", vec.tensor_scalar, out=gq[:], in0=gain[:, 0:1], scalar1=128.0 ** -0.5, scalar2=None, op0=ALU.mult,
          reads=[gain_d], writes=[gq_d])
        onesf, onesf_d = T("onesf", [128, 128])
        onesb, onesb_d = T("onesb", [128, 128], BF16)
        I("dve", vec.memset, onesf[:], 1.0, writes=[onesf_d])
        I("dve", vec.memset, onesb[:], 1.0, writes=[onesb_d])
        stg = [T("stg%d" % i, [128, 512]) for i in range(3)]
        stg_i = [0]

        def nstg():
            s = stg[stg_i[0]]
            stg_i[0] = (stg_i[0] + 1) % 3
            return s

        for j in range(16):
            s_, s_d = nstg()
            P.dma("sp", s_[:], rsel_i[:, 512 * j:512 * j + 512], writes=[s_d])
            I("pool", pool.tensor_copy, out=rsel[:, 512 * j:512 * j + 512], in_=s_[:], reads=[s_d], writes=[rsel_d])

        tA, tA_d = T("tA", [128, 512])
        tB, tB_d = T("tB", [128, 512])
        tC, tC_d = T("tC", [128, 512])
        ctl, ctl_d = T("ctl", [128, 640])
        stl, stl_d = T("stl", [128, 640])

        def norm_rope(dst, dst_d, src, src_d, gcol, gcol_d, cos, sin, cs_deps, N):
            I("act", act.activation, out=tA[:, 0:N], in_=src, func=AF.Square, reads=[src_d], writes=[tA_d])
            I("pe", pe.matmul, B[0][0][:, 0:N], lhsT=onesf[:], rhs=tA[:, 0:N], start=True, stop=True,
              reads=[onesf_d, tA_d], writes=[B[0][1]])
            I("dve", vec.tensor_scalar, out=tB[:, 0:N], in0=B[0][0][:, 0:N], scalar1=1.0 / 128, scalar2=1e-6,
              op0=ALU.mult, op1=ALU.add, reads=[B[0][1]], writes=[tB_d])
            I("act", act.activation, out=tB[:, 0:N], in_=tB[:, 0:N], func=AF.Sqrt, reads=[tB_d], writes=[tB_d])
            I("dve", vec.reciprocal, out=tB[:, 0:N], in_=tB[:, 0:N], reads=[tB_d], writes=[tB_d])
            I("dve", vec.scalar_tensor_tensor, out=tC[:, 0:N], in0=src, scalar=gcol, in1=tB[:, 0:N],
              op0=ALU.mult, op1=ALU.mult, reads=[src_d, gcol_d, tB_d], writes=[tC_d])
            I("pe", pe.matmul, B[1][0][:, 0:N], lhsT=permT, rhs=tC[:, 0:N], start=True, stop=True,
              reads=[pid_d, tC_d], writes=[B[1][1]])
            I("dve", vec.tensor_tensor, out=tA[:, 0:N], in0=tC[:, 0:N], in1=cos, op=ALU.mult,
              reads=[tC_d] + cs_deps, writes=[tA_d])
            I("dve", vec.tensor_tensor, out=tB[:, 0:N], in0=B[1][0][:, 0:N], in1=sin, op=ALU.mult,
              reads=[B[1][1]] + cs_deps, writes=[tB_d])
            I("dve", vec.tensor_tensor, out=dst, in0=tA[:, 0:N], in1=tB[:, 0:N], op=ALU.add,
              reads=[tA_d, tB_d], writes=[dst_d])

        ksT, ksT_d = T("ksT", [128, S], BF16)
        bigB, bigB_d = T("bigB", [128, S], BF16)
        vs1, vs1_d = T("vs1", [128, NKT, 129], BF16)
        kcmpT, kcmpT_d = T("kcmpT", [128, NCMP], BF16)
        kraw, kraw_d = T("kraw", [128, NCMP])
        vcmp1, vcmp1_d = T("vcmp1", [128, NNT, 129], BF16)
        wlf, wlf_d = T("wlf", [128, 2, 128])
        wlb, wlb_d = T("wlb", [128, 2, 128], BF16)
        qf, qf_d = T("qf", [128, 4, 128])
        Qg, Qg_d = T("Qg", [128, 512], BF16)
        cq, cq_d = T("cq", [128, 2, 128])
        kwf, kwf_d = T("kwf", [128, 640])
        kwT, kwT_d = T("kwT", [128, 640], BF16)
        vwf, vwf_d = T("vwf", [128, 5, 128])
        vw1, vw1_d = T("vw1", [128, 5, 129], BF16)
        cmk, cmk_d = T("cmk", [128, NNT, 128])
        tkt, tkt_d = T("tkt", [128, 3, NM])
        ET = [T("ET%d" % i, [128, 512], BF16) for i in range(2)]
        scs, scs_d = T("scs", [128, 512])
        rz, rz_d = T("rz", [128, 512])
        impT, impT_d = T("impT", [128, NMT, 128])
        score, score_d = T("score", [128, NM])
        sc2, sc2_d = T("sc2", [128, NM])
        m8, m8_d = T("m8", [128, 16])
        negq, negq_d = T("negq", [128, NM])
        negT, negT_d = T("negT", [128, NMT, 128], BF16)
        acc, acc_d = T("acc", [128, 4, 128])
        cf, cf_d = T("cf", [128, 8])
        I("dve", vec.memset, vs1[:, :, 128:129], 1.0, writes=[vs1_d])
        I("dve", vec.memset, vw1[:, :, 128:129], 1.0, writes=[vw1_d])
        eti = [0]
        sci = [0]

        def softmax_tile(sc_b, mask_ap, mask_deps):
            et, et_d = ET[eti[0]]
            eti[0] ^= 1
            if mask_ap is not None:
                I("dve", vec.tensor_tensor, out=scs[:].rearrange("p (j q) -> p j q", j=4),
                  in0=sc_b[0][:].rearrange("p (j q) -> p j q", j=4),
                  in1=mask_ap.unsqueeze(1).to_broadcast([128, 4, 128]), op=ALU.add,
                  reads=[sc_b[1]] + mask_deps, writes=[scs_d])
                I("act", act.activation, out=et[:], in_=scs[:], func=AF.Exp, reads=[scs_d], writes=[et_d])
            else:
                I("act", act.activation, out=et[:], in_=sc_b[0][:], func=AF.Exp, reads=[sc_b[1]], writes=[et_d])
            return et, et_d

        zer, zer_d = T("zer", [128, 512], BF16)
        I("dve", vec.memset, zer[:], 0.0, writes=[zer_d])

        def zero_bank(bk):
            I("pe", pe.matmul, bk[0][:], lhsT=onesb[:], rhs=zer[:], start=True, stop=False,
              reads=[onesb_d, zer_d], writes=[bk[1]])

        def pv_acc(et, et_d, vt, vt_d, first, last):
            if first:
                zero_bank(B[2])
                zero_bank(B[3])
            for j in range(4):
                bk = B[2 + j // 2]
                I("pe", pe.matmul, bk[0][:, (j % 2) * 129:(j % 2) * 129 + 129], lhsT=et[:, j * 128:(j + 1) * 128], rhs=vt,
                  start=False, stop=last, reads=[et_d, vt_d], writes=[bk[1]])

        def combine(br, g, ti, first):
            for j in range(4):
                bk = B[2 + j // 2]
                c0 = (j % 2) * 129
                I("dve", vec.tensor_scalar, out=cf[:, j:j + 1], in0=bk[0][:, c0 + 128:c0 + 129], scalar1=1e-30, scalar2=None,
                  op0=ALU.max, reads=[bk[1]], writes=[cf_d])
            I("dve", vec.reciprocal, out=cf[:, 0:4], in_=cf[:, 0:4], reads=[cf_d], writes=[cf_d])
            gc = br * 8 + g * 4
            I("dve", vec.tensor_tensor, out=cf[:, 4:8], in0=cf[:, 0:4], in1=gts[:, ti, gc:gc + 4], op=ALU.mult,
              reads=[cf_d, gts_d], writes=[cf_d])
            for j in range(4):
                bk = B[2 + j // 2]
                c0 = (j % 2) * 129
                if first:
                    I("dve", vec.tensor_scalar, out=acc[:, j, :], in0=bk[0][:, c0:c0 + 128], scalar1=cf[:, 4 + j:5 + j],
                      scalar2=None, op0=ALU.mult, reads=[bk[1], cf_d], writes=[acc_d])
                else:
                    I("dve", vec.scalar_tensor_tensor, out=acc[:, j, :], in0=bk[0][:, c0:c0 + 128], scalar=cf[:, 4 + j:5 + j],
                      in1=acc[:, j, :], op0=ALU.mult, op1=ALU.add, reads=[bk[1], cf_d, acc_d], writes=[acc_d])

        for g in range(2):
            for tl in range(S // 512):
                cs_ = slice(tl * 512, (tl + 1) * 512)
                s_, s_d = nstg()
                P.dma("sp", s_[:], kc_i[g][:, cs_], writes=[s_d])
                I("pool", pool.tensor_copy, out=ksT[:, cs_], in_=s_[:], reads=[s_d], writes=[ksT_d])
                s_, s_d = nstg()
                P.dma("sp", s_[:], vc_i[g][:, cs_], writes=[s_d])
                I("act", act.copy, out=bigB[:, cs_], in_=s_[:], reads=[s_d], writes=[bigB_d])
            nh = (NCMP + 511) // 512
            zero_bank(B[4])
            zero_bank(B[5])
            for l in range(32):
                P.dma("sp", wlf[:], cmpw[l], writes=[wlf_d])
                I("dve", vec.tensor_copy, out=wlb[:], in_=wlf[:], reads=[wlf_d], writes=[wlb_d])
                for hf in range(nh):
                    n0 = 512 * hf
                    cnt = min(512, NCMP - 1 - n0)
                    bk = B[2 + hf]
                    I("pe", pe.matmul, bk[0][:, 0:cnt], lhsT=wlb[:, 0, :], rhs=ksT[:, 16 * n0 + l:16 * n0 + l + 16 * (cnt - 1) + 1:16],
                      start=(l == 0), stop=False, reads=[wlb_d, ksT_d], writes=[bk[1]])
                    I("pe", pe.matmul, bk[0][:, 0:cnt], lhsT=wlb[:, 0, :], rhs=peb[:, 0, l:l + 1].to_broadcast([128, cnt]),
                      start=False, stop=(l == 31), reads=[wlb_d, peb_d], writes=[bk[1]])
                for nt in range(NNT):
                    n0 = 128 * nt
                    cnt = min(128, NCMP - 1 - n0)
                    bk = B[4 + nt // 4]
                    oc = (nt % 4) * 128
                    I("pe", pe.matmul, bk[0][0:cnt, oc:oc + 128], lhsT=bigB[:, 16 * n0 + l:16 * n0 + l + 16 * (cnt - 1) + 1:16],
                      rhs=wlb[:, 1, :], start=False, stop=False, reads=[wlb_d, bigB_d], writes=[bk[1]])
                    I("pe", pe.matmul, bk[0][0:cnt, oc:oc + 128], lhsT=peb[:, 1, l:l + 1].to_broadcast([128, cnt]),
                      rhs=wlb[:, 1, :], start=False, stop=(l == 31), reads=[wlb_d, peb_d], writes=[bk[1]])
            I("dve", vec.memset, kraw[:], 0.0, writes=[kraw_d])
            I("dve", vec.memset, vcmp1[:, :, 0:128], 0.0, writes=[vcmp1_d])
            I("dve", vec.memset, vcmp1[:, :, 128:129], 1.0, writes=[vcmp1_d])
            for hf in range(nh):
                cnt = min(512, NCMP - 1 - 512 * hf)
                I("act", act.copy, out=kraw[:, 512 * hf:512 * hf + cnt], in_=B[2 + hf][0][:, 0:cnt],
                  reads=[B[2 + hf][1]], writes=[kraw_d])
            for nt in range(NNT):
                cnt = min(128, NCMP - 1 - 128 * nt)
                oc = (nt % 4) * 128
                I("act", act.copy, out=vcmp1[0:cnt, nt, 0:128], in_=B[4 + nt // 4][0][0:cnt, oc:oc + 128],
                  reads=[B[4 + nt // 4][1]], writes=[vcmp1_d])
            for hf in range(nh):
                w_ = min(512, NCMP - 512 * hf)
                cs_ = slice(512 * hf, 512 * hf + w_)
                P.dma("sp", ctl[:, 0:w_], ccos[:, cs_], writes=[ctl_d])
                P.dma("sp", stl[:, 0:w_], csin[:, cs_], writes=[stl_d])
                norm_rope(kcmpT[:, cs_], kcmpT_d, kraw[:, cs_], kraw_d, gain[:, 1:2], gain_d,
                          ctl[:, 0:w_], stl[:, 0:w_], [ctl_d, stl_d], w_)
            for tl in range(S // 512):
                cs_ = slice(tl * 512, (tl + 1) * 512)
                s_, s_d = nstg()
                P.dma("sp", s_[:], ks_i[g][:, cs_], writes=[s_d])
                P.dma("sp", ctl[:, 0:512], cosk[:, cs_], writes=[ctl_d])
                P.dma("sp", stl[:, 0:512], sink[:, cs_], writes=[stl_d])
                norm_rope(ksT[:, cs_], ksT_d, s_[:], s_d, gain[:, 2:3], gain_d, ctl[:, 0:512], stl[:, 0:512],
                          [ctl_d, stl_d], 512)
            vsv = vs_i[g].rearrange("(k p) d -> p k d", p=128)
            for k4 in range(NKT // 4):
                s_, s_d = nstg()
                P.dma("sp", s_[:].rearrange("p (k d) -> p k d", k=4), vsv[:, 4 * k4:4 * k4 + 4, :], writes=[s_d])
                I("pool", pool.tensor_copy, out=vs1[:, 4 * k4:4 * k4 + 4, 0:128], in_=s_[:].rearrange("p (k d) -> p k d", k=4),
                  reads=[s_d], writes=[vs1_d])

            for ti in range(NOWN):
                P.dma("sp", qf[:], q_own[:, ti, 4 * g:4 * g + 4, :], writes=[qf_d])
                P.dma("sp", cq[:, 0, :], cosq[:, ti, :], writes=[cq_d])
                P.dma("sp", cq[:, 1, :], sinq[:, ti, :], writes=[cq_d])
                P.dma("sp", kwf[:], kw_i[g][:, ti, :], writes=[kwf_d])
                P.dma("sp", ctl[:], cosw[:, ti, :], writes=[ctl_d])
                P.dma("sp", stl[:], sinw[:, ti, :], writes=[stl_d])
                P.dma("sp", vwf[:], vw_i[g][:, ti, :, :], writes=[vwf_d])
                P.dma("sp", cmk[:], cmask[:, ti, :, :], writes=[cmk_d])
                P.dma("sp", tkt[:], tk_i[:, ti, :, :], writes=[tkt_d])
                _norm_rope_q(P, I, act, vec, pe, B, onesf, onesf_d, permT, pid_d, tA, tA_d, tB, tB_d, tC, tC_d,
                             Qg, Qg_d, qf, qf_d, gq, gq_d, cq, cq_d)
                norm_rope(kwT[:, 0:512], kwT_d, kwf[:, 0:512], kwf_d, gain[:, 3:4], gain_d, ctl[:, 0:512], stl[:, 0:512],
                          [ctl_d, stl_d], 512)
                norm_rope(kwT[:, 512:640], kwT_d, kwf[:, 512:640], kwf_d, gain[:, 3:4], gain_d, ctl[:, 512:640],
                          stl[:, 512:640], [ctl_d, stl_d], 128)
                I("pool", pool.tensor_copy, out=vw1[:, :, 0:128], in_=vwf[:], reads=[vwf_d], writes=[vw1_d])

                nnt = min(NNT, (8 * (8 * ti + 7) + 6) // 128 + 1)
                for nt in range(nnt):
                    sb = B[sci[0]]
                    sci[0] ^= 1
                    I("pe", pe.matmul, sb[0][:], lhsT=kcmpT[:, nt * 128:(nt + 1) * 128], rhs=Qg[:], start=True, stop=True,
                      reads=[kcmpT_d, Qg_d], writes=[sb[1]])
                    et, et_d = softmax_tile(sb, cmk[:, nt, :], [cmk_d])
                    pv_acc(et, et_d, vcmp1[:, nt, :], vcmp1_d, nt == 0, nt == nnt - 1)
                    for mt in range(NMT):
                        I("pe", pe.matmul, B[4 + mt][0][:], lhsT=ovl[:, nt, mt * 128:(mt + 1) * 128], rhs=et[:],
                          start=(nt == 0), stop=(nt == nnt - 1), reads=[ovl_d, et_d], writes=[B[4 + mt][1]])
                    I("pe", pe.matmul, B[6][0][:], lhsT=onesb[:], rhs=et[:], start=(nt == 0), stop=(nt == nnt - 1),
                      reads=[onesb_d, et_d], writes=[B[6][1]])
                combine(0, g, ti, True)
                I("dve", vec.tensor_scalar, out=rz[:], in0=B[6][0][:], scalar1=1e-30, scalar2=None, op0=ALU.max,
                  reads=[B[6][1]], writes=[rz_d])
                I("dve", vec.reciprocal, out=rz[:], in_=rz[:], reads=[rz_d], writes=[rz_d])
                for mt in range(NMT):
                    I("dve", vec.tensor_tensor, out=scs[:], in0=B[4 + mt][0][:], in1=rz[:], op=ALU.mult,
                      reads=[B[4 + mt][1], rz_d], writes=[scs_d])
                    I("dve", vec.tensor_reduce, out=impT[:, mt, :], in_=scs[:].rearrange("p (j q) -> p q j", j=4),
                      axis=AX.X, op=ALU.add, reads=[scs_d], writes=[impT_d])
                for mt in range(NMT):
                    I("pe", pe.transpose, B[7][0][:, mt * 128:(mt + 1) * 128], impT[:, mt, :], ident,
                      reads=[impT_d, pid_d], writes=[B[7][1]])
                MW = NMT * 128
                I("dve", vec.tensor_tensor, out=score[:, 0:MW], in0=B[7][0][:, 0:MW], in1=tkt[:, 0, 0:MW], op=ALU.mult,
                  reads=[B[7][1], tkt_d], writes=[score_d])
                I("dve", vec.tensor_tensor, out=score[:, 0:MW], in0=score[:, 0:MW], in1=tkt[:, 1, 0:MW], op=ALU.add,
                  reads=[score_d, tkt_d], writes=[score_d])
                I("dve", vec.tensor_tensor, out=score[:, 0:MW], in0=score[:, 0:MW], in1=tkt[:, 2, 0:MW], op=ALU.max,
                  reads=[score_d, tkt_d], writes=[score_d])
                I("dve", vec.max, out=m8[:, 0:8], in_=score[:, 0:MW], reads=[score_d], writes=[m8_d])
                I("dve", vec.match_replace, out=sc2[:, 0:MW], in_to_replace=m8[:, 0:8], in_values=score[:, 0:MW],
                  imm_value=-1e9, reads=[m8_d, score_d], writes=[sc2_d])
                I("dve", vec.max, out=m8[:, 8:16], in_=sc2[:, 0:MW], reads=[sc2_d], writes=[m8_d])
                I("dve", vec.tensor_scalar, out=negq[:, 0:MW], in0=score[:, 0:MW], scalar1=m8[:, 15:16], scalar2=NEG,
                  op0=ALU.is_lt, op1=ALU.mult, reads=[score_d, m8_d], writes=[negq_d])
                for mt in range(NMT):
                    I("pe", pe.transpose, B[7][0][:, mt * 128:(mt + 1) * 128], negq[:, mt * 128:(mt + 1) * 128], ident,
                      reads=[negq_d, pid_d], writes=[B[7][1]])
                I("act", act.copy, out=negT[:].rearrange("p m q -> p (m q)"), in_=B[7][0][:, 0:MW],
                  reads=[B[7][1]], writes=[negT_d])

                nk = 8 * ti + 8
                for kt in range(nk):
                    sb = B[sci[0]]
                    sci[0] ^= 1
                    I("pe", pe.matmul, sb[0][:], lhsT=ksT[:, kt * 128:(kt + 1) * 128], rhs=Qg[:], start=True, stop=False,
                      reads=[ksT_d, Qg_d], writes=[sb[1]])
                    ktl = kt % 64
                    I("pe", pe.matmul, sb[0][:].rearrange("p (j q) -> p j q", j=4), lhsT=rsel[:, ktl * 128:(ktl + 1) * 128],
                      rhs=negT[:, kt // 64, :].unsqueeze(1).to_broadcast([128, 4, 128]), start=False, stop=True,
                      reads=[rsel_d, negT_d], writes=[sb[1]])
                    if kt >= 8 * ti:
                        et, et_d = softmax_tile(sb, dmask[:, kt - 8 * ti, :], [dmask_d])
                    else:
                        et, et_d = softmax_tile(sb, None, [])
                    pv_acc(et, et_d, vs1[:, kt, :], vs1_d, kt == 0, kt == nk - 1)
                combine(1, g, ti, False)

                for w in range(5):
                    sb = B[sci[0]]
                    sci[0] ^= 1
                    I("pe", pe.matmul, sb[0][:], lhsT=kwT[:, w * 128:(w + 1) * 128], rhs=Qg[:], start=True, stop=True,
                      reads=[kwT_d, Qg_d], writes=[sb[1]])
                    et, et_d = softmax_tile(sb, wmask[:, 0 if ti == 0 else 1, w, :], [wmask_d])
                    pv_acc(et, et_d, vw1[:, w, :], vw1_d, w == 0, w == 4)
                combine(2, g, ti, False)
                P.dma("sp", y_o[:, ti, 4 * g:4 * g + 4, :], acc[:], reads=[acc_d], is_output=True)
        P.finish()
    return nc


def _norm_rope_q(P, I, act, vec, pe, B, onesf, onesf_d, permT, pid_d, tA, tA_d, tB, tB_d, tC, tC_d,
                 Qg, Qg_d, qf, qf_d, gq, gq_d, cq, cq_d):
    N = 512
    src = qf[:].rearrange("p j q -> p (j q)")
    I("act", act.activation, out=tA[:], in_=src, func=AF.Square, reads=[qf_d], writes=[tA_d])
    I("pe", pe.matmul, B[0][0][:], lhsT=onesf[:], rhs=tA[:], start=True, stop=True, reads=[onesf_d, tA_d], writes=[B[0][1]])
    I("dve", vec.tensor_scalar, out=tB[:], in0=B[0][0][:], scalar1=1.0 / 128, scalar2=1e-6, op0=ALU.mult, op1=ALU.add,
      reads=[B[0][1]], writes=[tB_d])
    I("act", act.activation, out=tB[:], in_=tB[:], func=AF.Sqrt, reads=[tB_d], writes=[tB_d])
    I("dve", vec.reciprocal, out=tB[:], in_=tB[:], reads=[tB_d], writes=[tB_d])
    I("dve", vec.scalar_tensor_tensor, out=tC[:], in0=src, scalar=gq[:, 0:1], in1=tB[:], op0=ALU.mult, op1=ALU.mult,
      reads=[qf_d, gq_d, tB_d], writes=[tC_d])
    I("pe", pe.matmul, B[1][0][:], lhsT=permT, rhs=tC[:], start=True, stop=True, reads=[pid_d, tC_d], writes=[B[1][1]])
    v4 = lambda ap: ap.rearrange("p (j q) -> p j q", j=4)
    I("dve", vec.tensor_tensor, out=v4(tA[:]), in0=v4(tC[:]), in1=cq[:, 0, :].unsqueeze(1).to_broadcast([128, 4, 128]),
      op=ALU.mult, reads=[tC_d, cq_d], writes=[tA_d])
    I("dve", vec.tensor_tensor, out=v4(tB[:]), in0=v4(B[1][0][:]), in1=cq[:, 1, :].unsqueeze(1).to_broadcast([128, 4, 128]),
      op=ALU.mult, reads=[B[1][1], cq_d], writes=[tB_d])
    I("dve", vec.tensor_tensor, out=Qg[:], in0=tA[:], in1=tB[:], op=ALU.add, reads=[tA_d, tB_d], writes=[Qg_d])


def _rope_np(pos):
    inv = (np.float32(10000.0) ** (-(np.arange(0, 128, 2, dtype=np.float32) / np.float32(128)))).astype(np.float32)
    ang = (pos.astype(np.float32)[:, None] * inv[None, :]).astype(np.float32)
    c, s = np.cos(ang).astype(np.float32), np.sin(ang).astype(np.float32)
    return (np.ascontiguousarray(np.concatenate([c, c], axis=1).T), np.ascontiguousarray(np.concatenate([s, s], axis=1).T))


def nsa_inputs(uT, p, S):
    NKT = S // 128
    NOWN = NKT // 8
    NCMP = S // 16
    NNT = NCMP // 128
    NMT = max(1, (S // 64 + 127) // 128)
    NM = NMT * 128
    n_slc = S // 64
    cosk, sink = _rope_np(np.arange(S))
    ccos, csin = _rope_np(np.arange(NCMP) * 16 + 31)
    permid = np.zeros((128, 256), np.float32)
    for m in range(64):
        permid[m + 64, m] = -1.0
        permid[m, m + 64] = 1.0
    permid[np.arange(128), 128 + np.arange(128)] = 1.0
    rsel = (np.arange(128)[:, None] == (np.arange(8192)[None, :] // 64)).astype(np.float32)
    n_all = np.arange(NCMP)
    c0 = n_all[:, None] * 16
    s0 = np.arange(NM)[None, :] * 64
    ov = np.clip(np.minimum(c0 + 32, s0 + 64) - np.maximum(c0, s0), 0, None).astype(np.float32) / 32.0
    ov[NCMP - 1:, :] = 0.0
    ov[:, n_slc:] = 0.0
    ovl = np.ascontiguousarray(ov.reshape(NNT, 128, NM).transpose(1, 0, 2))
    cmpw = np.ascontiguousarray(np.asarray(p["nsa_cmp_w"][0]).transpose(1, 2, 0, 3))
    peT = np.ascontiguousarray(np.asarray(p["nsa_cmp_pe"][0]).transpose(2, 0, 1))
    gain = np.ascontiguousarray(np.asarray(p["nsa_qk_gain"][0]).T)
    pp = np.arange(128)
    shared = {"cosk": cosk, "sink": sink, "ccos": ccos, "csin": csin, "gain": gain, "cmpw": cmpw, "peT": peT,
              "rsel": rsel, "ovl": ovl, "permid": permid}
    for g in range(2):
        shared["kc%d" % g] = np.ascontiguousarray(uT[1024 + g * 128:1024 + (g + 1) * 128])
        shared["vc%d" % g] = np.ascontiguousarray(uT[1280 + g * 128:1280 + (g + 1) * 128])
        shared["ks%d" % g] = np.ascontiguousarray(uT[1536 + g * 128:1536 + (g + 1) * 128])
        shared["vs_tok%d" % g] = np.ascontiguousarray(uT[1792 + g * 128:1792 + (g + 1) * 128].T)
    maps = []
    for c in range(NCORES):
        qts = [c + 8 * ti for ti in range(NOWN)]
        m = dict(shared)
        tok = np.concatenate([np.arange(128 * qt, 128 * qt + 128) for qt in qts])
        m["q_own"] = np.ascontiguousarray(uT[0:1024][:, tok].reshape(8, 128, NOWN, 128).transpose(1, 2, 0, 3))
        m["gates"] = np.ascontiguousarray(uT[2560:2584][:, tok].reshape(24, NOWN, 128).transpose(2, 1, 0))
        m["cosq"] = np.ascontiguousarray(cosk[:, tok].reshape(128, NOWN, 128))
        m["sinq"] = np.ascontiguousarray(sink[:, tok].reshape(128, NOWN, 128))
        wtok = np.concatenate([np.arange(128 * (qt - 4), 128 * (qt + 1)) for qt in qts])
        wvalid = wtok >= 0
        wtc = np.clip(wtok, 0, None)
        for g in range(2):
            kw = uT[2048 + g * 128:2048 + (g + 1) * 128][:, wtc] * 1.0
            kw[:, ~wvalid] = 0.0
            m["kw_own%d" % g] = np.ascontiguousarray(kw.reshape(128, NOWN, 640))
            vw = uT[2304 + g * 128:2304 + (g + 1) * 128][:, wtc] * 1.0
            vw[:, ~wvalid] = 0.0
            m["vw_own%d" % g] = np.ascontiguousarray(vw.reshape(128, NOWN, 5, 128).transpose(3, 1, 2, 0))
        m["cosw"] = np.ascontiguousarray(cosk[:, wtc].reshape(128, NOWN, 640))
        m["sinw"] = np.ascontiguousarray(sink[:, wtc].reshape(128, NOWN, 640))
        cm = np.zeros((128, NOWN, NNT, 128), np.float32)
        tkk = np.zeros((128, NOWN, 3, NM), np.float32)
        for ti, qt in enumerate(qts):
            t = 128 * qt + pp
            for nt in range(NNT):
                n = 128 * nt + pp
                vis = (16 * n[:, None] + 31) <= t[None, :]
                cm[:, ti, nt, :] = np.where(vis, 0.0, NEG)
            cur = t // 64
            mm = np.arange(NM)
            val = (mm[None, :] <= cur[:, None]) & (mm[None, :] < n_slc)
            forced = (mm[None, :] == 0) | (mm[None, :] == cur[:, None]) | (mm[None, :] == cur[:, None] - 1)
            tkk[:, ti, 0, :] = val
            tkk[:, ti, 1, :] = val.astype(np.float32) - 1.0
            tkk[:, ti, 2, :] = np.where(forced & val, 1e4, -2.0)
        m["cmask"] = cm
        m["tk"] = tkk
        dm = np.zeros((128, 8, 128), np.float32)
        for j in range(8):
            if j == c:
                dm[:, j, :] = np.where(pp[:, None] <= pp[None, :], 0.0, NEG)
            elif j > c:
                dm[:, j, :] = NEG
        m["dmask"] = dm
        wm = np.zeros((128, 2, 5, 128), np.float32)
        for var in range(2):
            for w in range(5):
                d = 128 * (4 - w) + pp[None, :] - pp[:, None]
                ok = (d >= 0) & (d < 512)
                if var == 0 and (c - 4 + w) < 0:
                    ok = np.zeros_like(ok)
                wm[:, var, w, :] = np.where(ok, 0.0, NEG)
        m["wmask"] = wm
        maps.append(m)
    return maps


def nsa_gather(results, S):
    NOWN = S // 128 // 8
    yT = np.zeros((1024, S), np.float32)
    for c in range(NCORES):
        y = results[c]["y"]
        for ti in range(NOWN):
            qt = c + 8 * ti
            yT[:, 128 * qt:128 * qt + 128] = y[:, ti].transpose(1, 2, 0).reshape(1024, 128)
    return yT


SEQ = 16384


def _run(nc, maps):
    res = run_bass_kernel_spmd(nc, maps, core_ids=list(range(NCORES)))
    return res.results


def kernel(**inp):
    p = {k: np.asarray(v, np.float32) for k, v in inp.items()}
    S = SEQ
    x = p["x"][0]
    xT = np.ascontiguousarray(x.T)
    tok = [slice(c * TOK, (c + 1) * TOK) for c in range(NCORES)]

    wt_a = wtile(p["w_in_a"][0], 48)
    g0 = gain_layout(p["norm_mix"][0])
    res = _run(build_stage_a(48), [{"xT": np.ascontiguousarray(xT[:, tok[c]]), "gain": g0, "wt": wt_a}
                                   for c in range(NCORES)])
    uT = np.concatenate([r["uT"] for r in res], axis=1)
    del res
    res = _run(build_nsa(S), nsa_inputs(uT[0:2584], p, S))
    y_nsa = nsa_gather(res, S)
    del res
    res = _run(build_rwkv(S), rwkv_inputs(uT[2584:6104], p, S))
    y_rw = np.concatenate([r["yT"] for r in res], axis=0)
    del res, uT
    yT = np.concatenate([y_nsa, y_rw], axis=0)
    gains = np.concatenate([gain_layout(p["norm_mlp"][0]), gain_layout(p["norm_mix"][1])], axis=1)
    wo, w1, w2, wn = wtile(p["w_out_a"][0]), wtile(p["w_ff1"][0]), wtile(p["w_ff2"][0]), wtile(p["w_in_b"][0], 48)
    res = _run(build_stage_c(48), [{"xT": np.ascontiguousarray(xT[:, tok[c]]), "yT": np.ascontiguousarray(yT[:, tok[c]]),
                                    "gain": gains, "wo": wo, "w1": w1, "w2": w2, "wn": wn} for c in range(NCORES)])
    x1T = np.concatenate([r["x2T"] for r in res], axis=1)
    uT = np.concatenate([r["uT"] for r in res], axis=1)
    del res, wo, w1, w2, wn, yT, xT
    res = _run(build_mix_b(S), mixb_inputs(uT, p, S))
    yT = np.zeros((2048, S), np.float32)
    for c in range(NCORES):
        yT[c * 128:(c + 1) * 128] = res[c]["yT"][0:128]
        yT[1024 + c * 128:1024 + (c + 1) * 128] = res[c]["yT"][128:256]
    del res, uT
    gains = np.concatenate([gain_layout(p["norm_mlp"][1]), gain_layout(p["norm_mlp"][1])], axis=1)
    wo, w1, w2 = wtile(p["w_out_b"][0]), wtile(p["w_ff1"][1]), wtile(p["w_ff2"][1])
    res = _run(build_stage_c(0), [{"xT": np.ascontiguousarray(x1T[:, tok[c]]), "yT": np.ascontiguousarray(yT[:, tok[c]]),
                                   "gain": gains, "wo": wo, "w1": w1, "w2": w2} for c in range(NCORES)])
    outT = np.concatenate([r["x2T"] for r in res], axis=1)
    return np.ascontiguousarray(outT.T)[None].astype(np.float32)
```

```python
from contextlib import ExitStack
import numpy as np
import concourse.bass as bass
import concourse.mybir as mybir
from concourse.bass_utils import run_bass_kernel_spmd

F32 = mybir.dt.float32
BF16 = mybir.dt.bfloat16
AF = mybir.ActivationFunctionType
ALU = mybir.AluOpType
AX = mybir.AxisListType

NCORES = 8
ROT = 4000


class Dep:
    __slots__ = ("w", "r", "name", "multi", "ws")

    def __init__(self, name=None, multi=False):
        self.w = None
        self.r = []
        self.name = name
        self.multi = multi
        self.ws = []


class Prog:
    def __init__(self, nc, es, n_dma_sems=6):
        self.nc = nc
        self.es = es
        self.eng = {"pe": nc.tensor, "act": nc.scalar, "dve": nc.vector,
                    "pool": nc.gpsimd, "sp": nc.sync}
        self.sem = {}
        self.cnt = {}
        self.waited = {e: {} for e in self.eng}
        for e in self.eng:
            self.sem[e] = es.enter_context(nc.semaphore("s_" + e))
            self.cnt[e] = 0
        self.pe_sems = {id(self.sem["pe"])}
        self.dma_sems = {}
        for q in ("sp", "pool", "act"):
            self.dma_sems[q] = [[es.enter_context(nc.semaphore("d_%s%d" % (q, i))), 0]
                                for i in range(n_dma_sems)]
        self.dma_rr = {q: 0 for q in self.dma_sems}
        self.sem_ids = {}
        self.out_events = []
        self.ninstr = 0

    def sbuf(self, name, shape, dtype):
        es = self.tes if getattr(self, "tes", None) is not None else self.es
        self.uid = getattr(self, "uid", 0) + 1
        return es.enter_context(self.nc.sbuf_tensor("%s_%d" % (name, self.uid), list(shape), dtype))

    def psum(self, name, shape, dtype=F32):
        es = self.tes if getattr(self, "tes", None) is not None else self.es
        self.uid = getattr(self, "uid", 0) + 1
        return es.enter_context(self.nc.psum_tensor("%s_%d" % (name, self.uid), list(shape), dtype))

    def dram(self, name, shape, dtype=F32):
        return self.nc.dram_tensor(name, list(shape), dtype).ap()

    def gd(self, ap):
        if not hasattr(self, "gdeps"):
            self.gdeps = {}
        k = ap.name if isinstance(ap.name, str) else ap.name()
        if k not in self.gdeps:
            self.gdeps[k] = Dep(k, multi=True)
        return self.gdeps[k]

    y_reads = ()

    def open_scope(self):
        self.tes = ExitStack()
        self.tes.__enter__()

    def close_scope(self):
        self.barrier()
        self.tes.__exit__(None, None, None)
        self.tes = None

    def barrier(self):
        evs = []
        for q, slots in self.dma_sems.items():
            for s_, v in slots:
                if v > 0:
                    evs.append((s_, v))
        for e in ("pe", "act", "dve", "pool"):
            if self.cnt[e] > 0:
                evs.append((self.sem[e], self.cnt[e]))
        if getattr(self, "cc_cnt", 0) > 0:
            evs.append((self.cc_sem, self.cc_cnt))
        for e in self.eng:
            for ev in evs:
                self._wait(e, ev)

    def collective(self, kind, src, dst, reads=(), writes=()):
        if getattr(self, "cc_sem", None) is None:
            self.cc_sem = self.es.enter_context(self.nc.semaphore("cc_sem"))
            self.cc_cnt = 0
        self._needs("pool", reads, writes)
        ins = self.nc.gpsimd.collective_compute(kind, ALU.bypass, replica_groups=[list(range(NCORES))],
                                                ins=[src.opt()], outs=[dst.opt()])
        self.cc_cnt += 1
        ins.then_inc(self.cc_sem)
        ev = (self.cc_sem, self.cc_cnt)
        self._record(ev, reads, writes)
        return ev

    def dep(self, name=None):
        return Dep(name)

    def _wait(self, e, ev):
        sem, val = ev
        k = id(sem)
        if self.waited[e].get(k, 0) >= val:
            return
        self.waited[e][k] = val
        self.eng[e].wait_ge(sem, val)

    def _needs(self, e, reads, writes):
        evs = []
        for d in reads:
            if d.multi:
                evs.extend(d.ws)
            elif d.w is not None:
                evs.append(d.w)
        for d in writes:
            if (not d.multi) and d.w is not None:
                evs.append(d.w)
            evs.extend(d.r)
        for ev in evs:
            if e == "pe" and id(ev[0]) in self.pe_sems:
                continue
            self._wait(e, ev)

    def _record(self, ev, reads, writes):
        for d in reads:
            d.r.append(ev)
            if len(d.r) > 64:
                d.r = d.r[-64:] if False else d.r
        for d in writes:
            if d.multi:
                d.ws.append(ev)
            else:
                d.w = ev
                d.r = []

    def I(self, e, fn, *args, reads=(), writes=(), **kw):
        self._needs(e, reads, writes)
        if self.cnt[e] >= ROT:
            self.sem[e] = self.es.enter_context(self.nc.semaphore("s_%s_%d" % (e, self.ninstr)))
            self.cnt[e] = 0
            self.old_final = getattr(self, "old_final", [])
            if e == "pe":
                self.pe_sems.add(id(self.sem[e]))
        ins = fn(*args, **kw)
        self.cnt[e] += 1
        ins.then_inc(self.sem[e], 1)
        ev = (self.sem[e], self.cnt[e])
        self._record(ev, reads, writes)
        self.ninstr += 1
        return ev

    def dma(self, q, out, in_, reads=(), writes=(), is_output=False, **kw):
        slots = self.dma_sems[q]
        i = self.dma_rr[q]
        self.dma_rr[q] = (i + 1) % len(slots)
        slot = slots[i]
        if slot[1] > 0:
            self._wait(q, (slot[0], slot[1]))
        self._needs(q, reads, writes)
        ins = self.eng[q].dma_start(out=out, in_=in_, **kw)
        slot[1] += 16
        ins.then_inc(slot[0], 16)
        ev = (slot[0], slot[1])
        self._record(ev, reads, writes)
        if is_output:
            self.out_events.append(ev)
        self.ninstr += 1
        return ev

    def finish(self):
        for q, slots in self.dma_sems.items():
            for s, v in slots:
                if v > 0:
                    self._wait("sp", (s, v))
        for e in ("pe", "act", "dve", "pool"):
            if self.cnt[e] > 0:
                self._wait("sp", (self.sem[e], self.cnt[e]))
        if getattr(self, "cc_cnt", 0) > 0:
            self._wait("sp", (self.cc_sem, self.cc_cnt))


def new_nc():
    return bass.Bass("TRN2", target_bir_lowering=False)


NT = 512
TOK = 2048
D = 2048
KC = 16


class DenseRes:
    def __init__(self, P, nw=3, nps=4, sq=None):
        self.P = P
        self.nw = nw
        self.wf = [P.sbuf("wf%d" % i, [128, 16, 128], F32) for i in range(nw)]
        self.wb = [P.sbuf("wb%d" % i, [128, 16, 128], BF16) for i in range(nw)]
        self.wf_d = [P.dep() for _ in range(nw)]
        self.wb_d = [P.dep() for _ in range(nw)]
        self.nps = nps
        self.ps = [P.psum("dps%d" % i, [128, NT], F32) for i in range(nps)]
        self.ps_d = [P.dep() for _ in range(nps)]
        self.wi = 0
        self.pi = 0
        self.ones = P.sbuf("ones_bf", [128, 128], BF16)
        self.ones_d = P.dep()
        P.I("dve", P.nc.vector.memset, self.ones[:], 1.0, writes=[self.ones_d])
        self.ss_ps = P.psum("ss_ps", [128, NT], F32)
        self.ss_d = P.dep()
        if sq is None:
            self.sq = P.sbuf("sq", [128, KC, NT], BF16)
            self.sq_d = P.dep()
        else:
            self.sq, self.sq_d = sq
        self.rs = P.sbuf("rs", [128, NT], F32)
        self.rs_d = P.dep()
        self.castsel = 0


def rmsnorm_T(P, R, xs, xs_d, gain_sb, gain_d, gcol0, hT, hT_d, eps=1e-6):
    nc = P.nc
    for c in range(KC):
        P.I("act", nc.scalar.activation, out=R.sq[:, c, :], in_=xs[:, c, :], func=AF.Square,
            reads=[xs_d], writes=[R.sq_d])
    for c in range(KC):
        P.I("pe", nc.tensor.matmul, R.ss_ps[:], lhsT=R.ones[:], rhs=R.sq[:, c, :],
            start=(c == 0), stop=(c == KC - 1), reads=[R.ones_d, R.sq_d], writes=[R.ss_d])
    P.I("dve", nc.vector.tensor_scalar, out=R.rs[:], in0=R.ss_ps[:], scalar1=1.0 / D, scalar2=eps,
        op0=ALU.mult, op1=ALU.add, reads=[R.ss_d], writes=[R.rs_d])
    P.I("act", nc.scalar.activation, out=R.rs[:], in_=R.rs[:], func=AF.Sqrt,
        reads=[R.rs_d], writes=[R.rs_d])
    P.I("dve", nc.vector.reciprocal, out=R.rs[:], in_=R.rs[:], reads=[R.rs_d], writes=[R.rs_d])
    for c in range(KC):
        P.I("dve", nc.vector.scalar_tensor_tensor, out=hT[:, c, :], in0=xs[:, c, :],
            scalar=gain_sb[:, gcol0 + c:gcol0 + c + 1], in1=R.rs[:], op0=ALU.mult, op1=ALU.mult,
            reads=[xs_d, gain_d, R.rs_d], writes=[hT_d])


def dense(P, R, wt, n_kg, n_oc, acts, epilogue, q="sp", extra=None, skip_main=()):
    nc = P.nc
    tiles = [(oc, kg) for oc in range(n_oc) for kg in range(n_kg)]
    PF = R.nw - 1
    slots = {}

    def issue_load(i):
        oc, kg = tiles[i]
        s = R.wi
        R.wi = (R.wi + 1) % R.nw
        slots[i] = s
        P.dma(q, R.wf[s][:].rearrange("p c j -> p (c j)"), wt[kg, oc], writes=[R.wf_d[s]])

    for i in range(min(PF, len(tiles))):
        issue_load(i)
    ps = None
    for i, (oc, kg) in enumerate(tiles):
        if i + PF < len(tiles):
            issue_load(i + PF)
        s = slots.pop(i)
        R.castsel ^= 1
        if R.castsel:
            P.I("pool", nc.gpsimd.tensor_copy, out=R.wb[s][:], in_=R.wf[s][:],
                reads=[R.wf_d[s]], writes=[R.wb_d[s]])
        else:
            P.I("act", nc.scalar.copy, out=R.wb[s][:], in_=R.wf[s][:],
                reads=[R.wf_d[s]], writes=[R.wb_d[s]])
        if oc not in skip_main:
            if kg == 0:
                pi = R.pi
                R.pi = (R.pi + 1) % R.nps
                ps, ps_d = R.ps[pi], R.ps_d[pi]
            a, a_d = acts[kg]
            for c in range(KC):
                P.I("pe", nc.tensor.matmul, ps[:], lhsT=R.wb[s][:, c, :], rhs=a[:, c, :],
                    start=(kg == 0 and c == 0), stop=(kg == n_kg - 1 and c == KC - 1),
                    reads=[R.wb_d[s], a_d], writes=[ps_d])
            if kg == n_kg - 1:
                epilogue(oc, ps, ps_d)
        if extra is not None and kg == n_kg - 1:
            extra(oc, R.wb[s], R.wb_d[s])


def emit_dense_stage(P, TOKN, xT, y_src, gain_ap, W, n_oc_next, x2T, uT, utok, tok_cols, fm_skip):
    nc = P.nc
    P.open_scope()
    gain_sb = P.sbuf("gain_sb", [128, 2 * KC], F32)
    gain_d = P.dep()
    P.dma("sp", gain_sb[:], gain_ap, writes=[gain_d])
    xs = P.sbuf("xs", [128, KC, NT], F32)
    xs_d = P.dep()
    aT = P.sbuf("aT", [128, KC, NT], BF16)
    aT_d = P.dep()
    zT = P.sbuf("zT", [128, 64, NT], BF16)
    zT_d = P.dep()
    yst = [P.sbuf("yst%d" % i, [128, 4, NT], F32) for i in range(2)]
    yst_d = [P.dep() for _ in range(2)]
    NO = 3
    uo = [P.sbuf("uo%d" % i, [128, NT], F32) for i in range(NO)]
    uo_d = [P.dep() for _ in range(NO)]
    oi = [0]
    R = DenseRes(P, sq=(zT, zT_d))
    xTv = xT.rearrange("(c p) t -> p c t", p=128)
    x2Tv = x2T.rearrange("(c p) t -> p c t", p=128) if x2T is not None else None

    def nxt():
        o = oi[0]
        oi[0] = (o + 1) % NO
        return o

    for tt in range(TOKN // NT):
        tsl = slice(tt * NT, (tt + 1) * NT)
        P.dma("sp", xs[:], xTv[:, :, tsl], writes=[xs_d])

        def epi_res(oc, ps, ps_d):
            P.I("dve", nc.vector.tensor_tensor, out=xs[:, oc, :], in0=ps[:], in1=xs[:, oc, :],
                op=ALU.add, reads=[ps_d], writes=[xs_d])

        def epi_relu2(oc, ps, ps_d):
            o = nxt()
            P.I("act", nc.scalar.activation, out=uo[o][:], in_=ps[:], func=AF.Square,
                reads=[ps_d], writes=[uo_d[o]])
            P.I("dve", nc.vector.scalar_tensor_tensor, out=zT[:, oc, :], in0=ps[:], scalar=0.0,
                in1=uo[o][:], op0=ALU.is_gt, op1=ALU.mult, reads=[ps_d, uo_d[o]], writes=[zT_d])

        def epi_out(oc, ps, ps_d):
            o = nxt()
            P.I("act", nc.scalar.copy, out=uo[o][:], in_=ps[:], reads=[ps_d], writes=[uo_d[o]])
            if isinstance(uT, SplitU):
                dst_ = uT.local(oc * 128, (oc + 1) * 128)
                P.dma("sp", dst_[:, tsl], uo[o][:], reads=[uo_d[o]], writes=[P.gd(dst_)], is_output=True)
            else:
                P.dma("sp", uT[oc * 128:(oc + 1) * 128, tsl], uo[o][:], reads=[uo_d[o]], writes=[P.gd(uT)], is_output=True)

        def extra_tok(oc, wb, wb_d):
            if oc not in tok_cols:
                return
            pi = R.pi
            R.pi = (R.pi + 1) % R.nps
            ps, ps_d = R.ps[pi], R.ps_d[pi]
            for a_ in range(NT // 128):
                for c in range(KC):
                    P.I("pe", nc.tensor.matmul, ps[:, a_ * 128:(a_ + 1) * 128], lhsT=aT[:, c, a_ * 128:(a_ + 1) * 128],
                        rhs=wb[:, c, :], start=(c == 0), stop=(c == KC - 1), reads=[wb_d, aT_d], writes=[ps_d])
            o = nxt()
            P.I("act", nc.scalar.copy, out=uo[o][:], in_=ps[:], reads=[ps_d], writes=[uo_d[o]])
            off = tok_cols[oc]
            P.dma("sp", utok[tsl, off:off + 128].rearrange("(a p) c -> p a c", p=128),
                  uo[o][:].rearrange("p (a c) -> p a c", a=NT // 128), reads=[uo_d[o]], writes=[P.gd(utok)], is_output=True)

        if y_src is not None:
            for j in range(4):
                b_ = j % 2
                P.dma("sp", yst[b_][:], y_src(tt, j), reads=P.y_reads, writes=[yst_d[b_]])
                P.I("dve", nc.vector.tensor_copy, out=aT[:, 4 * j:4 * j + 4, :], in_=yst[b_][:],
                    reads=[yst_d[b_]], writes=[aT_d])
            dense(P, R, W["wo"], 1, 16, [(aT, aT_d)], epi_res)
            rmsnorm_T(P, R, xs, xs_d, gain_sb, gain_d, 0, aT, aT_d)
            dense(P, R, W["w1"], 1, 64, [(aT, aT_d)], epi_relu2)
            dense(P, R, W["w2"], 4, 16, [(zT[:, 16 * k:16 * k + 16, :], zT_d) for k in range(4)], epi_res)
            P.dma("sp", x2Tv[:, :, tsl], xs[:], reads=[xs_d], writes=[P.gd(x2T)], is_output=True)
        if n_oc_next:
            rmsnorm_T(P, R, xs, xs_d, gain_sb, gain_d, KC, aT, aT_d)
            dense(P, R, W["wn"], 1, n_oc_next, [(aT, aT_d)], epi_out, extra=extra_tok, skip_main=fm_skip)
    P.close_scope()


def gain_layout(g):
    return np.ascontiguousarray(np.asarray(g, np.float32).reshape(-1, 128).T)


def wtile(W, n_oc=None):
    W = np.asarray(W, np.float32)
    K_, N = W.shape
    if n_oc is None:
        n_oc = (N + 127) // 128
    if n_oc * 128 != N:
        W = np.concatenate([W, np.zeros((K_, n_oc * 128 - N), np.float32)], axis=1)
    n_kg = K_ // 2048
    return np.ascontiguousarray(W.reshape(n_kg, 16, 128, n_oc, 128).transpose(0, 3, 2, 1, 4).reshape(n_kg, n_oc, 128, 2048))


G = 512


class ExtIO:
    def __init__(self, nc, specs, outs):
        self.t = {n: nc.dram_tensor(n, shp, F32, kind="ExternalInput").ap() for n, shp in specs.items()}
        self.o = {n: nc.dram_tensor(n, shp, F32, kind="ExternalOutput").ap() for n, shp in outs.items()}
        self.rd = []
        self.wr = []

    def par(self, n):
        return self.t[n]

    def fm(self, n, g):
        return self.t[n][:, g * G:(g + 1) * G].rearrange("p (a b) -> p a b", a=4)

    def fm3(self, n, g):
        return self.t[n][:, :, g * G:(g + 1) * G]

    def tm(self, n, g):
        return self.t[n].rearrange("(k p) i -> p k i", p=128)[:, 4 * g:4 * g + 4, :]

    def tm32(self, n, g):
        v = self.t[n].rearrange("(k q p) i -> p k q i", q=4, p=32)
        return [v[:, 4 * g + a, :, :] for a in range(4)]

    def out(self, n, r0, r1, g):
        return self.o[n][r0:r1, g * G:(g + 1) * G].rearrange("p (a b) -> p a b", a=4)


def v4(ap):
    return ap.rearrange("p (a b) -> p a b", a=4)


def build_mix_b(S):
    nc = new_nc()
    specs = {"l_gate": [128, S], "l_xb": [128, S], "l_par": [128, 16], "l_wa": [128, 128], "l_wx": [128, 128],
             "h_q": [128, S], "h_f": [128, S], "h_g": [128, S], "h_ftok": [S, 128], "h_vtok": [S, 128],
             "h_par": [128, 8], "h_lbrow": [128, 256], "mb_cst": [128, 1024]}
    io = ExtIO(nc, specs, {"yT": [256, S]})
    with ExitStack() as es:
        P = Prog(nc, es)
        emit_mix_b(P, S, io)
        P.finish()
    return nc


def emit_mix_b(P, S, io):
    nc = P.nc
    NG = S // G
    l_par, l_wa, l_wx, h_par, h_lbrow, cst = (io.par(n) for n in ("l_par", "l_wa", "l_wx", "h_par", "h_lbrow", "mb_cst"))
    if True:
        P.open_scope()
        I = P.I
        act, vec, pe = nc.scalar, nc.vector, nc.tensor

        def T(name, shape, dtype=F32):
            return P.sbuf(name, shape, dtype), P.dep(name)

        def PS(name, shape):
            return P.psum(name, shape, F32), P.dep(name)

        lpar, lpar_d = T("lpar", [128, 16])
        lwa, lwa_d = T("lwa", [128, 128])
        lwx, lwx_d = T("lwx", [128, 128])
        hpar, hpar_d = T("hpar", [128, 8])
        lbrow, lbrow_d = T("lbrow", [128, 256])
        cs, cs_d = T("cs", [128, 1024])
        for (t, d, src) in ((lpar, lpar_d, l_par), (lwa, lwa_d, l_wa), (lwx, lwx_d, l_wx),
                            (hpar, hpar_d, h_par), (lbrow, lbrow_d, h_lbrow), (cs, cs_d, cst)):
            P.dma("sp", t[:], src, writes=[d])
        Mgt32, Mle, cmask = cs[0:32, 0:32], cs[:, 128:256], cs[:, 256:768]
        ones_bf, ones_d = T("ones_bf", [128, 128], BF16)
        I("dve", vec.memset, ones_bf[:], 1.0, writes=[ones_d])
        lder, lder_d = T("lder", [128, 4])
        I("act", act.activation, out=lder[:, 0:1], in_=lpar[:, 7:8], func=AF.Exp, scale=-1.0,
          reads=[lpar_d], writes=[lder_d])
        I("act", act.activation, out=lder[:, 1:2], in_=lder[:, 0:1], func=AF.Ln, bias=1.0,
          reads=[lder_d], writes=[lder_d])
        I("dve", vec.tensor_scalar, out=lder[:, 2:3], in0=lder[:, 1:2], scalar1=-8.0, scalar2=None,
          op0=ALU.mult, reads=[lder_d], writes=[lder_d])
        I("dve", vec.tensor_scalar, out=lder[:, 3:4], in0=lder[:, 1:2], scalar1=-16.0, scalar2=None,
          op0=ALU.mult, reads=[lder_d], writes=[lder_d])
        hder, hder_d = T("hder", [128, 4])
        I("dve", vec.tensor_tensor, out=hder[:, 0:1], in0=hpar[:, 1:2], in1=hpar[:, 0:1], op=ALU.subtract,
          reads=[hpar_d], writes=[hder_d])
        I("act", act.activation, out=hder[:, 0:1], in_=hder[:, 0:1], func=AF.Sigmoid,
          reads=[hder_d], writes=[hder_d])
        I("dve", vec.tensor_scalar, out=hder[:, 1:2], in0=hder[:, 0:1], scalar1=-1.0, scalar2=1.0,
          op0=ALU.mult, op1=ALU.add, reads=[hder_d], writes=[hder_d])
        I("dve", vec.tensor_scalar, out=hder[:, 2:3], in0=hder[:, 0:1], scalar1=-1.0, scalar2=None,
          op0=ALU.add, reads=[hder_d], writes=[hder_d])
        omlrow, omlrow_d = T("omlrow", [128, 128])
        I("dve", vec.tensor_tensor, out=omlrow[:], in0=lbrow[:, 128:256], in1=lbrow[:, 0:128], op=ALU.subtract,
          reads=[lbrow_d], writes=[omlrow_d])
        I("act", act.activation, out=omlrow[:], in_=omlrow[:], func=AF.Sigmoid, reads=[omlrow_d], writes=[omlrow_d])
        I("dve", vec.tensor_scalar, out=omlrow[:], in0=omlrow[:], scalar1=-1.0, scalar2=1.0,
          op0=ALU.mult, op1=ALU.add, reads=[omlrow_d], writes=[omlrow_d])

        names = ["xb", "gt", "xc", "r", "ii", "a", "a2", "h", "t1", "t2"]
        L = {n: T("L_" + n, [128, G + 3] if n == "xb" else [128, G]) for n in names}
        lr_ps, lr_ps_d = PS("lr_ps", [128, G])
        li_ps, li_ps_d = PS("li_ps", [128, G])
        I("dve", vec.memset, L["xb"][0][:, 0:3], 0.0, writes=[L["xb"][1]])
        hprev, hprev_d = T("hprev", [128, 1])
        I("dve", vec.memset, hprev[:], 0.0, writes=[hprev_d])

        hn = ["f", "q", "g", "sg", "k", "lf", "bc", "E", "Ei", "qt", "kt", "o", "y", "rs"]
        H = {n: T("H_" + n, [128, G]) for n in hn}
        osq, osq_d = T("H_osq", [128, G], BF16)
        ftok, ftok_d = T("ftok", [32, 16, 128])
        vtok, vtok_d = T("vtok", [128, 4, 128])
        vtok32, vtok32_d = T("vtok32", [32, 16, 128])
        ktok, ktok_d = T("ktok", [32, 16, 128])
        lftok, lftok_d = T("lftok", [32, 16, 128])
        khat, khat_d = T("khat", [32, 16, 128])
        attm, attm_d = T("attm", [128, 128])
        Sb = [T("Sst%d" % i, [128, 128]) for i in range(2)]
        I("dve", vec.memset, Sb[0][0][:], 0.0, writes=[Sb[0][1]])
        ex_ps, ex_ps_d = PS("ex_ps", [32, 4, 128])
        att_ps, att_ps_d = PS("att_ps", [128, 128])
        o_ps, o_ps_d = PS("o_ps", [128, 128])
        sk_ps, sk_ps_d = PS("sk_ps", [128, 128])
        ss_ps, ss_ps_d = PS("ss_ps", [128, G])
        si = 0

        for g in range(NG):
            gs = slice(g * G, (g + 1) * G)
            xb, xb_d = L["xb"]
            gt, gt_d = L["gt"]
            xc, xc_d = L["xc"]
            P.dma("sp", v4(xb[:, 3:G + 3]), io.fm("l_xb", g), reads=io.rd, writes=[xb_d])
            P.dma("sp", v4(gt[:]), io.fm("l_gate", g), reads=io.rd, writes=[gt_d])
            I("dve", vec.tensor_scalar, out=xc[:], in0=xb[:, 0:G], scalar1=lpar[:, 0:1], scalar2=lpar[:, 4:5],
              op0=ALU.mult, op1=ALU.add, reads=[xb_d, lpar_d], writes=[xc_d])
            for j in range(1, 4):
                I("dve", vec.scalar_tensor_tensor, out=xc[:], in0=xb[:, j:G + j], scalar=lpar[:, j:j + 1],
                  in1=xc[:], op0=ALU.mult, op1=ALU.add, reads=[xb_d, lpar_d, xc_d], writes=[xc_d])
            I("pe", pe.matmul, lr_ps[:], lhsT=lwa[:], rhs=xc[:], start=True, stop=True,
              reads=[lwa_d, xc_d], writes=[lr_ps_d])
            I("pe", pe.matmul, li_ps[:], lhsT=lwx[:], rhs=xc[:], start=True, stop=True,
              reads=[lwx_d, xc_d], writes=[li_ps_d])
            r, r_d = L["r"]
            ii, ii_d = L["ii"]
            a, a_d = L["a"]
            a2, a2_d = L["a2"]
            h, h_d = L["h"]
            t1, t1_d = L["t1"]
            t2, t2_d = L["t2"]
            I("act", act.activation, out=r[:], in_=lr_ps[:], func=AF.Sigmoid, bias=lpar[:, 5:6],
              reads=[lr_ps_d, lpar_d], writes=[r_d])
            I("act", act.activation, out=ii[:], in_=li_ps[:], func=AF.Sigmoid, bias=lpar[:, 6:7],
              reads=[li_ps_d, lpar_d], writes=[ii_d])
            I("act", act.activation, out=a[:], in_=r[:], func=AF.Exp, scale=lder[:, 2:3],
              reads=[r_d, lder_d], writes=[a_d])
            I("act", act.activation, out=a2[:], in_=r[:], func=AF.Exp, scale=lder[:, 3:4],
              reads=[r_d, lder_d], writes=[a2_d])
            I("dve", vec.tensor_scalar, out=a2[:], in0=a2[:], scalar1=-1.0, scalar2=1.0, op0=ALU.mult, op1=ALU.add,
              reads=[a2_d], writes=[a2_d])
            I("act", act.activation, out=a2[:], in_=a2[:], func=AF.Sqrt, reads=[a2_d], writes=[a2_d])
            I("dve", vec.tensor_tensor, out=ii[:], in0=ii[:], in1=xc[:], op=ALU.mult, reads=[ii_d, xc_d], writes=[ii_d])
            I("dve", vec.tensor_tensor, out=ii[:], in0=ii[:], in1=a2[:], op=ALU.mult, reads=[ii_d, a2_d], writes=[ii_d])
            I("dve", vec.tensor_copy, out=xb[:, 0:3], in_=xb[:, G:G + 3], reads=[xb_d], writes=[xb_d])
            I("dve", vec.tensor_tensor_scan, out=h[:], data0=a[:], data1=ii[:], initial=hprev[:],
              op0=ALU.mult, op1=ALU.add, reads=[a_d, ii_d, hprev_d], writes=[h_d])
            I("dve", vec.tensor_copy, out=hprev[:], in_=h[:, G - 1:G], reads=[h_d], writes=[hprev_d])
            I("act", act.activation, out=t1[:], in_=gt[:], func=AF.Square, reads=[gt_d], writes=[t1_d])
            I("dve", vec.tensor_scalar, out=t1[:], in0=t1[:], scalar1=0.044715, scalar2=1.0, op0=ALU.mult, op1=ALU.add,
              reads=[t1_d], writes=[t1_d])
            I("dve", vec.tensor_tensor, out=t1[:], in0=t1[:], in1=gt[:], op=ALU.mult, reads=[t1_d, gt_d], writes=[t1_d])
            I("act", act.activation, out=t1[:], in_=t1[:], func=AF.Sigmoid, scale=1.5957691216057308,
              reads=[t1_d], writes=[t1_d])
            I("dve", vec.tensor_tensor, out=t1[:], in0=t1[:], in1=gt[:], op=ALU.mult, reads=[t1_d, gt_d], writes=[t1_d])
            I("dve", vec.tensor_tensor, out=t2[:], in0=t1[:], in1=h[:], op=ALU.mult, reads=[t1_d, h_d], writes=[t2_d])
            P.dma("sp", io.out("yT", 0, 128, g), v4(t2[:]), reads=[t2_d], writes=io.wr, is_output=True)

            f_, f_d = H["f"]
            q_, q_d = H["q"]
            g_, g_d = H["g"]
            P.dma("sp", v4(f_[:]), io.fm("h_f", g), reads=io.rd, writes=[f_d])
            P.dma("sp", v4(q_[:]), io.fm("h_q", g), reads=io.rd, writes=[q_d])
            P.dma("sp", v4(g_[:]), io.fm("h_g", g), reads=io.rd, writes=[g_d])
            for a_, src_ in enumerate(io.tm32("h_ftok", g)):
                P.dma("sp", ftok[:, 4 * a_:4 * a_ + 4, :], src_, reads=io.rd, writes=[ftok_d])
            for a_, src_ in enumerate(io.tm32("h_vtok", g)):
                P.dma("sp", vtok32[:, 4 * a_:4 * a_ + 4, :], src_, reads=io.rd, writes=[vtok32_d])
            P.dma("sp", vtok[:], io.tm("h_vtok", g), reads=io.rd, writes=[vtok_d])
            sg, sg_d = H["sg"]
            k_, k_d = H["k"]
            lf, lf_d = H["lf"]
            bc, bc_d = H["bc"]
            E, E_d = H["E"]
            Ei, Ei_d = H["Ei"]
            qt, qt_d = H["qt"]
            kt, kt_d = H["kt"]
            o_, o_d = H["o"]
            y_, y_d = H["y"]
            rs, rs_d = H["rs"]
            I("act", act.activation, out=sg[:], in_=f_[:], func=AF.Sigmoid, scale=-1.0, reads=[f_d], writes=[sg_d])
            I("dve", vec.tensor_scalar, out=k_[:], in0=sg[:], scalar1=hder[:, 1:2], scalar2=None, op0=ALU.mult,
              reads=[sg_d, hder_d], writes=[k_d])
            I("dve", vec.tensor_scalar, out=lf[:], in0=k_[:], scalar1=-1.0, scalar2=1.0, op0=ALU.mult, op1=ALU.add,
              reads=[k_d], writes=[lf_d])
            I("act", act.activation, out=lf[:], in_=lf[:], func=AF.Ln, reads=[lf_d], writes=[lf_d])
            I("dve", vec.tensor_tensor_scan, out=bc[:], data0=cmask, data1=lf[:], initial=0.0,
              op0=ALU.mult, op1=ALU.add, reads=[cs_d, lf_d], writes=[bc_d])
            I("act", act.activation, out=E[:], in_=bc[:], func=AF.Exp, reads=[bc_d], writes=[E_d])
            I("act", act.activation, out=Ei[:], in_=bc[:], func=AF.Exp, scale=-1.0, reads=[bc_d], writes=[Ei_d])
            I("act", act.activation, out=qt[:], in_=q_[:], func=AF.Silu, reads=[q_d], writes=[qt_d])
            I("dve", vec.tensor_tensor, out=qt[:], in0=qt[:], in1=E[:], op=ALU.mult, reads=[qt_d, E_d], writes=[qt_d])
            I("dve", vec.tensor_tensor, out=kt[:], in0=k_[:], in1=Ei[:], op=ALU.mult, reads=[k_d, Ei_d], writes=[kt_d])
            I("act", act.activation, out=ktok[:], in_=ftok[:], func=AF.Sigmoid, scale=-1.0,
              reads=[ftok_d], writes=[ktok_d])
            I("dve", vec.tensor_tensor, out=ktok[:], in0=ktok[:],
              in1=omlrow[0:32, :].unsqueeze(1).to_broadcast([32, 16, 128]), op=ALU.mult,
              reads=[ktok_d, omlrow_d], writes=[ktok_d])
            I("dve", vec.tensor_scalar, out=lftok[:], in0=ktok[:], scalar1=-1.0, scalar2=1.0, op0=ALU.mult, op1=ALU.add,
              reads=[ktok_d], writes=[lftok_d])
            I("act", act.activation, out=lftok[:], in_=lftok[:], func=AF.Ln, reads=[lftok_d], writes=[lftok_d])
            for b in range(4):
                I("pe", pe.matmul, ex_ps[:], lhsT=Mgt32, rhs=lftok[:, 4 * b:4 * b + 4, :], start=True, stop=True,
                  reads=[cs_d, lftok_d], writes=[ex_ps_d])
                I("act", act.activation, out=khat[:, 4 * b:4 * b + 4, :], in_=ex_ps[:], func=AF.Exp,
                  reads=[ex_ps_d], writes=[khat_d])
            I("dve", vec.tensor_tensor, out=khat[:], in0=khat[:], in1=ktok[:], op=ALU.mult,
              reads=[khat_d, ktok_d], writes=[khat_d])
            for b in range(4):
                bs = slice(b * 128, (b + 1) * 128)
                I("pe", pe.matmul, att_ps[:], lhsT=kt[:, bs], rhs=qt[:, bs], start=True, stop=True,
                  reads=[kt_d, qt_d], writes=[att_ps_d])
                I("dve", vec.tensor_tensor, out=attm[:], in0=att_ps[:], in1=Mle, op=ALU.mult,
                  reads=[att_ps_d, cs_d], writes=[attm_d])
                I("pe", pe.matmul, o_ps[:], lhsT=vtok[:, b, :], rhs=attm[:], start=True, stop=False,
                  reads=[vtok_d, attm_d], writes=[o_ps_d])
                for k in range(4):
                    c0 = b * 128 + k * 32
                    Sc, Sc_d = Sb[si]
                    Sn, Sn_d = Sb[1 - si]
                    I("pe", pe.matmul, o_ps[:, k * 32:(k + 1) * 32], lhsT=Sc[:], rhs=qt[:, c0:c0 + 32],
                      start=False, stop=(k == 3), reads=[Sc_d, qt_d], writes=[o_ps_d])
                    I("pe", pe.matmul, sk_ps[:], lhsT=khat[:, 4 * b + k, :], rhs=vtok32[:, 4 * b + k, :],
                      start=True, stop=True, reads=[khat_d, vtok32_d], writes=[sk_ps_d])
                    I("dve", vec.scalar_tensor_tensor, out=Sn[:], in0=Sc[:], scalar=E[:, c0 + 31:c0 + 32],
                      in1=sk_ps[:], op0=ALU.mult, op1=ALU.add, reads=[Sc_d, E_d, sk_ps_d], writes=[Sn_d])
                    si = 1 - si
                I("act", act.copy, out=o_[:, bs], in_=o_ps[:], reads=[o_ps_d], writes=[o_d])
            I("act", act.activation, out=osq[:], in_=o_[:], func=AF.Square, reads=[o_d], writes=[osq_d])
            I("pe", pe.matmul, ss_ps[:], lhsT=ones_bf[:], rhs=osq[:], start=True, stop=True,
              reads=[ones_d, osq_d], writes=[ss_ps_d])
            I("dve", vec.tensor_scalar, out=rs[:], in0=ss_ps[:], scalar1=1.0 / 128, scalar2=1e-6, op0=ALU.mult, op1=ALU.add,
              reads=[ss_ps_d], writes=[rs_d])
            I("act", act.activation, out=rs[:], in_=rs[:], func=AF.Sqrt, reads=[rs_d], writes=[rs_d])
            I("dve", vec.reciprocal, out=rs[:], in_=rs[:], reads=[rs_d], writes=[rs_d])
            I("dve", vec.tensor_tensor, out=o_[:], in0=o_[:], in1=rs[:], op=ALU.mult, reads=[o_d, rs_d], writes=[o_d])
            I("act", act.activation, out=g_[:], in_=g_[:], func=AF.Silu, reads=[g_d], writes=[g_d])
            I("dve", vec.scalar_tensor_tensor, out=y_[:], in0=o_[:], scalar=hpar[:, 2:3], in1=g_[:],
              op0=ALU.mult, op1=ALU.mult, reads=[o_d, hpar_d, g_d], writes=[y_d])
            P.dma("sp", io.out("yT", 128, 256, g), v4(y_[:]), reads=[y_d], writes=io.wr, is_output=True)
        P.close_scope()


def mixb_consts():
    t = np.arange(128)
    same = (t[:, None] // 32) == (t[None, :] // 32)
    Mgt = (same & (t[:, None] > t[None, :])).astype(np.float32)
    Mle = (same & (t[:, None] <= t[None, :])).astype(np.float32)
    cm = np.ones(512, np.float32)
    cm[::32] = 0.0
    out = np.zeros((128, 1024), np.float32)
    out[:, 0:128] = Mgt
    out[:, 128:256] = Mle
    out[:, 256:768] = cm[None, :]
    return out


def mixb_params(p, c):
    r = slice(c * 128, (c + 1) * 128)
    lpar = np.zeros((128, 16), np.float32)
    lpar[:, 0:4] = p["lru_conv_w"][0][:, r].T
    lpar[:, 4] = p["lru_conv_b"][0][r]
    lpar[:, 5] = p["lru_ba"][0][r]
    lpar[:, 6] = p["lru_bx"][0][r]
    lpar[:, 7] = p["lru_lambda"][0][r]
    hpar = np.zeros((128, 8), np.float32)
    hpar[:, 0] = p["hg_lb"][0][r]
    hpar[:, 1] = p["hg_lb"][1][r]
    hpar[:, 2] = p["hg_norm"][0][r]
    lbrow = np.zeros((128, 256), np.float32)
    lbrow[:, 0:128] = p["hg_lb"][0][r][None, :]
    lbrow[:, 128:256] = p["hg_lb"][1][r][None, :]
    return {"l_par": lpar, "l_wa": np.ascontiguousarray(p["lru_wa"][0][c]), "l_wx": np.ascontiguousarray(p["lru_wx"][0][c]),
            "h_par": hpar, "h_lbrow": lbrow}


def mixb_inputs(uT, p, S):
    maps = []
    cst = mixb_consts()
    for c in range(NCORES):
        r = slice(c * 128, (c + 1) * 128)
        lpar = np.zeros((128, 16), np.float32)
        lpar[:, 0:4] = p["lru_conv_w"][0][:, r].T
        lpar[:, 4] = p["lru_conv_b"][0][r]
        lpar[:, 5] = p["lru_ba"][0][r]
        lpar[:, 6] = p["lru_bx"][0][r]
        lpar[:, 7] = p["lru_lambda"][0][r]
        hpar = np.zeros((128, 8), np.float32)
        hpar[:, 0] = p["hg_lb"][0][r]
        hpar[:, 1] = p["hg_lb"][1][r]
        hpar[:, 2] = p["hg_norm"][0][r]
        lbrow = np.zeros((128, 256), np.float32)
        lbrow[:, 0:128] = p["hg_lb"][0][r][None, :]
        lbrow[:, 128:256] = p["hg_lb"][1][r][None, :]
        o = 2048
        maps.append({
            "l_gate": np.ascontiguousarray(uT[c * 128:(c + 1) * 128]),
            "l_xb": np.ascontiguousarray(uT[1024 + c * 128:1024 + (c + 1) * 128]),
            "l_par": lpar, "l_wa": np.ascontiguousarray(p["lru_wa"][0][c]),
            "l_wx": np.ascontiguousarray(p["lru_wx"][0][c]),
            "h_q": np.ascontiguousarray(uT[o + c * 128:o + (c + 1) * 128]),
            "h_f": np.ascontiguousarray(uT[o + 1024 + c * 128:o + 1024 + (c + 1) * 128]),
            "h_g": np.ascontiguousarray(uT[o + 3072 + c * 128:o + 3072 + (c + 1) * 128]),
            "h_ftok": np.ascontiguousarray(uT[o + 1024 + c * 128:o + 1024 + (c + 1) * 128].T),
            "h_vtok": np.ascontiguousarray(uT[o + 2048 + c * 128:o + 2048 + (c + 1) * 128].T),
            "h_par": hpar, "h_lbrow": lbrow, "mb_cst": cst,
        })
    return maps


def build_rwkv(S):
    nc = new_nc()
    specs = {"rw_r": [128, S], "rw_k": [128, S], "rw_v": [128, S], "rw_wl": [96, S], "rw_al": [96, S],
             "rw_gl0": [128, S], "rw_gl1": [128, S], "rw_par": [128, 24], "rw_w2": [96, 128], "rw_a2": [96, 128],
             "rw_g2": [128, 2, 128], "rw_cst": [128, 192]}
    io = ExtIO(nc, specs, {"rw_y": [128, S]})
    with ExitStack() as es:
        P = Prog(nc, es)
        emit_rwkv(P, S, io)
        P.finish()
    return nc


def emit_rwkv(P, S, io):
    nc = P.nc
    NG = S // G
    TC = 8
    par, w2_i, a2_i, g2_i, cst = (io.par(n) for n in ("rw_par", "rw_w2", "rw_a2", "rw_g2", "rw_cst"))
    if True:
        P.open_scope()
        I = P.I
        act, vec, pe, pool = nc.scalar, nc.vector, nc.tensor, nc.gpsimd

        def T(name, shape, dtype=F32):
            return P.sbuf(name, shape, dtype), P.dep(name)

        def PS(name, shape):
            return P.psum(name, shape, F32), P.dep(name)

        pr, pr_d = T("par_sb", [128, 24])
        w2, w2_d = T("w2_sb", [96, 128])
        a2, a2_d = T("a2_sb", [96, 128])
        g2, g2_d = T("g2_sb", [128, 2, 128])
        cs, cs_d = T("cs_sb", [128, 192])
        for (t, d, src) in ((pr, pr_d, par), (w2, w2_d, w2_i), (a2, a2_d, a2_i), (g2, g2_d, g2_i), (cs, cs_d, cst)):
            P.dma("sp", t[:], src, writes=[d])
        bones = cs[:, 0:128]
        I64 = cs[:, 128:192]
        for (src, dst) in ((0, 14), (1, 15), (2, 16), (10, 17), (11, 18), (12, 19), (13, 20), (6, 21)):
            I("dve", vec.tensor_scalar, out=pr[:, dst:dst + 1], in0=pr[:, src:src + 1], scalar1=-1.0, scalar2=1.0,
              op0=ALU.mult, op1=ALU.add, reads=[pr_d], writes=[pr_d])

        x3, x3_d = T("x3", [128, 3, G + 1])
        wlr, wlr_d = T("wlr", [96, G + 1])
        alr, alr_d = T("alr", [96, G + 1])
        glr, glr_d = T("glr", [128, 2, G + 1])
        names = ["r", "k", "v", "t", "w", "a", "kk", "sq", "nr", "kf", "al", "be", "y", "yc", "gg", "tmp"]
        W = {n: T("W_" + n, [128, G]) for n in names}
        wls, wls_d = T("wls", [96, G])
        als, als_d = T("als", [96, G])
        gls, gls_d = T("gls", [128, 2, G])
        ps_a, ps_a_d = PS("ps_a", [128, G])
        ps_b, ps_b_d = PS("ps_b", [128, G])
        bcp = [PS("bcp%d" % i, [128, TC, 64]) for i in range(5)]
        Rb = [[T("Rb%d_%d" % (i, p), [128, TC, 64]) for p in range(2)] for i in range(5)]
        bc = [[T("bc%d_%d" % (i, p), [128, TC, 64]) for p in range(2)] for i in range(5)]
        St = [T("St%d" % i, [128, 64]) for i in range(2)]
        junk, _ = T("junk", [128, 64])
        sa, sa_d = T("sa", [128, 2])
        I("dve", vec.memset, St[0][0][:], 0.0, writes=[St[0][1]])
        I("dve", vec.memset, x3[:, :, 0:1], 0.0, writes=[x3_d])
        I("dve", vec.memset, wlr[:, 0:1], 0.0, writes=[wlr_d])
        I("dve", vec.memset, alr[:, 0:1], 0.0, writes=[alr_d])
        I("dve", vec.memset, glr[:, :, 0:1], 0.0, writes=[glr_d])
        si = 0
        par_i = 0
        nstep = 0

        def shift(dst, dst_d, src_cur, src_prev, src_d, mu_col, omm_col, np_=128):
            tmp, tmp_d = W["tmp"]
            I("dve", vec.tensor_scalar, out=tmp[0:np_, :], in0=src_prev, scalar1=pr[0:np_, mu_col:mu_col + 1], scalar2=None,
              op0=ALU.mult, reads=[src_d, pr_d], writes=[tmp_d])
            I("dve", vec.scalar_tensor_tensor, out=dst, in0=src_cur, scalar=pr[0:np_, omm_col:omm_col + 1], in1=tmp[0:np_, :],
              op0=ALU.mult, op1=ALU.add, reads=[src_d, pr_d, tmp_d], writes=[dst_d])

        for g in range(NG):
            gs = slice(g * G, (g + 1) * G)
            gs1 = slice(g * G, (g + 1) * G + 1)
            for b_, nm in enumerate(("rw_r", "rw_k", "rw_v")):
                P.dma("sp", v4(x3[:, b_, 1:G + 1]), io.fm(nm, g), reads=io.rd, writes=[x3_d])
            P.dma("sp", v4(wlr[:, 1:G + 1]), io.fm("rw_wl", g), reads=io.rd, writes=[wlr_d])
            P.dma("sp", v4(alr[:, 1:G + 1]), io.fm("rw_al", g), reads=io.rd, writes=[alr_d])
            P.dma("sp", v4(glr[:, 0, 1:G + 1]), io.fm("rw_gl0", g), reads=io.rd, writes=[glr_d])
            P.dma("sp", v4(glr[:, 1, 1:G + 1]), io.fm("rw_gl1", g), reads=io.rd, writes=[glr_d])
            r_, r_d = W["r"]
            k_, k_d = W["k"]
            v_, v_d = W["v"]
            shift(r_[:], r_d, x3[:, 0, 1:G + 1], x3[:, 0, 0:G], x3_d, 0, 14)
            shift(k_[:], k_d, x3[:, 1, 1:G + 1], x3[:, 1, 0:G], x3_d, 1, 15)
            shift(v_[:], v_d, x3[:, 2, 1:G + 1], x3[:, 2, 0:G], x3_d, 2, 16)
            shift(wls[:], wls_d, wlr[:, 1:G + 1], wlr[:, 0:G], wlr_d, 10, 17, 96)
            shift(als[:], als_d, alr[:, 1:G + 1], alr[:, 0:G], alr_d, 11, 18, 96)
            shift(gls[:, 0, :], gls_d, glr[:, 0, 1:G + 1], glr[:, 0, 0:G], glr_d, 12, 19)
            shift(gls[:, 1, :], gls_d, glr[:, 1, 1:G + 1], glr[:, 1, 0:G], glr_d, 13, 20)
            I("dve", vec.tensor_copy, out=x3[:, :, 0:1], in_=x3[:, :, G:G + 1], reads=[x3_d], writes=[x3_d])
            I("dve", vec.tensor_copy, out=wlr[:, 0:1], in_=wlr[:, G:G + 1], reads=[wlr_d], writes=[wlr_d])
            I("dve", vec.tensor_copy, out=alr[:, 0:1], in_=alr[:, G:G + 1], reads=[alr_d], writes=[alr_d])
            I("dve", vec.tensor_copy, out=glr[:, :, 0:1], in_=glr[:, :, G:G + 1], reads=[glr_d], writes=[glr_d])
            w_, w_d = W["w"]
            a_, a_d = W["a"]
            I("act", act.activation, out=wls[:], in_=wls[:], func=AF.Tanh, reads=[wls_d], writes=[wls_d])
            I("pe", pe.matmul, ps_a[:], lhsT=w2[:], rhs=wls[:], start=True, stop=True, reads=[w2_d, wls_d], writes=[ps_a_d])
            I("act", act.activation, out=w_[:], in_=ps_a[:], func=AF.Sigmoid, bias=pr[:, 3:4], reads=[ps_a_d, pr_d], writes=[w_d])
            I("act", act.activation, out=w_[:], in_=w_[:], func=AF.Exp, scale=-0.6065306597126334, reads=[w_d], writes=[w_d])
            I("pe", pe.matmul, ps_b[:], lhsT=a2[:], rhs=als[:], start=True, stop=True, reads=[a2_d, als_d], writes=[ps_b_d])
            I("act", act.activation, out=a_[:], in_=ps_b[:], func=AF.Sigmoid, bias=pr[:, 4:5], reads=[ps_b_d, pr_d], writes=[a_d])
            gg, gg_d = W["gg"]
            I("act", act.activation, out=gls[:], in_=gls[:], func=AF.Sigmoid, reads=[gls_d], writes=[gls_d])
            for c2 in range(2):
                I("pe", pe.matmul, ps_a[:], lhsT=g2[:, c2, :], rhs=gls[:, c2, :], start=(c2 == 0), stop=(c2 == 1),
                  reads=[g2_d, gls_d], writes=[ps_a_d])
            I("act", act.copy, out=gg[:], in_=ps_a[:], reads=[ps_a_d], writes=[gg_d])
            kk, kk_d = W["kk"]
            sq, sq_d = W["sq"]
            nr, nr_d = W["nr"]
            kf, kf_d = W["kf"]
            al, al_d = W["al"]
            be, be_d = W["be"]
            I("dve", vec.tensor_scalar, out=kk[:], in0=k_[:], scalar1=pr[:, 5:6], scalar2=None, op0=ALU.mult,
              reads=[k_d, pr_d], writes=[kk_d])
            I("act", act.activation, out=sq[:], in_=kk[:], func=AF.Square, reads=[kk_d], writes=[sq_d])
            I("pe", pe.matmul, ps_b[:], lhsT=bones, rhs=sq[:], start=True, stop=True, reads=[cs_d, sq_d], writes=[ps_b_d])
            I("act", act.activation, out=nr[:], in_=ps_b[:], func=AF.Sqrt, reads=[ps_b_d], writes=[nr_d])
            I("dve", vec.tensor_scalar, out=nr[:], in0=nr[:], scalar1=1e-12, scalar2=None, op0=ALU.max,
              reads=[nr_d], writes=[nr_d])
            I("dve", vec.reciprocal, out=nr[:], in_=nr[:], reads=[nr_d], writes=[nr_d])
            I("dve", vec.tensor_tensor, out=kk[:], in0=kk[:], in1=nr[:], op=ALU.mult, reads=[kk_d, nr_d], writes=[kk_d])
            I("dve", vec.tensor_scalar, out=al[:], in0=kk[:], scalar1=-1.0, scalar2=None, op0=ALU.mult,
              reads=[kk_d], writes=[al_d])
            I("dve", vec.tensor_tensor, out=be[:], in0=kk[:], in1=a_[:], op=ALU.mult, reads=[kk_d, a_d], writes=[be_d])
            I("dve", vec.tensor_scalar, out=kf[:], in0=a_[:], scalar1=pr[:, 6:7], scalar2=pr[:, 21:22], op0=ALU.mult, op1=ALU.add,
              reads=[a_d, pr_d], writes=[kf_d])
            I("dve", vec.tensor_tensor, out=kf[:], in0=kf[:], in1=k_[:], op=ALU.mult, reads=[kf_d, k_d], writes=[kf_d])
            y_, y_d = W["y"]
            vecs = [(w_, w_d), (al, al_d), (be, be_d), (kf, kf_d), (r_, r_d)]
            last_ev = None
            for ch in range(G // TC):
                t0 = ch * TC
                for xi, (xv, xv_d) in enumerate(vecs):
                    Rt, Rt_d = Rb[xi][par_i]
                    bt, bt_d = bc[xi][par_i]
                    bp, bp_d = bcp[xi]
                    I("pool", pool.tensor_tensor, out=Rt[:], in0=xv[:, t0:t0 + TC].unsqueeze(2).to_broadcast([128, TC, 64]),
                      in1=I64.unsqueeze(1).to_broadcast([128, TC, 64]), op=ALU.mult,
                      reads=[xv_d, cs_d], writes=[Rt_d])
                    I("pe", pe.matmul, bp[:], lhsT=bones, rhs=Rt[:], start=True, stop=True, reads=[cs_d, Rt_d], writes=[bp_d])
                    I("act", act.copy, out=bt[:], in_=bp[:], reads=[bp_d], writes=[bt_d])
                wb, alb, beb, kb, rb = [bc[xi][par_i] for xi in range(5)]
                for tl in range(TC):
                    t = t0 + tl
                    Sc, Sc_d = St[si]
                    Sn, Sn_d = St[1 - si]
                    sc = nstep % 2
                    I("dve", vec.scalar_tensor_tensor, out=junk[:], in0=Sc[:], scalar=1.0, in1=alb[0][:, tl, :],
                      op0=ALU.mult, op1=ALU.mult, accum_out=sa[:, sc:sc + 1], reads=[Sc_d, alb[1]], writes=[sa_d])
                    I("dve", vec.tensor_tensor, out=Sn[:], in0=Sc[:], in1=wb[0][:, tl, :], op=ALU.mult,
                      reads=[Sc_d, wb[1]], writes=[Sn_d])
                    I("dve", vec.scalar_tensor_tensor, out=Sn[:], in0=beb[0][:, tl, :], scalar=sa[:, sc:sc + 1], in1=Sn[:],
                      op0=ALU.mult, op1=ALU.add, reads=[beb[1], sa_d, Sn_d], writes=[Sn_d])
                    I("dve", vec.scalar_tensor_tensor, out=Sn[:], in0=kb[0][:, tl, :], scalar=v_[:, t:t + 1], in1=Sn[:],
                      op0=ALU.mult, op1=ALU.add, reads=[kb[1], v_d, Sn_d], writes=[Sn_d])
                    last_ev = I("dve", vec.scalar_tensor_tensor, out=junk[:], in0=Sn[:], scalar=1.0, in1=rb[0][:, tl, :],
                                op0=ALU.mult, op1=ALU.mult, accum_out=y_[:, t:t + 1], reads=[Sn_d, rb[1], y_d], writes=[])
                    si = 1 - si
                    nstep += 1
                par_i = 1 - par_i
            y_d.w = last_ev
            y_d.r = []
            yc, yc_d = W["yc"]
            I("pe", pe.matmul, ps_a[:], lhsT=bones, rhs=y_[:], start=True, stop=True, reads=[cs_d, y_d], writes=[ps_a_d])
            I("dve", vec.scalar_tensor_tensor, out=yc[:], in0=ps_a[:], scalar=-1.0 / 64, in1=y_[:], op0=ALU.mult, op1=ALU.add,
              reads=[ps_a_d, y_d], writes=[yc_d])
            I("act", act.activation, out=sq[:], in_=yc[:], func=AF.Square, reads=[yc_d], writes=[sq_d])
            I("pe", pe.matmul, ps_b[:], lhsT=bones, rhs=sq[:], start=True, stop=True, reads=[cs_d, sq_d], writes=[ps_b_d])
            I("dve", vec.tensor_scalar, out=nr[:], in0=ps_b[:], scalar1=1.0 / 64, scalar2=64e-5, op0=ALU.mult, op1=ALU.add,
              reads=[ps_b_d], writes=[nr_d])
            I("act", act.activation, out=nr[:], in_=nr[:], func=AF.Sqrt, reads=[nr_d], writes=[nr_d])
            I("dve", vec.reciprocal, out=nr[:], in_=nr[:], reads=[nr_d], writes=[nr_d])
            I("dve", vec.tensor_tensor, out=yc[:], in0=yc[:], in1=nr[:], op=ALU.mult, reads=[yc_d, nr_d], writes=[yc_d])
            I("dve", vec.tensor_scalar, out=yc[:], in0=yc[:], scalar1=pr[:, 8:9], scalar2=pr[:, 9:10], op0=ALU.mult, op1=ALU.add,
              reads=[yc_d, pr_d], writes=[yc_d])
            I("dve", vec.scalar_tensor_tensor, out=sq[:], in0=r_[:], scalar=pr[:, 7:8], in1=kf[:], op0=ALU.mult, op1=ALU.mult,
              reads=[r_d, pr_d, kf_d], writes=[sq_d])
            I("pe", pe.matmul, ps_a[:], lhsT=bones, rhs=sq[:], start=True, stop=True, reads=[cs_d, sq_d], writes=[ps_a_d])
            I("dve", vec.tensor_tensor, out=sq[:], in0=ps_a[:], in1=v_[:], op=ALU.mult, reads=[ps_a_d, v_d], writes=[sq_d])
            I("dve", vec.tensor_tensor, out=yc[:], in0=yc[:], in1=sq[:], op=ALU.add, reads=[yc_d, sq_d], writes=[yc_d])
            I("dve", vec.tensor_tensor, out=yc[:], in0=yc[:], in1=gg[:], op=ALU.mult, reads=[yc_d, gg_d], writes=[yc_d])
            P.dma("sp", io.out("rw_y", 0, 128, g), v4(yc[:]), reads=[yc_d], writes=io.wr, is_output=True)
        P.close_scope()


def rwkv_cst():
    cst = np.zeros((128, 192), np.float32)
    hh = np.arange(128) // 64
    cst[:, 0:128] = (hh[:, None] == hh[None, :]).astype(np.float32)
    cst[np.arange(128), 128 + (np.arange(128) % 64)] = 1.0
    return cst


def rwkv_params(p, c):
    r = slice(c * 128, (c + 1) * 128)
    mu = p["rw_mu"][0]
    par = np.zeros((128, 24), np.float32)
    par[:, 0] = mu[0:1024][r]
    par[:, 1] = mu[1024:2048][r]
    par[:, 2] = mu[2048:3072][r]
    par[:, 3] = p["rw_w0"][0][r]
    par[:, 4] = p["rw_a0"][0][r]
    par[:, 5] = p["rw_k_k"][0][r]
    par[:, 6] = p["rw_k_a"][0][r]
    par[:, 7] = p["rw_r_k"][0].reshape(-1)[r]
    par[:, 8] = p["rw_lnx_w"][0][r]
    par[:, 9] = p["rw_lnx_b"][0][r]
    par[0:96, 10] = mu[3072:3168]
    par[0:96, 11] = mu[3168:3264]
    par[:, 12] = mu[3264:3392]
    par[:, 13] = mu[3392:3520]
    return {"rw_par": par, "rw_w2": np.ascontiguousarray(p["rw_w2"][0][:, r]),
            "rw_a2": np.ascontiguousarray(p["rw_a2"][0][:, r]),
            "rw_g2": np.ascontiguousarray(p["rw_g2"][0][:, r].reshape(2, 128, 128).transpose(1, 0, 2)),
            "rw_cst": rwkv_cst()}


def rwkv_inputs(uT, p, S):
    maps = []
    for c in range(NCORES):
        r = slice(c * 128, (c + 1) * 128)
        m = rwkv_params(p, c)
        m.update({"rw_r": np.ascontiguousarray(uT[0:1024][r]), "rw_k": np.ascontiguousarray(uT[1024:2048][r]),
                  "rw_v": np.ascontiguousarray(uT[2048:3072][r]), "rw_wl": np.ascontiguousarray(uT[3072:3168]),
                  "rw_al": np.ascontiguousarray(uT[3168:3264]), "rw_gl0": np.ascontiguousarray(uT[3264:3392]),
                  "rw_gl1": np.ascontiguousarray(uT[3392:3520])})
        maps.append(m)
    return maps


NEG = -30000.0


def nsa_dims(S):
    NKT = S // 128
    NOWN = NKT // 8
    NCMP = S // 16
    NNT = NCMP // 128
    NMT = max(1, (S // 64 + 127) // 128)
    NM = NMT * 128
    return NKT, NOWN, NCMP, NNT, NMT, NM


def nsa_const_specs(S):
    NKT, NOWN, NCMP, NNT, NMT, NM = nsa_dims(S)
    return {"cosq": [128, NOWN, 128], "sinq": [128, NOWN, 128], "cosk": [128, S], "sink": [128, S],
            "ccos": [128, NCMP], "csin": [128, NCMP], "ngain": [128, 4], "cmpw": [32, 128, 2, 128],
            "peT": [128, 2, 32], "cmask": [128, NOWN, NNT, 128], "dmask": [128, 8, 128], "wmask": [128, 12, 128],
            "tk": [128, NOWN, 3, NM], "rsel": [128, 8192], "ovl": [128, NNT, NM], "permid": [128, 256]}


class NsaExtIO(ExtIO):
    def __init__(self, nc, S):
        NKT, NOWN, NCMP, NNT, NMT, NM = nsa_dims(S)
        specs = dict(nsa_const_specs(S))
        specs.update({"q_own": [128, NOWN, 8, 128], "gates": [128, NOWN, 24]})
        for g in range(2):
            specs.update({"kc%d" % g: [128, S], "vc%d" % g: [128, S], "ks%d" % g: [128, S], "kw%d" % g: [128, S],
                          "vs_tok%d" % g: [S, 128], "vw_tok%d" % g: [S, 128]})
        ExtIO.__init__(self, nc, specs, {"ny": [1024, NOWN * 128]})

    def nq(self, ti, g):
        return self.t["q_own"][:, ti, 4 * g:4 * g + 4, :]

    def ngates(self):
        return self.t["gates"]

    def tmw(self, n, ti):
        v = self.t[n].rearrange("(k p) d -> p k d", p=128)
        if ti == 0:
            return [(4, 12, v[:, 0:8, :])]
        return [(0, 12, v[:, 8 * ti - 4:8 * ti + 8, :])]

    def nout(self, ti, g):
        return self.o["ny"][4 * g * 128:(4 * g + 4) * 128, ti * 128:(ti + 1) * 128].rearrange("(j d) q -> d j q", d=128)


def build_nsa(S):
    nc = new_nc()
    io = NsaExtIO(nc, S)
    with ExitStack() as es:
        P = Prog(nc, es)
        emit_nsa(P, S, io)
        P.finish()
    return nc


def emit_nsa(P, S, io):
    nc = P.nc
    NKT, NOWN, NCMP, NNT, NMT, NM = nsa_dims(S)
    (cosq, sinq, cosk, sink, ccos, csin, gain_i, cmpw, peT_i, cmask, dmask_i, wmask_i, tk_i, rsel_i, ovl_i, pi_i) = (
        io.par(n) for n in ("cosq", "sinq", "cosk", "sink", "ccos", "csin", "ngain", "cmpw", "peT", "cmask", "dmask",
                            "wmask", "tk", "rsel", "ovl", "permid"))
    gates = io.ngates()
    if True:
        P.open_scope()
        I = P.I
        act, vec, pe, pool = nc.scalar, nc.vector, nc.tensor, nc.gpsimd

        def T(name, shape, dtype=F32):
            return P.sbuf(name, shape, dtype), P.dep(name)

        B = [(P.psum("bank%d" % i, [128, 512], F32), P.dep("bank%d" % i)) for i in range(8)]

        gain, gain_d = T("gain_sb", [128, 4])
        pid, pid_d = T("pid_sb", [128, 256])
        dmask, dmask_d = T("dmask_sb", [128, 8, 128])
        wmask, wmask_d = T("wmask_sb", [128, 12, 128])
        peT, peT_d = T("peT_sb", [128, 2, 32])
        peb, peb_d = T("peb", [128, 2, 32], BF16)
        ovf, ovf_d = T("ovf", [128, NNT, NM])
        ovl, ovl_d = T("ovl_sb", [128, NNT, NM], BF16)
        gts, gts_d = T("gts", [128, NOWN, 24])
        rsel, rsel_d = T("rsel_sb", [128, 8192], BF16)
        for (t, d, src) in ((gain, gain_d, gain_i), (pid, pid_d, pi_i), (dmask, dmask_d, dmask_i),
                            (wmask, wmask_d, wmask_i), (peT, peT_d, peT_i), (ovf, ovf_d, ovl_i), (gts, gts_d, gates)):
            P.dma("sp", t[:], src, reads=io.rd, writes=[d])
        permT = pid[:, 0:128]
        ident = pid[:, 128:256]
        I("dve", vec.tensor_copy, out=peb[:], in_=peT[:], reads=[peT_d], writes=[peb_d])
        I("dve", vec.tensor_copy, out=ovl[:], in_=ovf[:], reads=[ovf_d], writes=[ovl_d])
        I("act", act.activation, out=gts[:], in_=gts[:], func=AF.Sigmoid, reads=[gts_d], writes=[gts_d])
        gq, gq_d = T("gq", [128, 1])
        I("dve", vec.tensor_scalar, out=gq[:], in0=gain[:, 0:1], scalar1=128.0 ** -0.5, scalar2=None, op0=ALU.mult,
          reads=[gain_d], writes=[gq_d])
        onesf, onesf_d = T("onesf", [128, 128])
        onesb, onesb_d = T("onesb", [128, 128], BF16)
        I("dve", vec.memset, onesf[:], 1.0, writes=[onesf_d])
        I("dve", vec.memset, onesb[:], 1.0, writes=[onesb_d])
        stg = [T("stg%d" % i, [128, 512]) for i in range(3)]
        stg_i = [0]

        def nstg():
            s = stg[stg_i[0]]
            stg_i[0] = (stg_i[0] + 1) % 3
            return s

        for j in range(16):
            s_, s_d = nstg()
            P.dma("sp", s_[:], rsel_i[:, 512 * j:512 * j + 512], writes=[s_d])
            I("pool", pool.tensor_copy, out=rsel[:, 512 * j:512 * j + 512], in_=s_[:], reads=[s_d], writes=[rsel_d])

        tA, tA_d = T("tA", [128, 512])
        tB, tB_d = T("tB", [128, 512])
        tC, tC_d = T("tC", [128, 512])
        ctl, ctl_d = T("ctl", [128, 640])
        stl, stl_d = T("stl", [128, 640])

        def norm_rope(dst, dst_d, src, src_d, gcol, gcol_d, cos, sin, cs_deps, N):
            I("act", act.activation, out=tA[:, 0:N], in_=src, func=AF.Square, reads=[src_d], writes=[tA_d])
            I("pe", pe.matmul, B[0][0][:, 0:N], lhsT=onesf[:], rhs=tA[:, 0:N], start=True, stop=True,
              reads=[onesf_d, tA_d], writes=[B[0][1]])
            I("dve", vec.tensor_scalar, out=tB[:, 0:N], in0=B[0][0][:, 0:N], scalar1=1.0 / 128, scalar2=1e-6,
              op0=ALU.mult, op1=ALU.add, reads=[B[0][1]], writes=[tB_d])
            I("act", act.activation, out=tB[:, 0:N], in_=tB[:, 0:N], func=AF.Sqrt, reads=[tB_d], writes=[tB_d])
            I("dve", vec.reciprocal, out=tB[:, 0:N], in_=tB[:, 0:N], reads=[tB_d], writes=[tB_d])
            I("dve", vec.scalar_tensor_tensor, out=tC[:, 0:N], in0=src, scalar=gcol, in1=tB[:, 0:N],
              op0=ALU.mult, op1=ALU.mult, reads=[src_d, gcol_d, tB_d], writes=[tC_d])
            I("pe", pe.matmul, B[1][0][:, 0:N], lhsT=permT, rhs=tC[:, 0:N], start=True, stop=True,
              reads=[pid_d, tC_d], writes=[B[1][1]])
            I("dve", vec.tensor_tensor, out=tA[:, 0:N], in0=tC[:, 0:N], in1=cos, op=ALU.mult,
              reads=[tC_d] + cs_deps, writes=[tA_d])
            I("dve", vec.tensor_tensor, out=tB[:, 0:N], in0=B[1][0][:, 0:N], in1=sin, op=ALU.mult,
              reads=[B[1][1]] + cs_deps, writes=[tB_d])
            I("dve", vec.tensor_tensor, out=dst, in0=tA[:, 0:N], in1=tB[:, 0:N], op=ALU.add,
              reads=[tA_d, tB_d], writes=[dst_d])

        ksT, ksT_d = T("ksT", [128, S], BF16)
        bigB, bigB_d = T("bigB", [128, S], BF16)
        vs1, vs1_d = T("vs1", [128, NKT, 129], BF16)
        kcmpT, kcmpT_d = T("kcmpT", [128, NCMP], BF16)
        kraw, kraw_d = T("kraw", [128, NCMP])
        vcmp1, vcmp1_d = T("vcmp1", [128, NNT, 129], BF16)
        wlf, wlf_d = T("wlf", [128, 2, 128])
        wlb, wlb_d = T("wlb", [128, 2, 128], BF16)
        qf, qf_d = T("qf", [128, 4, 128])
        Qg, Qg_d = T("Qg", [128, 512], BF16)
        cq, cq_d = T("cq", [128, 2, 128])
        vwf, vwf_d = T("vwf", [128, 12, 128])
        vw1, vw1_d = T("vw1", [128, 12, 129], BF16)
        yo, yo_d = T("yo", [128, 512])
        I("dve", vec.memset, vwf[:], 0.0, writes=[vwf_d])
        cmk, cmk_d = T("cmk", [128, NNT, 128])
        tkt, tkt_d = T("tkt", [128, 3, NM])
        ET = [T("ET%d" % i, [128, 512], BF16) for i in range(2)]
        scs, scs_d = T("scs", [128, 512])
        rz, rz_d = T("rz", [128, 512])
        impT, impT_d = T("impT", [128, NMT, 128])
        score, score_d = T("score", [128, NM])
        sc2, sc2_d = T("sc2", [128, NM])
        m8, m8_d = T("m8", [128, 16])
        negq, negq_d = T("negq", [128, NM])
        negT, negT_d = T("negT", [128, NMT, 128], BF16)
        acc, acc_d = T("acc", [128, 4, 128])
        cf, cf_d = T("cf", [128, 8])
        I("dve", vec.memset, vs1[:, :, 128:129], 1.0, writes=[vs1_d])
        I("dve", vec.memset, vw1[:, :, 128:129], 1.0, writes=[vw1_d])
        eti = [0]
        sci = [0]

        def softmax_tile(sc_b, mask_ap, mask_deps):
            et, et_d = ET[eti[0]]
            eti[0] ^= 1
            if mask_ap is not None:
                I("dve", vec.tensor_tensor, out=scs[:].rearrange("p (j q) -> p j q", j=4),
                  in0=sc_b[0][:].rearrange("p (j q) -> p j q", j=4),
                  in1=mask_ap.unsqueeze(1).to_broadcast([128, 4, 128]), op=ALU.add,
                  reads=[sc_b[1]] + mask_deps, writes=[scs_d])
                I("act", act.activation, out=et[:], in_=scs[:], func=AF.Exp, reads=[scs_d], writes=[et_d])
            else:
                I("act", act.activation, out=et[:], in_=sc_b[0][:], func=AF.Exp, reads=[sc_b[1]], writes=[et_d])
            return et, et_d

        zer, zer_d = T("zer", [128, 512], BF16)
        I("dve", vec.memset, zer[:], 0.0, writes=[zer_d])

        def zero_bank(bk):
            I("pe", pe.matmul, bk[0][:], lhsT=onesb[:], rhs=zer[:], start=True, stop=False,
              reads=[onesb_d, zer_d], writes=[bk[1]])

        def pv_acc(et, et_d, vt, vt_d, first, last):
            if first:
                zero_bank(B[2])
                zero_bank(B[3])
            for j in range(4):
                bk = B[2 + j // 2]
                I("pe", pe.matmul, bk[0][:, (j % 2) * 129:(j % 2) * 129 + 129], lhsT=et[:, j * 128:(j + 1) * 128], rhs=vt,
                  start=False, stop=last, reads=[et_d, vt_d], writes=[bk[1]])

        def combine(br, g, ti, first):
            for j in range(4):
                bk = B[2 + j // 2]
                c0 = (j % 2) * 129
                I("dve", vec.tensor_scalar, out=cf[:, j:j + 1], in0=bk[0][:, c0 + 128:c0 + 129], scalar1=1e-30, scalar2=None,
                  op0=ALU.max, reads=[bk[1]], writes=[cf_d])
            I("dve", vec.reciprocal, out=cf[:, 0:4], in_=cf[:, 0:4], reads=[cf_d], writes=[cf_d])
            gc = br * 8 + g * 4
            I("dve", vec.tensor_tensor, out=cf[:, 4:8], in0=cf[:, 0:4], in1=gts[:, ti, gc:gc + 4], op=ALU.mult,
              reads=[cf_d, gts_d], writes=[cf_d])
            for j in range(4):
                bk = B[2 + j // 2]
                c0 = (j % 2) * 129
                if first:
                    I("dve", vec.tensor_scalar, out=acc[:, j, :], in0=bk[0][:, c0:c0 + 128], scalar1=cf[:, 4 + j:5 + j],
                      scalar2=None, op0=ALU.mult, reads=[bk[1], cf_d], writes=[acc_d])
                else:
                    I("dve", vec.scalar_tensor_tensor, out=acc[:, j, :], in0=bk[0][:, c0:c0 + 128], scalar=cf[:, 4 + j:5 + j],
                      in1=acc[:, j, :], op0=ALU.mult, op1=ALU.add, reads=[bk[1], cf_d, acc_d], writes=[acc_d])

        for g in range(2):
            for tl in range(S // 512):
                cs_ = slice(tl * 512, (tl + 1) * 512)
                s_, s_d = nstg()
                P.dma("sp", v4(s_[:]), io.fm("kc%d" % g, tl), reads=io.rd, writes=[s_d])
                I("pool", pool.tensor_copy, out=ksT[:, cs_], in_=s_[:], reads=[s_d], writes=[ksT_d])
                s_, s_d = nstg()
                P.dma("sp", v4(s_[:]), io.fm("vc%d" % g, tl), reads=io.rd, writes=[s_d])
                I("act", act.copy, out=bigB[:, cs_], in_=s_[:], reads=[s_d], writes=[bigB_d])
            nh = (NCMP + 511) // 512
            zero_bank(B[4])
            zero_bank(B[5])
            for l in range(32):
                P.dma("sp", wlf[:], cmpw[l], writes=[wlf_d])
                I("dve", vec.tensor_copy, out=wlb[:], in_=wlf[:], reads=[wlf_d], writes=[wlb_d])
                for hf in range(nh):
                    n0 = 512 * hf
                    cnt = min(512, NCMP - 1 - n0)
                    bk = B[2 + hf]
                    I("pe", pe.matmul, bk[0][:, 0:cnt], lhsT=wlb[:, 0, :], rhs=ksT[:, 16 * n0 + l:16 * n0 + l + 16 * (cnt - 1) + 1:16],
                      start=(l == 0), stop=False, reads=[wlb_d, ksT_d], writes=[bk[1]])
                    I("pe", pe.matmul, bk[0][:, 0:cnt], lhsT=wlb[:, 0, :], rhs=peb[:, 0, l:l + 1].to_broadcast([128, cnt]),
                      start=False, stop=(l == 31), reads=[wlb_d, peb_d], writes=[bk[1]])
                for nt in range(NNT):
                    n0 = 128 * nt
                    cnt = min(128, NCMP - 1 - n0)
                    bk = B[4 + nt // 4]
                    oc = (nt % 4) * 128
                    I("pe", pe.matmul, bk[0][0:cnt, oc:oc + 128], lhsT=bigB[:, 16 * n0 + l:16 * n0 + l + 16 * (cnt - 1) + 1:16],
                      rhs=wlb[:, 1, :], start=False, stop=False, reads=[wlb_d, bigB_d], writes=[bk[1]])
                    I("pe", pe.matmul, bk[0][0:cnt, oc:oc + 128], lhsT=peb[:, 1, l:l + 1].to_broadcast([128, cnt]),
                      rhs=wlb[:, 1, :], start=False, stop=(l == 31), reads=[wlb_d, peb_d], writes=[bk[1]])
            I("dve", vec.memset, kraw[:], 0.0, writes=[kraw_d])
            I("dve", vec.memset, vcmp1[:, :, 0:128], 0.0, writes=[vcmp1_d])
            I("dve", vec.memset, vcmp1[:, :, 128:129], 1.0, writes=[vcmp1_d])
            for hf in range(nh):
                cnt = min(512, NCMP - 1 - 512 * hf)
                I("act", act.copy, out=kraw[:, 512 * hf:512 * hf + cnt], in_=B[2 + hf][0][:, 0:cnt],
                  reads=[B[2 + hf][1]], writes=[kraw_d])
            for nt in range(NNT):
                cnt = min(128, NCMP - 1 - 128 * nt)
                oc = (nt % 4) * 128
                I("act", act.copy, out=vcmp1[0:cnt, nt, 0:128], in_=B[4 + nt // 4][0][0:cnt, oc:oc + 128],
                  reads=[B[4 + nt // 4][1]], writes=[vcmp1_d])
            for hf in range(nh):
                w_ = min(512, NCMP - 512 * hf)
                cs_ = slice(512 * hf, 512 * hf + w_)
                P.dma("sp", ctl[:, 0:w_], ccos[:, cs_], writes=[ctl_d])
                P.dma("sp", stl[:, 0:w_], csin[:, cs_], writes=[stl_d])
                norm_rope(kcmpT[:, cs_], kcmpT_d, kraw[:, cs_], kraw_d, gain[:, 1:2], gain_d,
                          ctl[:, 0:w_], stl[:, 0:w_], [ctl_d, stl_d], w_)
            for tl in range(S // 512):
                cs_ = slice(tl * 512, (tl + 1) * 512)
                s_, s_d = nstg()
                P.dma("sp", ctl[:, 0:512], cosk[:, cs_], writes=[ctl_d])
                P.dma("sp", stl[:, 0:512], sink[:, cs_], writes=[stl_d])
                P.dma("sp", v4(s_[:]), io.fm("ks%d" % g, tl), reads=io.rd, writes=[s_d])
                norm_rope(ksT[:, cs_], ksT_d, s_[:], s_d, gain[:, 2:3], gain_d, ctl[:, 0:512], stl[:, 0:512],
                          [ctl_d, stl_d], 512)
                s_, s_d = nstg()
                P.dma("sp", v4(s_[:]), io.fm("kw%d" % g, tl), reads=io.rd, writes=[s_d])
                norm_rope(bigB[:, cs_], bigB_d, s_[:], s_d, gain[:, 3:4], gain_d, ctl[:, 0:512], stl[:, 0:512],
                          [ctl_d, stl_d], 512)
            for k4 in range(NKT // 4):
                s_, s_d = nstg()
                P.dma("sp", s_[:].rearrange("p (k d) -> p k d", k=4), io.tm("vs_tok%d" % g, k4), reads=io.rd, writes=[s_d])
                I("pool", pool.tensor_copy, out=vs1[:, 4 * k4:4 * k4 + 4, 0:128], in_=s_[:].rearrange("p (k d) -> p k d", k=4),
                  reads=[s_d], writes=[vs1_d])

            for ti in range(NOWN):
                P.dma("sp", qf[:], io.nq(ti, g), reads=io.rd, writes=[qf_d])
                P.dma("sp", cq[:, 0, :], cosq[:, ti, :], writes=[cq_d])
                P.dma("sp", cq[:, 1, :], sinq[:, ti, :], writes=[cq_d])
                for (j0, j1, src_ap) in io.tmw("vw_tok%d" % g, ti):
                    P.dma("sp", vwf[:, j0:j1, :], src_ap, reads=io.rd, writes=[vwf_d])
                P.dma("sp", cmk[:], cmask[:, ti, :, :], writes=[cmk_d])
                P.dma("sp", tkt[:], tk_i[:, ti, :, :], writes=[tkt_d])
                _norm_rope_q(P, I, act, vec, pe, B, onesf, onesf_d, permT, pid_d, tA, tA_d, tB, tB_d, tC, tC_d,
                             Qg, Qg_d, qf, qf_d, gq, gq_d, cq, cq_d)
                I("pool", pool.tensor_copy, out=vw1[:, :, 0:128], in_=vwf[:], reads=[vwf_d], writes=[vw1_d])

                nnt = min(NNT, (8 * (8 * ti + 7) + 6) // 128 + 1)
                for nt in range(nnt):
                    sb = B[sci[0]]
                    sci[0] ^= 1
                    I("pe", pe.matmul, sb[0][:], lhsT=kcmpT[:, nt * 128:(nt + 1) * 128], rhs=Qg[:], start=True, stop=True,
                      reads=[kcmpT_d, Qg_d], writes=[sb[1]])
                    et, et_d = softmax_tile(sb, cmk[:, nt, :], [cmk_d])
                    pv_acc(et, et_d, vcmp1[:, nt, :], vcmp1_d, nt == 0, nt == nnt - 1)
                    for mt in range(NMT):
                        I("pe", pe.matmul, B[4 + mt][0][:], lhsT=ovl[:, nt, mt * 128:(mt + 1) * 128], rhs=et[:],
                          start=(nt == 0), stop=(nt == nnt - 1), reads=[ovl_d, et_d], writes=[B[4 + mt][1]])
                    I("pe", pe.matmul, B[6][0][:], lhsT=onesb[:], rhs=et[:], start=(nt == 0), stop=(nt == nnt - 1),
                      reads=[onesb_d, et_d], writes=[B[6][1]])
                combine(0, g, ti, True)
                I("dve", vec.tensor_scalar, out=rz[:], in0=B[6][0][:], scalar1=1e-30, scalar2=None, op0=ALU.max,
                  reads=[B[6][1]], writes=[rz_d])
                I("dve", vec.reciprocal, out=rz[:], in_=rz[:], reads=[rz_d], writes=[rz_d])
                for mt in range(NMT):
                    I("dve", vec.tensor_tensor, out=scs[:], in0=B[4 + mt][0][:], in1=rz[:], op=ALU.mult,
                      reads=[B[4 + mt][1], rz_d], writes=[scs_d])
                    I("dve", vec.tensor_reduce, out=impT[:, mt, :], in_=scs[:].rearrange("p (j q) -> p q j", j=4),
                      axis=AX.X, op=ALU.add, reads=[scs_d], writes=[impT_d])
                for mt in range(NMT):
                    I("pe", pe.transpose, B[7][0][:, mt * 128:(mt + 1) * 128], impT[:, mt, :], ident,
                      reads=[impT_d, pid_d], writes=[B[7][1]])
                MW = NMT * 128
                I("dve", vec.tensor_tensor, out=score[:, 0:MW], in0=B[7][0][:, 0:MW], in1=tkt[:, 0, 0:MW], op=ALU.mult,
                  reads=[B[7][1], tkt_d], writes=[score_d])
                I("dve", vec.tensor_tensor, out=score[:, 0:MW], in0=score[:, 0:MW], in1=tkt[:, 1, 0:MW], op=ALU.add,
                  reads=[score_d, tkt_d], writes=[score_d])
                I("dve", vec.tensor_tensor, out=score[:, 0:MW], in0=score[:, 0:MW], in1=tkt[:, 2, 0:MW], op=ALU.max,
                  reads=[score_d, tkt_d], writes=[score_d])
                I("dve", vec.max, out=m8[:, 0:8], in_=score[:, 0:MW], reads=[score_d], writes=[m8_d])
                I("dve", vec.match_replace, out=sc2[:, 0:MW], in_to_replace=m8[:, 0:8], in_values=score[:, 0:MW],
                  imm_value=-1e9, reads=[m8_d, score_d], writes=[sc2_d])
                I("dve", vec.max, out=m8[:, 8:16], in_=sc2[:, 0:MW], reads=[sc2_d], writes=[m8_d])
                I("dve", vec.tensor_scalar, out=negq[:, 0:MW], in0=score[:, 0:MW], scalar1=m8[:, 15:16], scalar2=NEG,
                  op0=ALU.is_lt, op1=ALU.mult, reads=[score_d, m8_d], writes=[negq_d])
                for mt in range(NMT):
                    I("pe", pe.transpose, B[7][0][:, mt * 128:(mt + 1) * 128], negq[:, mt * 128:(mt + 1) * 128], ident,
                      reads=[negq_d, pid_d], writes=[B[7][1]])
                I("act", act.copy, out=negT[:].rearrange("p m q -> p (m q)"), in_=B[7][0][:, 0:MW],
                  reads=[B[7][1]], writes=[negT_d])

                nk = 8 * ti + 8
                for kt in range(nk):
                    sb = B[sci[0]]
                    sci[0] ^= 1
                    I("pe", pe.matmul, sb[0][:], lhsT=ksT[:, kt * 128:(kt + 1) * 128], rhs=Qg[:], start=True, stop=False,
                      reads=[ksT_d, Qg_d], writes=[sb[1]])
                    ktl = kt % 64
                    I("pe", pe.matmul, sb[0][:].rearrange("p (j q) -> p j q", j=4), lhsT=rsel[:, ktl * 128:(ktl + 1) * 128],
                      rhs=negT[:, kt // 64, :].unsqueeze(1).to_broadcast([128, 4, 128]), start=False, stop=True,
                      reads=[rsel_d, negT_d], writes=[sb[1]])
                    if kt >= 8 * ti:
                        et, et_d = softmax_tile(sb, dmask[:, kt - 8 * ti, :], [dmask_d])
                    else:
                        et, et_d = softmax_tile(sb, None, [])
                    pv_acc(et, et_d, vs1[:, kt, :], vs1_d, kt == 0, kt == nk - 1)
                combine(1, g, ti, False)

                j0w = 4 if ti == 0 else 0
                for jw in range(j0w, 12):
                    kt = 8 * ti - 4 + jw
                    sb = B[sci[0]]
                    sci[0] ^= 1
                    I("pe", pe.matmul, sb[0][:], lhsT=bigB[:, kt * 128:(kt + 1) * 128], rhs=Qg[:], start=True, stop=True,
                      reads=[bigB_d, Qg_d], writes=[sb[1]])
                    et, et_d = softmax_tile(sb, wmask[:, jw, :], [wmask_d])
                    pv_acc(et, et_d, vw1[:, jw, :], vw1_d, jw == j0w, jw == 11)
                combine(2, g, ti, False)
                for j in range(4):
                    I("pe", pe.transpose, B[7][0][:, j * 128:(j + 1) * 128], acc[:, j, :], ident,
                      reads=[acc_d, pid_d], writes=[B[7][1]])
                I("act", act.copy, out=yo[:], in_=B[7][0][:], reads=[B[7][1]], writes=[yo_d])
                P.dma("sp", io.nout(ti, g), v4(yo[:]), reads=[yo_d], writes=io.wr, is_output=True)
        P.close_scope()


def _norm_rope_q(P, I, act, vec, pe, B, onesf, onesf_d, permT, pid_d, tA, tA_d, tB, tB_d, tC, tC_d,
                 Qg, Qg_d, qf, qf_d, gq, gq_d, cq, cq_d):
    N = 512
    src = qf[:].rearrange("p j q -> p (j q)")
    I("act", act.activation, out=tA[:], in_=src, func=AF.Square, reads=[qf_d], writes=[tA_d])
    I("pe", pe.matmul, B[0][0][:], lhsT=onesf[:], rhs=tA[:], start=True, stop=True, reads=[onesf_d, tA_d], writes=[B[0][1]])
    I("dve", vec.tensor_scalar, out=tB[:], in0=B[0][0][:], scalar1=1.0 / 128, scalar2=1e-6, op0=ALU.mult, op1=ALU.add,
      reads=[B[0][1]], writes=[tB_d])
    I("act", act.activation, out=tB[:], in_=tB[:], func=AF.Sqrt, reads=[tB_d], writes=[tB_d])
    I("dve", vec.reciprocal, out=tB[:], in_=tB[:], reads=[tB_d], writes=[tB_d])
    I("dve", vec.scalar_tensor_tensor, out=tC[:], in0=src, scalar=gq[:, 0:1], in1=tB[:], op0=ALU.mult, op1=ALU.mult,
      reads=[qf_d, gq_d, tB_d], writes=[tC_d])
    I("pe", pe.matmul, B[1][0][:], lhsT=permT, rhs=tC[:], start=True, stop=True, reads=[pid_d, tC_d], writes=[B[1][1]])
    v4 = lambda ap: ap.rearrange("p (j q) -> p j q", j=4)
    I("dve", vec.tensor_tensor, out=v4(tA[:]), in0=v4(tC[:]), in1=cq[:, 0, :].unsqueeze(1).to_broadcast([128, 4, 128]),
      op=ALU.mult, reads=[tC_d, cq_d], writes=[tA_d])
    I("dve", vec.tensor_tensor, out=v4(tB[:]), in0=v4(B[1][0][:]), in1=cq[:, 1, :].unsqueeze(1).to_broadcast([128, 4, 128]),
      op=ALU.mult, reads=[B[1][1], cq_d], writes=[tB_d])
    I("dve", vec.tensor_tensor, out=Qg[:], in0=tA[:], in1=tB[:], op=ALU.add, reads=[tA_d, tB_d], writes=[Qg_d])


def _rope_np(pos):
    inv = (np.float32(10000.0) ** (-(np.arange(0, 128, 2, dtype=np.float32) / np.float32(128)))).astype(np.float32)
    ang = (pos.astype(np.float32)[:, None] * inv[None, :]).astype(np.float32)
    c, s = np.cos(ang).astype(np.float32), np.sin(ang).astype(np.float32)
    return (np.ascontiguousarray(np.concatenate([c, c], axis=1).T), np.ascontiguousarray(np.concatenate([s, s], axis=1).T))


def nsa_shared_consts(p, S):
    NKT, NOWN, NCMP, NNT, NMT, NM = nsa_dims(S)
    n_slc = S // 64
    cosk, sink = _rope_np(np.arange(S))
    ccos, csin = _rope_np(np.arange(NCMP) * 16 + 31)
    permid = np.zeros((128, 256), np.float32)
    for m in range(64):
        permid[m + 64, m] = -1.0
        permid[m, m + 64] = 1.0
    permid[np.arange(128), 128 + np.arange(128)] = 1.0
    rsel = (np.arange(128)[:, None] == (np.arange(8192)[None, :] // 64)).astype(np.float32)
    n_all = np.arange(NCMP)
    c0 = n_all[:, None] * 16
    s0 = np.arange(NM)[None, :] * 64
    ov = np.clip(np.minimum(c0 + 32, s0 + 64) - np.maximum(c0, s0), 0, None).astype(np.float32) / 32.0
    ov[NCMP - 1:, :] = 0.0
    ov[:, n_slc:] = 0.0
    ovl = np.ascontiguousarray(ov.reshape(NNT, 128, NM).transpose(1, 0, 2))
    cmpw = np.ascontiguousarray(np.asarray(p["nsa_cmp_w"][0]).transpose(1, 2, 0, 3))
    peT = np.ascontiguousarray(np.asarray(p["nsa_cmp_pe"][0]).transpose(2, 0, 1))
    gain = np.ascontiguousarray(np.asarray(p["nsa_qk_gain"][0]).T)
    return {"cosk": cosk, "sink": sink, "ccos": ccos, "csin": csin, "ngain": gain, "cmpw": cmpw, "peT": peT,
            "rsel": rsel, "ovl": ovl, "permid": permid}


def nsa_core_consts(shared, S, c):
    NKT, NOWN, NCMP, NNT, NMT, NM = nsa_dims(S)
    n_slc = S // 64
    pp = np.arange(128)
    qts = [c + 8 * ti for ti in range(NOWN)]
    tok = np.concatenate([np.arange(128 * qt, 128 * qt + 128) for qt in qts])
    m = {}
    m["cosq"] = np.ascontiguousarray(shared["cosk"][:, tok].reshape(128, NOWN, 128))
    m["sinq"] = np.ascontiguousarray(shared["sink"][:, tok].reshape(128, NOWN, 128))
    cm = np.zeros((128, NOWN, NNT, 128), np.float32)
    tkk = np.zeros((128, NOWN, 3, NM), np.float32)
    for ti, qt in enumerate(qts):
        t = 128 * qt + pp
        for nt in range(NNT):
            n = 128 * nt + pp
            vis = (16 * n[:, None] + 31) <= t[None, :]
            cm[:, ti, nt, :] = np.where(vis, 0.0, NEG)
        cur = t // 64
        mm = np.arange(NM)
        val = (mm[None, :] <= cur[:, None]) & (mm[None, :] < n_slc)
        forced = (mm[None, :] == 0) | (mm[None, :] == cur[:, None]) | (mm[None, :] == cur[:, None] - 1)
        tkk[:, ti, 0, :] = val
        tkk[:, ti, 1, :] = val.astype(np.float32) - 1.0
        tkk[:, ti, 2, :] = np.where(forced & val, 1e4, -2.0)
    m["cmask"] = cm
    m["tk"] = tkk
    dm = np.zeros((128, 8, 128), np.float32)
    for j in range(8):
        if j == c:
            dm[:, j, :] = np.where(pp[:, None] <= pp[None, :], 0.0, NEG)
        elif j > c:
            dm[:, j, :] = NEG
    m["dmask"] = dm
    wm = np.zeros((128, 12, 128), np.float32)
    for j in range(12):
        d = 128 * (c + 4 - j) + pp[None, :] - pp[:, None]
        wm[:, j, :] = np.where((d >= 0) & (d < 512), 0.0, NEG)
    m["wmask"] = wm
    return m


def nsa_inputs(uT, p, S):
    NKT, NOWN, NCMP, NNT, NMT, NM = nsa_dims(S)
    shared = nsa_shared_consts(p, S)
    data = {}
    for g in range(2):
        data["kc%d" % g] = np.ascontiguousarray(uT[1024 + g * 128:1024 + (g + 1) * 128])
        data["vc%d" % g] = np.ascontiguousarray(uT[1280 + g * 128:1280 + (g + 1) * 128])
        data["ks%d" % g] = np.ascontiguousarray(uT[1536 + g * 128:1536 + (g + 1) * 128])
        data["vs_tok%d" % g] = np.ascontiguousarray(uT[1792 + g * 128:1792 + (g + 1) * 128].T)
        data["kw%d" % g] = np.ascontiguousarray(uT[2048 + g * 128:2048 + (g + 1) * 128])
        data["vw_tok%d" % g] = np.ascontiguousarray(uT[2304 + g * 128:2304 + (g + 1) * 128].T)
    maps = []
    for c in range(NCORES):
        m = dict(shared)
        m.update(data)
        m.update(nsa_core_consts(shared, S, c))
        tok = np.concatenate([np.arange(128 * (c + 8 * ti), 128 * (c + 8 * ti) + 128) for ti in range(NOWN)])
        m["q_own"] = np.ascontiguousarray(uT[0:1024][:, tok].reshape(8, 128, NOWN, 128).transpose(1, 2, 0, 3))
        m["gates"] = np.ascontiguousarray(uT[2560:2584][:, tok].reshape(24, NOWN, 128).transpose(2, 1, 0))
        maps.append(m)
    return maps


def nsa_gather(results, S):
    NOWN = S // 128 // 8
    yT = np.zeros((1024, S), np.float32)
    for c in range(NCORES):
        y = results[c]["ny"]
        for ti in range(NOWN):
            qt = c + 8 * ti
            yT[:, 128 * qt:128 * qt + 128] = y[:, ti * 128:(ti + 1) * 128]
    return yT


class FusedIO:
    def __init__(self, P, ext, pid, S):
        self.P, self.ext, self.S = P, ext, S
        self.rd, self.wr = [], []

    def par(self, n):
        return self.ext[n]

    @staticmethod
    def gpos(g):
        return g // 2, 4 * (g % 2)


class NsaFusedIO(FusedIO):
    def __init__(self, P, ext, pid, S, U0, utok0, AGt0, yT0):
        FusedIO.__init__(self, P, ext, pid, S)
        self.U0, self.utok0, self.yT0 = U0, utok0, yT0
        self.AGt = AGt0.rearrange("(r ti p) c -> p ti r c", r=8, p=128)
        self.rd = U0.deps_ag() + U0.deps_loc() + [P.gd(AGt0), P.gd(utok0)]
        self.wr = [P.gd(yT0)]

    def nq(self, ti, g):
        return self.U0.local(0, 1024).rearrange("(h d) (ti r) -> d ti h r", d=128, r=128)[:, ti, 4 * g:4 * g + 4, :]

    def ngates(self):
        return self.utok0.rearrange("(ti r) c -> r ti c", r=128)[:, :, 512:536]

    def fm(self, n, tl):
        base = {"kc": 1024, "vc": 1280, "ks": 1536, "kw": 2048}[n[:2]] + int(n[2]) * 128
        ti, r0 = self.gpos(tl)
        return self.U0.agv(base, base + 128)[:, ti, r0:r0 + 4, :]

    def tm(self, n, k4):
        g = int(n[-1])
        ti, r0 = self.gpos(k4)
        return self.AGt[:, ti, r0:r0 + 4, g * 128:(g + 1) * 128]

    def tmw(self, n, ti):
        g = int(n[-1])
        cs = slice(256 + g * 128, 256 + (g + 1) * 128)
        if ti == 0:
            return [(4, 12, self.AGt[:, 0, 0:8, cs])]
        return [(0, 4, self.AGt[:, ti - 1, 4:8, cs]), (4, 12, self.AGt[:, ti, 0:8, cs])]

    def nout(self, ti, g):
        return self.yT0[4 * g * 128:(4 * g + 4) * 128, ti * 128:(ti + 1) * 128].rearrange("(j d) q -> d j q", d=128)


class RwFusedIO(FusedIO):
    def __init__(self, P, ext, pid, S, U0, rwin, yrw):
        FusedIO.__init__(self, P, ext, pid, S)
        self.U0 = U0
        self.rwin = rwin
        self.yrw = yrw.rearrange("f (r ti p) -> f r ti p", r=8, p=128)
        self.rd = U0.deps_ag() + [P.gd(rwin)]
        self.wr = [P.gd(yrw)]

    def fm(self, n, g):
        ti, r0 = self.gpos(g)
        o = 2584
        if n in ("rw_r", "rw_k", "rw_v"):
            b_ = ("rw_r", "rw_k", "rw_v").index(n)
            return self.rwin[b_].rearrange("f r (ti p) -> f ti r p", p=128)[:, ti, r0:r0 + 4, :]
        b0, nr = {"rw_wl": (3072, 96), "rw_al": (3168, 96), "rw_gl0": (3264, 128), "rw_gl1": (3392, 128)}[n]
        return self.U0.agv(o + b0, o + b0 + nr)[:, ti, r0:r0 + 4, :]

    def out(self, n, a, b, g):
        ti, r0 = self.gpos(g)
        return self.yrw[a:b, r0:r0 + 4, ti, :]


MB_BASES = (0, 1024, 2048, 3072, 5120)


class MbFusedIO(FusedIO):
    def __init__(self, P, ext, pid, S, mbin, mbtok, yb):
        FusedIO.__init__(self, P, ext, pid, S)
        self.mbin, self.mbtok = mbin, mbtok
        self.yb = yb.rearrange("f (r ti p) -> f r ti p", r=8, p=128)
        self.rd = [P.gd(mbin), P.gd(mbtok)]
        self.wr = [P.gd(yb)]

    def fm(self, n, g):
        ti, r0 = self.gpos(g)
        k = ("l_gate", "l_xb", "h_q", "h_f", "h_g").index(n)
        return self.mbin[k].rearrange("f r (ti p) -> f ti r p", p=128)[:, ti, r0:r0 + 4, :]

    def tm(self, n, g):
        ti, r0 = self.gpos(g)
        return self.mbtok[1].rearrange("(r ti p) c -> p ti r c", r=8, p=128)[:, ti, r0:r0 + 4, :]

    def tm32(self, n, g):
        ti, r0 = self.gpos(g)
        k = 0 if n == "h_ftok" else 1
        v = self.mbtok[k].rearrange("(r ti q p) c -> p ti r q c", r=8, q=4, p=32)
        return [v[:, ti, r0 + a, :, :] for a in range(4)]

    def out(self, n, a, b, g):
        ti, r0 = self.gpos(g)
        return self.yb[a:b, r0:r0 + 4, ti, :]


class SplitU:
    def __init__(self, P, name, TOKN, split_oc):
        self.P, self.TOKN, self.split = P, TOKN, split_oc * 128
        self.loc = [P.dram(name + "a", [self.split, TOKN]), P.dram(name + "b", [6144 - self.split, TOKN])]
        self.ag = [P.dram("AG" + name + "a", [8 * self.split, TOKN]), P.dram("AG" + name + "b", [8 * (6144 - self.split), TOKN])]

    def part(self, f0, f1):
        k = 0 if f1 <= self.split else 1
        assert k == 1 or f0 < self.split
        assert not (f0 < self.split < f1), (f0, f1, self.split)
        off = 0 if k == 0 else self.split
        return k, f0 - off, f1 - off

    def local(self, f0, f1):
        k, a, b = self.part(f0, f1)
        return self.loc[k][a:b, :]

    def deps_loc(self):
        return [self.P.gd(t) for t in self.loc]

    def deps_ag(self):
        return [self.P.gd(t) for t in self.ag]

    def gather(self):
        for k in range(2):
            self.P.collective("AllGather", self.loc[k], self.ag[k], reads=[self.P.gd(self.loc[k])], writes=[self.P.gd(self.ag[k])])

    def agv(self, f0, f1):
        k, a, b = self.part(f0, f1)
        return self.ag[k].rearrange("(r f) (ti p) -> f ti r p", r=8, p=128)[a:b]

    def agf(self, f0, f1):
        k, a, b = self.part(f0, f1)
        return self.ag[k].rearrange("(r f) t -> f r t", r=8)[a:b]


TOKC0 = {14: 0, 15: 128, 18: 256, 19: 384, 20: 512}
SKIP0 = (14, 15, 18, 19)
TOKC1 = dict([(24 + k, 128 * k) for k in range(8)] + [(32 + k, 1024 + 128 * k) for k in range(8)])
SKIP1 = tuple(range(32, 40))
MB_SPECS = {"l_par": [128, 16], "l_wa": [128, 128], "l_wx": [128, 128], "h_par": [128, 8], "h_lbrow": [128, 256],
            "mb_cst": [128, 1024]}
RW_SPECS = {"rw_par": [128, 24], "rw_w2": [96, 128], "rw_a2": [96, 128], "rw_g2": [128, 2, 128], "rw_cst": [128, 192]}


def build_fused(S, dbg=False):
    TOKN = S // 8
    nc = new_nc()
    ext = {}

    def E(n, shp):
        ext[n] = nc.dram_tensor(n, list(shp), F32, kind="ExternalInput").ap()
        return ext[n]

    xT = E("xT", [D, TOKN])
    g0, g1, g2 = E("gains0", [128, 32]), E("gains1", [128, 32]), E("gains2", [128, 32])
    w_in_a, wo_a, w1_a, w2_a = E("w_in_a", [1, 48, 128, 2048]), E("wo_a", [1, 16, 128, 2048]), E("w1_a", [1, 64, 128, 2048]), E("w2_a", [4, 16, 128, 2048])
    w_in_b, wo_b, w1_b, w2_b = E("w_in_b", [1, 48, 128, 2048]), E("wo_b", [1, 16, 128, 2048]), E("w1_b", [1, 64, 128, 2048]), E("w2_b", [4, 16, 128, 2048])
    for n, shp in list(nsa_const_specs(S).items()) + list(RW_SPECS.items()) + list(MB_SPECS.items()):
        E(n, shp)
    outT = nc.dram_tensor("outT", [D, TOKN], F32, kind="ExternalOutput").ap()
    with ExitStack() as es:
        P = Prog(nc, es)
        pid0 = nc.partition_id()
        pid = None
        v128 = pid0 * 128
        vT = pid0 * TOKN
        rwin = P.dram("rwin", [3, 128, 8, TOKN])
        ymine0 = P.dram("ymine0", [1024, TOKN])
        mbin = P.dram("mbin", [5, 128, 8, TOKN])
        mbtok = P.dram("mbtok", [2, 8 * TOKN, 128])
        ymine1 = P.dram("ymine1", [2048, TOKN])
        U0, utok0 = SplitU(P, "u0", TOKN, 20), P.dram("utok0", [TOKN, 640])
        U1 = SplitU(P, "u1", TOKN, 24)
        AGt0 = P.dram("AGt0", [8 * TOKN, 640])
        yT0 = P.dram("yT0", [1024, TOKN])
        yrw, AGyrw = P.dram("yrw", [128, S]), P.dram("AGyrw", [1024, S])
        x1T, utok1 = P.dram("x1T", [D, TOKN]), P.dram("utok1", [TOKN, 2048])
        AGt1 = P.dram("AGt1", [8 * TOKN, 2048])
        yb, AGyb = P.dram("yb", [256, S]), P.dram("AGyb", [2048, S])
        gd = P.gd
        emit_dense_stage(P, TOKN, xT, None, g0, {"wn": w_in_a}, 48, None, U0, utok0, TOKC0, SKIP0)
        U0.gather()
        P.collective("AllGather", utok0, AGt0, reads=[gd(utok0)], writes=[gd(AGt0)])
        for b_ in range(3):
            c0_ = 2584 + b_ * 1024
            P.dma("sp", rwin[b_], U0.agf(c0_, c0_ + 1024)[bass.ds(v128, 128), :, :], reads=U0.deps_ag(), writes=[gd(rwin)])
        emit_nsa(P, S, NsaFusedIO(P, ext, pid, S, U0, utok0, AGt0, yT0))
        emit_rwkv(P, S, RwFusedIO(P, ext, pid, S, U0, rwin, yrw))
        P.collective("AllGather", yrw, AGyrw, reads=[gd(yrw)], writes=[gd(AGyrw)])
        P.dma("sp", ymine0, AGyrw[:, bass.ds(vT, TOKN)], reads=[gd(AGyrw)], writes=[gd(ymine0)])

        def y_src0(tt, j):
            if j < 2:
                return yT0[512 * j:512 * j + 512, tt * NT:(tt + 1) * NT].rearrange("(c p) t -> p c t", p=128)
            return ymine0[512 * (j - 2):512 * (j - 2) + 512, tt * NT:(tt + 1) * NT].rearrange("(c p) t -> p c t", p=128)

        P.y_reads = [gd(yT0), gd(ymine0)]
        emit_dense_stage(P, TOKN, xT, y_src0, g1, {"wo": wo_a, "w1": w1_a, "w2": w2_a, "wn": w_in_b}, 48,
                         x1T, U1, utok1, TOKC1, SKIP1)
        U1.gather()
        P.collective("AllGather", utok1, AGt1, reads=[gd(utok1)], writes=[gd(AGt1)])
        for k_, base_ in enumerate(MB_BASES):
            P.dma("sp", mbin[k_], U1.agf(base_, base_ + 1024)[bass.ds(v128, 128), :, :], reads=U1.deps_ag(), writes=[gd(mbin)])
        for k_, off_ in enumerate((0, 1024)):
            P.dma("sp", mbtok[k_], AGt1[:, off_:off_ + 1024][:, bass.ds(v128, 128)], reads=[gd(AGt1)], writes=[gd(mbtok)])
        emit_mix_b(P, S, MbFusedIO(P, ext, pid, S, mbin, mbtok, yb))
        P.collective("AllGather", yb, AGyb, reads=[gd(yb)], writes=[gd(AGyb)])
        P.dma("sp", ymine1, AGyb[:, bass.ds(vT, TOKN)], reads=[gd(AGyb)], writes=[gd(ymine1)])

        def y_src1(tt, j):
            v = ymine1.rearrange("(c two p) t -> p c two t", two=2, p=128)
            jj, half = (j, 0) if j < 2 else (j - 2, 1)
            return v[:, 4 * jj:4 * jj + 4, half, tt * NT:(tt + 1) * NT]

        P.y_reads = [gd(ymine1), gd(x1T)]
        emit_dense_stage(P, TOKN, x1T, y_src1, g2, {"wo": wo_b, "w1": w1_b, "w2": w2_b}, 0, outT, None, None, {}, ())
        P.finish()
    return nc


def fused_inputs(p, S):
    TOKN = S // 8
    NOWN = TOKN // 128
    x = p["x"][0]
    z16 = np.zeros((128, 16), np.float32)
    shared = {
        "gains0": np.concatenate([z16, gain_layout(p["norm_mix"][0])], axis=1),
        "gains1": np.concatenate([gain_layout(p["norm_mlp"][0]), gain_layout(p["norm_mix"][1])], axis=1),
        "gains2": np.concatenate([gain_layout(p["norm_mlp"][1]), z16], axis=1),
        "w_in_a": wtile(p["w_in_a"][0], 48), "wo_a": wtile(p["w_out_a"][0]), "w1_a": wtile(p["w_ff1"][0]), "w2_a": wtile(p["w_ff2"][0]),
        "w_in_b": wtile(p["w_in_b"][0], 48), "wo_b": wtile(p["w_out_b"][0]), "w1_b": wtile(p["w_ff1"][1]), "w2_b": wtile(p["w_ff2"][1]),
    }
    nsh = nsa_shared_consts(p, S)
    shared.update(nsh)
    mbc = mixb_consts()
    maps = []
    for c in range(NCORES):
        m = dict(shared)
        tok = np.concatenate([np.arange(128 * (c + 8 * ti), 128 * (c + 8 * ti) + 128) for ti in range(NOWN)])
        m["xT"] = np.ascontiguousarray(x[tok].T)
        m.update(nsa_core_consts(nsh, S, c))
        m.update(rwkv_params(p, c))
        m.update(mixb_params(p, c))
        m["mb_cst"] = mbc
        maps.append(m)
    return maps


def fused_gather(results, S):
    TOKN = S // 8
    NOWN = TOKN // 128
    out = np.zeros((S, D), np.float32)
    for c in range(NCORES):
        o = results[c]["outT"]
        for ti in range(NOWN):
            qt = c + 8 * ti
            out[128 * qt:128 * qt + 128] = o[:, ti * 128:(ti + 1) * 128].T
    return out


SEQ = 16384


def _run(nc, maps):
    res = run_bass_kernel_spmd(nc, maps, core_ids=list(range(NCORES)))
    return res.results


def kernel(**inp):
    p = {k: np.asarray(v, np.float32) for k, v in inp.items()}
    S = SEQ
    nc = build_fused(S)
    maps = fused_inputs(p, S)
    res = _run(nc, maps)
    return fused_gather(res, S)[None].astype(np.float32)
```
